# Optimizing a Trainium2 kernel written in Bass

```python
import jax, jax.numpy as jnp
from jax import lax
import numpy as np

D_MODEL = 1024
BATCH = 4
SEQ = 4096
DEPTH = 1

EXPAND = 2
D_MIX = EXPAND * D_MODEL
HEAD_DIM = 64
D_SSD = D_MIX // 2
D_SB = D_MIX - D_SSD
N_SSD_HEADS = D_SSD // HEAD_DIM
N_SB_HEADS = D_SB // HEAD_DIM
SSD_GROUPS = 2
SSD_STATE = 128
CONV_WIDTH = 4
D_CONV = D_SSD + 2 * SSD_GROUPS * SSD_STATE
SSD_CHUNK = 128
SB_BLOCK = 128
D_FF = -(-8 * D_MODEL // (3 * 256)) * 256
N_MOD = 6
EPS = 1e-6
IN_SPLITS = [int(s) for s in np.cumsum([D_SSD, D_CONV, N_SSD_HEADS, D_SB, D_SB])]
D_IN_PROJ = IN_SPLITS[-1] + D_SB

kernel_name = "hymba_ssd_stickbreaking_adaln_block"


def rms_norm(x, w):
    xf = x.astype(jnp.float32)
    y = xf * lax.rsqrt(jnp.mean(xf * xf, axis=-1, keepdims=True) + EPS)
    return (y * w.astype(jnp.float32)).astype(x.dtype)


def causal_depthwise_conv(u, w, b):
    out = lax.conv_general_dilated(
        u, w[:, None, :], window_strides=(1,), padding=[(CONV_WIDTH - 1, 0)],
        dimension_numbers=("NWC", "WIO", "NWC"), feature_group_count=u.shape[-1])
    return out + b


def ssd_chunked(x, dt, a, b_mat, c_mat):
    bsz, seq, n_heads, p = x.shape
    g, n = b_mat.shape[2], b_mat.shape[3]
    e = n_heads // g
    nc, t = seq // SSD_CHUNK, SSD_CHUNK
    xg = (x * dt[..., None]).reshape(bsz, nc, t, g, e, p)
    a_dt = jnp.moveaxis((a * dt).reshape(bsz, nc, t, g, e), 2, -1)
    a_cs = jnp.cumsum(a_dt, axis=-1)
    bm = b_mat.reshape(bsz, nc, t, g, n)
    cm = c_mat.reshape(bsz, nc, t, g, n)
    idx = jnp.arange(t)
    causal = idx[:, None] >= idx[None, :]
    seg = a_cs[..., :, None] - a_cs[..., None, :]
    decay = jnp.exp(jnp.where(causal, seg, -jnp.inf))
    scores = jnp.einsum("bclgn,bcsgn->bcgls", cm, bm)
    y_diag = jnp.einsum("bcgls,bcgels,bcsgep->bclgep", scores, decay, xg)
    decay_states = jnp.exp(a_cs[..., -1:] - a_cs)
    states = jnp.einsum("bclgn,bcgel,bclgep->bcgepn", bm, decay_states, xg)
    chunk_decay = jnp.exp(a_cs[..., -1])

    def step(carry, inp):
        st, dec = inp
        return carry * dec[..., None, None] + st, carry

    init = jnp.zeros_like(states[:, 0])
    _, prev_states = lax.scan(step, init, (jnp.moveaxis(states, 1, 0), jnp.moveaxis(chunk_decay, 1, 0)))
    prev_states = jnp.moveaxis(prev_states, 0, 1)
    y_off = jnp.einsum("bclgn,bcgepn,bcgel->bclgep", cm, prev_states, jnp.exp(a_cs))
    return (y_diag + y_off).reshape(bsz, seq, n_heads, p)


def ssd_mixer(z, xbc, dt_raw, conv_w, conv_b, dt_bias, a_log, d_skip, norm_w):
    bsz, seq, _ = z.shape
    f32 = jnp.float32
    xbc = jax.nn.silu(causal_depthwise_conv(xbc, conv_w, conv_b)).astype(f32)
    xs = xbc[..., :D_SSD]
    bm = xbc[..., D_SSD:D_SSD + SSD_GROUPS * SSD_STATE].reshape(bsz, seq, SSD_GROUPS, SSD_STATE)
    cm = xbc[..., D_SSD + SSD_GROUPS * SSD_STATE:].reshape(bsz, seq, SSD_GROUPS, SSD_STATE)
    xh = xs.reshape(bsz, seq, N_SSD_HEADS, HEAD_DIM)
    dt = jax.nn.softplus(dt_raw.astype(f32) + dt_bias.astype(f32))
    a = -jnp.exp(a_log.astype(f32))
    y = ssd_chunked(xh, dt, a, bm, cm) + d_skip.astype(f32)[:, None] * xh
    y = y.reshape(bsz, seq, D_SSD) * jax.nn.silu(z.astype(f32))
    yg = y.reshape(bsz, seq, SSD_GROUPS, D_SSD // SSD_GROUPS)
    yg = yg * lax.rsqrt(jnp.mean(yg * yg, axis=-1, keepdims=True) + EPS)
    return (yg.reshape(bsz, seq, D_SSD) * norm_w.astype(f32)).astype(z.dtype)


def stick_breaking_mixer(q, k, v, q_norm_w, k_norm_w):
    bsz, seq, _ = q.shape
    f32 = jnp.float32
    q = rms_norm(q.reshape(bsz, seq, N_SB_HEADS, HEAD_DIM), q_norm_w).astype(f32).transpose(0, 2, 1, 3)
    k = rms_norm(k.reshape(bsz, seq, N_SB_HEADS, HEAD_DIM), k_norm_w).astype(f32).transpose(0, 2, 1, 3)
    vh = v.reshape(bsz, seq, N_SB_HEADS, HEAD_DIM).astype(f32).transpose(0, 2, 1, 3)
    scale = HEAD_DIM ** -0.5
    outs = []
    for blk in range(seq // SB_BLOCK):
        start, end = blk * SB_BLOCK, (blk + 1) * SB_BLOCK
        qb, kp, vp = q[:, :, start:end], k[:, :, :end], vh[:, :, :end]
        logits = jnp.einsum("bhqd,bhkd->bhqk", qb, kp) * scale
        t_pos = start + jnp.arange(SB_BLOCK)
        s_pos = jnp.arange(end)
        strict = s_pos[None, :] < t_pos[:, None]
        log_rem = jnp.where(strict, jax.nn.log_sigmoid(-logits), 0.0)
        log_between = lax.cumsum(log_rem, axis=3, reverse=True) - log_rem
        weights = jnp.where(strict, jnp.exp(jax.nn.log_sigmoid(logits) + log_between), 0.0)
        outs.append(jnp.einsum("bhqk,bhkd->bhqd", weights, vp))
    o = jnp.concatenate(outs, axis=2)
    return o.transpose(0, 2, 1, 3).reshape(bsz, seq, D_SB).astype(v.dtype)


def setup_inputs(seed: int = 0) -> dict:
    key = jax.random.key(seed)
    ks = jax.random.split(key, 24)
    f32 = jnp.float32
    nrm = lambda k, shape, s: jax.random.normal(k, shape, f32) * s
    dt0 = jnp.exp(jax.random.uniform(ks[8], (DEPTH, N_SSD_HEADS), f32,
                                     float(np.log(1e-3)), float(np.log(1e-1))))
    dt_bias = dt0 + jnp.log(-jnp.expm1(-dt0))
    return {
        "x": nrm(ks[0], (BATCH, SEQ, D_MODEL), 1.0),
        "c": nrm(ks[1], (BATCH, D_MODEL), 1.0),
        "w_ada": nrm(ks[2], (DEPTH, D_MODEL, N_MOD * D_MODEL), 0.5 * D_MODEL ** -0.5),
        "b_ada": nrm(ks[3], (DEPTH, N_MOD * D_MODEL), 0.01),
        "norm1_w": 1.0 + nrm(ks[4], (DEPTH, D_MODEL), 0.02),
        "w_in": nrm(ks[5], (DEPTH, D_MODEL, D_IN_PROJ), D_MODEL ** -0.5),
        "conv_w": nrm(ks[6], (DEPTH, CONV_WIDTH, D_CONV), CONV_WIDTH ** -0.5),
        "conv_b": nrm(ks[7], (DEPTH, D_CONV), 0.01),
        "dt_bias": dt_bias,
        "a_log": jnp.log(jax.random.uniform(ks[9], (DEPTH, N_SSD_HEADS), f32, 1.0, 16.0)),
        "d_skip": 1.0 + nrm(ks[10], (DEPTH, N_SSD_HEADS), 0.02),
        "ssd_norm_w": 1.0 + nrm(ks[11], (DEPTH, D_SSD), 0.02),
        "q_norm_w": 1.0 + nrm(ks[12], (DEPTH, HEAD_DIM), 0.02),
        "k_norm_w": 1.0 + nrm(ks[13], (DEPTH, HEAD_DIM), 0.02),
        "w_out": nrm(ks[14], (DEPTH, D_MIX, D_MODEL), D_MIX ** -0.5),
        "norm2_w": 1.0 + nrm(ks[15], (DEPTH, D_MODEL), 0.02),
        "w_gate": nrm(ks[16], (DEPTH, D_MODEL, D_FF), D_MODEL ** -0.5),
        "w_up": nrm(ks[17], (DEPTH, D_MODEL, D_FF), D_MODEL ** -0.5),
        "w_down": nrm(ks[18], (DEPTH, D_FF, D_MODEL), D_FF ** -0.5),
    }


def reference(x, c, w_ada, b_ada, norm1_w, w_in, conv_w, conv_b, dt_bias, a_log, d_skip,
              ssd_norm_w, q_norm_w, k_norm_w, w_out, norm2_w, w_gate, w_up, w_down):
    cond = jax.nn.silu(c)
    for layer in range(DEPTH):
        mod = (cond @ w_ada[layer] + b_ada[layer])[:, None, :]
        sh1, sc1, g1, sh2, sc2, g2 = jnp.split(mod, N_MOD, axis=-1)
        h = rms_norm(x, norm1_w[layer]) * (1.0 + sc1) + sh1
        proj = h @ w_in[layer]
        z, xbc, dt_raw, q, k, v = jnp.split(proj, IN_SPLITS, axis=-1)
        y_ssd = ssd_mixer(z, xbc, dt_raw, conv_w[layer], conv_b[layer], dt_bias[layer],
                          a_log[layer], d_skip[layer], ssd_norm_w[layer])
        y_sb = stick_breaking_mixer(q, k, v, q_norm_w[layer], k_norm_w[layer])
        mix = jnp.concatenate([y_ssd, y_sb], axis=-1) @ w_out[layer]
        x = x + g1 * mix
        h = rms_norm(x, norm2_w[layer]) * (1.0 + sc2) + sh2
        ffn = (jax.nn.silu(h @ w_gate[layer]) * (h @ w_up[layer])) @ w_down[layer]
        x = x + g2 * ffn
    return x
```

```python
import types
import numpy as np
import concourse.bass as bass
import concourse.mybir as mybir
from concourse.bass_utils import run_bass_kernel_spmd

F32 = mybir.dt.float32
BF16 = mybir.dt.bfloat16
AF = mybir.ActivationFunctionType
ALU = mybir.AluOpType

D = 1024
SEQ = 4096
NB = 4
DFF = 2816
NCH = 22
EPS = 1e-6
WIN = 2824
C_Z, C_X, C_B, C_C, C_Q, C_K, C_V, C_DT = 0, 512, 1024, 1152, 1280, 1792, 2304, 2816
TT = 512
NT = SEQ // TT
SB_BASE = 20480
SB_END = 229376
NO_ALIAS = False
STOP = None


def _freeze(fn):
    if fn is None or fn.__closure__ is None:
        return fn
    cells = []
    for c in fn.__closure__:
        try:
            cells.append(types.CellType(c.cell_contents))
        except ValueError:
            cells.append(c)
    return types.FunctionType(fn.__code__, fn.__globals__, fn.__name__, fn.__defaults__, tuple(cells))


class Sched:
    ENG = ("pe", "act", "dve", "pool", "sp")

    def __init__(self, nc):
        self.nc = nc
        self.e = {"pe": nc.tensor, "act": nc.scalar, "dve": nc.vector, "pool": nc.gpsimd, "sp": nc.sync}
        self.ops = []
        self.last_w = {}
        self.readers = {}
        self.fence_idx = None
        self.last_on = {}
        self.last_dma = {}

    def _add(self, eng, fn, r, w, kind, sk=None):
        idx = len(self.ops)
        deps = {}
        for k in r:
            p = self.last_w.get(k)
            if p is not None:
                deps[p] = True
        for k in w:
            p = self.last_w.get(k)
            if p is not None:
                deps.setdefault(p, False)
            for p in self.readers.get(k, ()):
                if p != idx:
                    deps.setdefault(p, False)
        if self.fence_idx is not None:
            deps[self.fence_idx] = True
        op = dict(eng=eng, fn=_freeze(fn), kind=kind, deps=deps, sk=sk, sig=False)
        for p, raw in deps.items():
            po = self.ops[p]
            need = po["kind"] != "c" or kind != "c" or po["eng"] != eng or raw or eng != "pe"
            if need:
                po["sig"] = True
        for k in r:
            self.readers.setdefault(k, []).append(idx)
        for k in w:
            self.last_w[k] = idx
            self.readers[k] = []
        self.ops.append(op)
        if kind == "c":
            self.last_on[eng] = idx
        else:
            self.last_dma[sk] = idx
        return idx

    def op(self, eng, fn, r=(), w=()):
        r = tuple(r)
        w = tuple(w) + tuple(k for k in r if isinstance(k, tuple) and k and k[0] in ("ps", "ps0") and k not in w)
        return self._add(eng, fn, r, w, "c")

    def dma(self, fn, r=(), w=(), q="sp", sk=None):
        w = tuple(w)
        if sk is None:
            sk = ("dma",) + tuple(w[:1])
        return self._add(q, fn, tuple(r), w, "d", sk)

    def cc(self, fn, r=(), w=(), sk=None):
        return self._add("pool", fn, tuple(r), tuple(w), "cc", sk)

    def fence(self):
        deps = {}
        for e, i in self.last_on.items():
            deps[i] = True
        for sk, i in self.last_dma.items():
            deps[i] = True
        idx = len(self.ops)
        op = dict(eng="sp", fn=None, kind="f", deps=deps, sk=None, sig=True)
        for p in deps:
            self.ops[p]["sig"] = True
        self.ops.append(op)
        self.fence_idx = idx
        self.last_on = {"sp": idx}
        self.last_dma = {}

    def emit(self):
        nc = self.nc
        sems = {}

        def sem_for(name):
            if name not in sems:
                sems[name] = nc.alloc_semaphore("s%d" % len(sems))
            return sems[name]

        cnt = {}
        for op in self.ops:
            if op["kind"] in ("c", "f"):
                key = ("eng", op["eng"])
                inc = 1
            elif op["kind"] == "d":
                key = op["sk"]
                inc = 16
                op["sig"] = True
            else:
                key = op["sk"]
                inc = 1
                op["sig"] = True
            if op["sig"]:
                cnt[key] = cnt.get(key, 0) + inc
                op["sem"] = key
                op["cnt"] = cnt[key]
                op["inc"] = inc
        seen = {e: {} for e in self.ENG}
        nwait = 0
        for op in self.ops:
            eng = op["eng"]
            E = self.e[eng]
            sn = seen[eng]
            need = {}
            for p, raw in op["deps"].items():
                po = self.ops[p]
                if po["kind"] == "c" and op["kind"] == "c" and po["eng"] == eng and not raw and eng == "pe":
                    continue
                k, c = po["sem"], po["cnt"]
                if sn.get(k, 0) >= c:
                    continue
                if need.get(k, 0) < c:
                    need[k] = c
            for k, c in need.items():
                E.wait_ge(sem_for(k), c)
                nwait += 1
                sn[k] = c
            own = ("eng", eng)
            for p, raw in op["deps"].items():
                po = self.ops[p]
                snap = po.get("snap")
                if snap:
                    skipped = po["kind"] == "c" and op["kind"] == "c" and po["eng"] == eng and not raw and eng == "pe"
                    for k, c in snap.items():
                        if skipped and k == own:
                            continue
                        if sn.get(k, 0) < c:
                            sn[k] = c
            if op["kind"] == "f":
                E.sem_inc(sem_for(op["sem"]), 1)
            else:
                ins = op["fn"]()
                if op["sig"]:
                    ins.then_inc(sem_for(op["sem"]), op["inc"])
            if op["sig"]:
                op["snap"] = dict(sn)
                op["snap"][op["sem"]] = op["cnt"]
            op["fn"] = None
        self.stats = (len(self.ops), nwait, len(sems))


class Alloc:
    def __init__(self, nc):
        self.nc = nc
        self.off = SB_BASE
        self.n = 0

    def __call__(self, shape, dt, name=None):
        nbytes = int(np.prod(shape[1:])) * (4 if dt == F32 else 2)
        nbytes = (nbytes + 63) // 64 * 64
        assert NO_ALIAS or self.off + nbytes <= SB_END, ("SBUF overflow", name, self.off, nbytes)
        self.n += 1
        t = self.nc.alloc_sbuf_tensor_at("t%d_%s" % (self.n, name or "x"), list(shape), dt, offset=self.off)
        self.off += nbytes
        return t

    def mark(self):
        return self.off

    def release(self, m):
        if not NO_ALIAS:
            self.off = m


def build(debug=False, upto=3, ntiles=NT, fake_cc=False, skip1a=False, only3=False):
    nc = bass.Bass("TRN2", target_bir_lowering=False)
    S = Sched(nc)
    A = Alloc(nc)
    pe, act, dve, pool, sp = nc.tensor, nc.scalar, nc.vector, nc.gpsimd, nc.sync

    def din(name, shape, dt=F32):
        return nc.dram_tensor(name, list(shape), dt, kind="ExternalInput")

    x_d = din("x", [SEQ, D])
    xh_d = din("xh", [SEQ // 2, D])
    cT_d = din("cT", [128, 8])
    wada_d = din("w_ada", [D, 6 * D])
    bpp_d = din("b_pp", [128, 32])
    bg_d = din("b_g", [2, D])
    n1w_d = din("n1w", [128, 8])
    n2w_d = din("n2w", [128, 8])
    win_d = din("w_in", [D, WIN])
    cw_d = din("conv_w", [128, 6, 4])
    cb_d = din("conv_b", [128, 6])
    vec8_d = din("vec8", [3, 8])
    dsk_d = din("dsk", [128, 4])
    snw_d = din("snw", [128, 4])
    qkw_d = din("qkw", [128, 2])
    wout_d = din("w_out", [2 * D, D])
    wg_d = din("w_gate", [D, DFF])
    wu_d = din("w_up", [D, DFF])
    wd_d = din("w_down", [DFF, D])
    consts_d = din("consts", [128, 7, 128])
    flags_d = din("flags", [128, 2])
    out_d = nc.dram_tensor("out", [SEQ // 2, D], F32, kind="ExternalOutput")

    winb_d = nc.dram_tensor("winb", [D, WIN], BF16, kind="ExternalOutput")
    woutb_d = nc.dram_tensor("woutb", [2 * D, D], BF16, kind="ExternalOutput")
    wgb_d = nc.dram_tensor("wgb", [D, DFF], BF16, kind="ExternalOutput")
    wub_d = nc.dram_tensor("wub", [D, DFF], BF16, kind="ExternalOutput")
    wdb_d = nc.dram_tensor("wdb", [DFF, D], BF16, kind="ExternalOutput")
    hT_d2 = nc.dram_tensor("hTs", [128, 8 * SEQ], BF16, kind="ExternalInput") if skip1a else nc.dram_tensor("hTs", [128, 8 * SEQ], BF16, kind="ExternalOutput")

    class _HT:
        def ap(self):
            return hT_d2.ap().rearrange("p (k t) -> p k t", k=8)
    hT_d = _HT()
    HS = SEQ // 2
    yl_ssd = [nc.dram_tensor("yl_ssd%d" % i, [512, HS], BF16) for i in range(2)]
    yl_sb = [nc.dram_tensor("yl_sb%d" % i, [512, HS], BF16) for i in range(2)]
    if only3:
        skip1a = True
        ya_ssd = [nc.dram_tensor("ya_ssd%d" % i, [1024, HS], BF16, kind="ExternalInput") for i in range(2)]
        ya_sb = [nc.dram_tensor("ya_sb%d" % i, [1024, HS], BF16, kind="ExternalInput") for i in range(2)]
    else:
        ya_ssd = [nc.dram_tensor("ya_ssd%d" % i, [1024, HS], BF16) for i in range(2)]
        ya_sb = [nc.dram_tensor("ya_sb%d" % i, [1024, HS], BF16) for i in range(2)]

    def gather(src, dst, hf, name, ci):
        keys = [(name, t) for t in range(hf * 4, min(ntiles, hf * 4 + 4))]
        if not keys:
            return
        if fake_cc:
            ncol = (min(ntiles, hf * 4 + 4) - hf * 4) * TT
            for hh_ in range(2):
                S.dma((lambda hh_=hh_: sp.dma_start(out=dst[hf].ap()[hh_ * 512:(hh_ + 1) * 512, 0:ncol], in_=src[hf].ap()[:, 0:ncol])),
                      r=keys, w=[("ya", name, hf)], sk=("dma", "fcc", name, hf, hh_))
        else:
            S.cc(lambda: pool.collective_compute("AllGather", ALU.bypass, replica_groups=[[0, 1], [2, 3], [4, 5], [6, 7]],
                                                 ins=[src[hf].ap().opt()], outs=[dst[hf].ap().opt()]),
                 r=keys, w=[("ya", name, hf)], sk=("cc", ci))

    dbg = {}

    def dbg_out(name, shape, dt=F32):
        if debug:
            dbg[name] = nc.dram_tensor(name, list(shape), dt, kind="ExternalOutput")
            return dbg[name]
        return None

    ps = nc.alloc_psum_tensor("ps", [128, 8 * 512], F32)

    def bank(b, c0=0, c1=512):
        return ps[:, b * 512 + c0: b * 512 + c1]

    cst = A([128, 7, 128], F32, "cst")
    ident = cst[:, 0, :]
    UI = cst[:, 1, :]
    US = cst[:, 2, :]
    LI = cst[:, 3, :]
    LS = cst[:, 4, :]
    ONESF = cst[:, 5, :]
    BD = cst[:, 6, :]
    cbf = A([128, 4, 128], BF16, "cbf")
    negLI_b, negI_b, ones_b, US_b = cbf[:, 0, :], cbf[:, 1, :], cbf[:, 2, :], cbf[:, 3, :]
    flags = A([128, 2], F32, "flags")
    sv = A([128, 96], F32, "sv")
    cT = sv[:, 0:8]
    condT = sv[:, 8:16]
    n1w = sv[:, 16:24]
    n2w = sv[:, 24:32]
    modpp = sv[:, 32:64]
    s1 = sv[:, 64:72]
    s2 = sv[:, 72:80]
    bpp = A([128, 32], F32, "bpp")
    g1bc = A([128, D], F32, "g1bc")
    g2bc = A([128, D], F32, "g2bc")
    cw = A([128, 6, 4], F32, "cw")
    cbv = A([128, 6], F32, "cbv")
    v8 = A([128, 3, 8], F32, "v8")
    Aneg = A([128, 8], F32, "Aneg")
    dsk = A([128, 4], F32, "dsk")
    snw = A([128, 4], F32, "snw")
    qkw = A([128, 2], F32, "qkw")
    epsc = A([128, 1], F32, "epsc")

    S.dma(lambda: sp.dma_start(out=cst[:], in_=consts_d.ap()), w=["cst"])
    for (t, d, k) in ((flags, flags_d, "flags"), (bpp, bpp_d, "bpp"), (cw, cw_d, "cw"), (cbv, cbv_d if False else cb_d, "cbv"),
                      (dsk, dsk_d, "dsk"), (snw, snw_d, "snw"), (qkw, qkw_d, "qkw")):
        S.dma((lambda t=t, d=d: sp.dma_start(out=t[:], in_=d.ap())), w=[k])
    S.dma(lambda: sp.dma_start(out=sv[:, 0:8], in_=cT_d.ap()), w=["cT"])
    S.dma(lambda: sp.dma_start(out=sv[:, 16:24], in_=n1w_d.ap()), w=["n1w"])
    S.dma(lambda: sp.dma_start(out=sv[:, 24:32], in_=n2w_d.ap()), w=["n2w"])
    for r in range(2):
        S.dma((lambda r=r: sp.dma_start(out=v8[:, r, :], in_=vec8_d.ap()[r:r + 1, :].partition_broadcast(128))),
              w=[("v8", r)])
    S.dma(lambda: sp.dma_start(out=g1bc[:], in_=bg_d.ap()[0:1, :].partition_broadcast(128)), w=["g1bc"])
    S.dma(lambda: sp.dma_start(out=g2bc[:], in_=bg_d.ap()[1:2, :].partition_broadcast(128)), w=["g2bc"])

    S.op("dve", lambda: dve.memset(epsc[:], EPS), w=["epsc"])
    S.op("dve", lambda: dve.tensor_scalar_mul(cbf[:, 0, :], LI, -1.0), r=["cst"], w=["cbf0"])
    S.op("dve", lambda: dve.tensor_scalar_mul(cbf[:, 1, :], ident, -1.0), r=["cst"], w=["cbf1"])
    S.op("dve", lambda: dve.tensor_copy(cbf[:, 2, :], ONESF), r=["cst"], w=["cbf2"])
    S.op("dve", lambda: dve.tensor_copy(cbf[:, 3, :], US), r=["cst"], w=["cbf3"])
    CB = ["cbf0", "cbf1", "cbf2", "cbf3"]
    S.op("dve", lambda: dve.tensor_scalar_mul(qkw[:, 0:1], qkw[:, 0:1], 0.125), r=["qkw"], w=["qkw"])
    S.op("act", lambda: act.activation(Aneg[:], v8[:, 1, :], AF.Exp), r=[("v8", 1)], w=["Aneg"])
    S.op("dve", lambda: dve.tensor_scalar_mul(Aneg[:], Aneg[:], -1.0), r=["Aneg"], w=["Aneg"])
    S.op("act", lambda: act.activation(condT, cT, AF.Silu), r=["cT"], w=["condT"])

    PW = 2048
    mP = A.mark()
    stf = [A([128, PW], F32, "stf%d" % i) for i in range(2)]
    stb = [A([128, PW], BF16, "stb%d" % i) for i in range(2)]
    prep_list = []

    def prep_matrix(src, dst, rows, cols, key):
        nr = rows // 128
        ncp = (cols + PW - 1) // PW
        cw_ = (cols + ncp - 1) // ncp
        for r in range(nr):
            for c in range(ncp):
                c0, c1 = c * cw_, min(cols, (c + 1) * cw_)
                prep_list.append((src, dst, r, c0, c1, (key, r)))

    prep_matrix(win_d, winb_d, D, WIN, "winb")
    prep_matrix(wout_d, woutb_d, 2 * D, D, "woutb")
    prep_matrix(wg_d, wgb_d, D, DFF, "wgb")
    prep_matrix(wu_d, wub_d, D, DFF, "wub")
    prep_matrix(wd_d, wdb_d, DFF, D, "wdb")
    prep_state = {"next_load": 0, "next_cast": 0}

    def prep_load(i):
        src, dst, r, c0, c1, key = prep_list[i]
        b = i % 2
        S.dma((lambda: pool.dma_start(out=stf[b][:, 0:c1 - c0], in_=src.ap()[r * 128:(r + 1) * 128, c0:c1])),
              w=[("stf", b)], q="pool")

    def prep_cast_store(i):
        src, dst, r, c0, c1, key = prep_list[i]
        b = i % 2
        S.op("pool", (lambda: pool.tensor_copy(stb[b][:, 0:c1 - c0], stf[b][:, 0:c1 - c0])),
             r=[("stf", b)], w=[("stb", b)])
        S.dma((lambda: pool.dma_start(out=dst.ap()[r * 128:(r + 1) * 128, c0:c1], in_=stb[b][:, 0:c1 - c0])),
              r=[("stb", b)], w=[key + (c0,)], q="pool", sk=("dma", "stbo", b))

    def prep_advance(n):
        for _ in range(n):
            i = prep_state["next_cast"]
            if i >= len(prep_list):
                return
            if prep_state["next_load"] == 0:
                prep_load(0)
                prep_state["next_load"] = 1
            if prep_state["next_load"] < len(prep_list) and prep_state["next_load"] == i + 1:
                prep_load(i + 1)
                prep_state["next_load"] = i + 2
            prep_cast_store(i)
            prep_state["next_cast"] = i + 1

    def prep_keys(key, rows, cols):
        nr = rows // 128
        ncp = (cols + PW - 1) // PW
        cw_ = (cols + ncp - 1) // ncp
        return [(key, r, c * cw_) for r in range(nr) for c in range(ncp)]

    def finish():
        prep_advance(1000)
        S.fence()
        S.emit()
        return nc, S, dbg

    prep_advance(16)

    m0 = A.mark()
    cbc = A([128, 8, 128], F32, "cbc")
    NWST = 6
    wst = [A([128, 2048], F32, "wst%d" % i) for i in range(NWST)]
    S.op("dve", lambda: dve.tensor_copy(cbc[:], condT.unsqueeze(2).to_broadcast([128, 8, 128])), r=["condT"], w=["cbc"])
    pc = 0
    for cg in range(3):
        for k in range(8):
            b = pc % NWST
            qn = "sp" if pc % 2 == 0 else "act"
            pc += 1
            S.dma((lambda b=b, k=k, cg=cg, qn=qn: S.e[qn].dma_start(out=wst[b][:], in_=wada_d.ap()[k * 128:(k + 1) * 128, cg * 2048:(cg + 1) * 2048])),
                  w=[("wst", b)], q=qn)
            st, sp_ = (k == 0), (k == 7)

            def ppmm(colbase, ncols, wofs, b=b, k=k, st=st, sp_=sp_):
                for cc in range(ncols):
                    S.op("pe", (lambda cc=cc: pe.matmul(bank(0, colbase + cc, colbase + cc + 1),
                                                        wst[b][:, wofs + cc * 128: wofs + (cc + 1) * 128],
                                                        condT[:, k:k + 1], start=(st and colbase == 0 and cc == 0), stop=sp_,
                                                        skip_group_check=True)),
                         r=[("wst", b), "condT"], w=[("ps0", colbase + cc)])

            def bcmm(bk, wofs, b=b, k=k, st=st, sp_=sp_):
                for h in range(2):
                    S.op("pe", (lambda h=h: pe.matmul(bank(bk + h), cbc[:, k, :],
                                                      wst[b][:, wofs + h * 512: wofs + (h + 1) * 512], start=st, stop=sp_)),
                         r=[("wst", b), "cbc"], w=[("ps", bk + h)])
            if cg == 0:
                ppmm(0, 16, 0)
            elif cg == 1:
                bcmm(1, 0)
                ppmm(16, 8, 1024)
            else:
                ppmm(24, 8, 0)
                bcmm(3, 1024)
    S.op("dve", lambda: dve.tensor_tensor(modpp, bank(0, 0, 32), bpp[:], ALU.add),
         r=[("ps0", c) for c in range(32)] + ["bpp"], w=["modpp"])
    for h in range(2):
        S.op("dve", (lambda h=h: dve.tensor_tensor(g1bc[:, h * 512:(h + 1) * 512], bank(1 + h), g1bc[:, h * 512:(h + 1) * 512], ALU.add)),
             r=[("ps", 1 + h), "g1bc"], w=["g1bc"])
        S.op("dve", (lambda h=h: dve.tensor_tensor(g2bc[:, h * 512:(h + 1) * 512], bank(3 + h), g2bc[:, h * 512:(h + 1) * 512], ALU.add)),
             r=[("ps", 3 + h), "g2bc"], w=["g2bc"])
    S.op("dve", lambda: dve.scalar_tensor_tensor(s1, modpp[:, 8:16], 1.0, n1w, ALU.add, ALU.mult), r=["modpp", "n1w"], w=["s1"])
    S.op("dve", lambda: dve.scalar_tensor_tensor(s2, modpp[:, 24:32], 1.0, n2w, ALU.add, ALU.mult), r=["modpp", "n2w"], w=["s2"])
    t1 = modpp[:, 0:8]
    t2 = modpp[:, 16:24]
    if debug:
        d_mod = dbg_out("d_mod", [128, 96])
        S.dma(lambda: sp.dma_start(out=d_mod.ap()[:, 0:80], in_=sv[:, 0:80]), r=["modpp", "s1", "s2", "condT", "cT", "n1w", "n2w"], w=["d_mod"])
        d_g = dbg_out("d_g", [128, 2 * D])
        S.dma(lambda: sp.dma_start(out=d_g.ap()[:, 0:D], in_=g1bc[:]), r=["g1bc"], w=["d_g1"])
        S.dma(lambda: sp.dma_start(out=d_g.ap()[:, D:2 * D], in_=g2bc[:]), r=["g2bc"], w=["d_g2"])
    S.fence()
    A.release(m0)
    if upto < 1:
        prep_advance(1000)
        S.fence()
        S.emit()
        return nc, S, dbg

    WINK = prep_keys("winb", D, WIN)
    winb_v = winb_d.ap().rearrange("(k p) c -> p k c", p=128)

    def rstd_from_ssq(ssq, n, keyr, keyw):
        S.op("act", lambda: act.activation(ssq, ssq, AF.Ln, bias=epsc[:], scale=1.0 / n), r=[keyr, "epsc"], w=[keyw])
        S.op("act", lambda: act.activation(ssq, ssq, AF.Exp, scale=-0.5), r=[keyw], w=[keyw])

    m1 = A.mark()
    W1 = 1288
    w1 = A([128, 8, W1], BF16, "w1")
    S.dma(lambda: sp.dma_start(out=w1[:, :, 0:1280], in_=winb_v[:, :, 0:1280]), r=WINK, w=["w1a"])
    S.dma(lambda: sp.dma_start(out=w1[:, :, 1280:1288], in_=winb_v[:, :, C_DT:C_DT + 8]), r=WINK, w=["w1b"])
    W1K = ["w1a", "w1b"]
    xs = [A([128, D], F32, "xs%d" % i) for i in range(4)]
    sq_junk = A([128, D], BF16, "sqj")
    ssq = A([128, 8], F32, "ssq")
    hT = A([128, 8, TT], BF16, "hT")
    zs = A([128, 4, TT], F32, "zs")
    u = A([128, 6, TT + 3], F32, "u")
    acc = A([128, 6, TT], F32, "acc")
    BTb = A([128, TT], BF16, "BTb")
    CTb = A([128, TT], BF16, "CTb")
    dtb = A([128, 4, 8], F32, "dtb")
    adt4 = A([128, 4, 8], F32, "adt")
    rseg2 = [A([128, 8, 128], F32, "rseg%d" % i) for i in range(2)]
    dec2 = [A([128, 8, 128], F32, "dec%d" % i) for i in range(2)]
    eac4 = A([128, 4, 4, 128], F32, "eac")
    dst4 = A([128, 4, 8], F32, "dst")
    cdec4 = A([128, 4, 8], F32, "cdec")
    xg4 = A([128, 4, 512], BF16, "xg")
    xgd4 = A([128, 4, 512], BF16, "xgd")
    Btok4 = A([128, 4, 128], BF16, "Btok")
    sctm2 = [A([128, 128], F32, "sctm%d" % i) for i in range(2)]
    G4 = A([128, 4, 8, 128], BF16, "G")
    ydg4 = A([128, 4, 512], F32, "ydg")
    prevT = A([128, 8, 64], F32, "prevT")
    prevb = A([128, 512], BF16, "prevb")
    tmpA = A([128, 512], F32, "tmpA")
    ytile = A([128, 4, TT], F32, "ytile")
    ysq = A([128, 4, TT], F32, "ysq")
    rbc = A([128, TT], F32, "rbc")
    yT = A([128, 4, TT], BF16, "yT")
    yl_ssd_v = [t_.ap().rearrange("(c p) t -> p c t", p=128) for t_ in yl_ssd]

    S.op("dve", lambda: dve.memset(u[:], 0.0), w=["u_halo"] + [("u", j) for j in range(6)])
    S.op("dve", lambda: dve.memset(prevT[:], 0.0), w=["prevT"])
    S.op("dve", lambda: dve.memset(prevb[:], 0.0), w=["prevb"])
    if debug and not skip1a:
        d_hT = dbg_out("d_hT", [128, 8, TT], BF16)
        d_xc = dbg_out("d_xc", [128, 6, TT])
        d_yt = dbg_out("d_yt", [128, 4, TT])

    if STOP == "w1":
        return finish()
    HTK = [("hT", c, s) for c in range(8) for s in range(4)]

    def hT_part1(tt):
        for s in range(4):
            r0 = (tt * 4 + s) * 128
            S.dma((lambda s=s, r0=r0: sp.dma_start(out=xs[s][:], in_=x_d.ap()[r0:r0 + 128, :])), w=[("xs", s)])
        for s in range(4):
            S.op("act", (lambda s=s: act.activation(sq_junk[:], xs[s][:], AF.Square, accum_out=ssq[:, s:s + 1])), r=[("xs", s)], w=["sqj", ("ssq", s)])
        for s in range(4):
            S.op("act", (lambda s=s: act.activation(ssq[:, s:s + 1], ssq[:, s:s + 1], AF.Ln, bias=epsc[:], scale=1.0 / D)), r=[("ssq", s), "epsc"], w=[("ssq", s)])
        for s in range(4):
            S.op("act", (lambda s=s: act.activation(ssq[:, s:s + 1], ssq[:, s:s + 1], AF.Exp, scale=-0.5)), r=[("ssq", s)], w=[("ssq", s)])
        for s in range(4):
            S.op("dve", (lambda s=s: dve.tensor_scalar_mul(xs[s][:], xs[s][:], ssq[:, s:s + 1])), r=[("xs", s), ("ssq", s)], w=[("xs", s)])

    def hT_tr(tt, s):
        for half in range(2):
            bk = 4 + (s % 2) * 2 + half
            for c4 in range(4):
                c = half * 4 + c4
                S.op("pe", (lambda c=c, c4=c4, bk=bk: pe.transpose(bank(bk, c4 * 128, (c4 + 1) * 128), xs[s][:, c * 128:(c + 1) * 128], ident)),
                     r=[("xs", s), "cst"], w=[("ps", bk)])

    def hT_ev(tt, s):
        for half in range(2):
            bk = 4 + (s % 2) * 2 + half
            for c4 in range(4):
                c = half * 4 + c4
                S.op("dve", (lambda c=c, c4=c4, bk=bk: dve.tensor_scalar(hT[:, c, s * 128:(s + 1) * 128], bank(bk, c4 * 128, (c4 + 1) * 128),
                                                                      s1[:, c:c + 1], t1[:, c:c + 1], ALU.mult, ALU.add)),
                     r=[("ps", bk), "s1", "modpp"], w=[("hT", c, s)])

    def hT_part2(tt):
        hT_tr(tt, 0); hT_tr(tt, 1); hT_ev(tt, 0); hT_tr(tt, 2); hT_ev(tt, 1); hT_tr(tt, 3); hT_ev(tt, 2); hT_ev(tt, 3)
        S.dma((lambda: sp.dma_start(out=hT_d.ap()[:, :, tt * TT:(tt + 1) * TT], in_=hT[:])), r=HTK, w=[("hTd", tt)], sk=("dma", "hTd"))
        if debug and tt == 1:
            S.dma(lambda: sp.dma_start(out=d_hT.ap(), in_=hT[:]), r=HTK, w=["d_hT"])

    if not skip1a:
        hT_part1(0)
        hT_part2(0)
    for tt in range(0 if skip1a else ntiles):
        prep_advance(6)
        for j in range(10):
            bk = 4 + (j % 4)
            for k in range(8):
                S.op("pe", (lambda j=j, k=k, bk=bk: pe.matmul(bank(bk), w1[:, k, j * 128:(j + 1) * 128], hT[:, k, :], start=(k == 0), stop=(k == 7))),
                     r=W1K + [("hT", k, s) for s in range(4)], w=[("ps", bk)])
            if j < 4:
                S.op("act", (lambda j=j, bk=bk: act.activation(zs[:, j, :], bank(bk), AF.Silu)), r=[("ps", bk)], w=[("zs", j)])
            else:
                jj = j - 4
                S.op("act", (lambda jj=jj, bk=bk: act.activation(acc[:, jj, :], bank(bk), AF.Identity, bias=cbv[:, jj:jj + 1], scale=cw[:, jj, 3:4])),
                     r=[("ps", bk), "cw", "cbv"], w=[("acc", jj)])
                S.op("act", (lambda jj=jj, bk=bk: act.copy(u[:, jj, 3:TT + 3], bank(bk))), r=[("ps", bk), "u_halo"], w=[("u", jj)])
                for kk in (2, 1, 0):
                    S.op("dve", (lambda jj=jj, kk=kk: dve.scalar_tensor_tensor(acc[:, jj, :], u[:, jj, kk:kk + TT], cw[:, jj, kk:kk + 1], acc[:, jj, :], ALU.mult, ALU.add)),
                         r=[("u", jj), ("acc", jj), "cw"], w=[("acc", jj)])
                if jj < 4:
                    S.op("act", (lambda jj=jj: act.activation(acc[:, jj, :], acc[:, jj, :], AF.Silu)), r=[("acc", jj)], w=[("acc", jj)])
                else:
                    S.op("act", (lambda jj=jj: act.activation(acc[:, jj, :], acc[:, jj, :], AF.Silu)), r=[("acc", jj)], w=[("acc", jj)])
                    tb = BTb if jj == 4 else CTb
                    S.op("dve", (lambda jj=jj, tb=tb: dve.tensor_copy(tb[:], acc[:, jj, :])), r=[("acc", jj)], w=["BTb" if jj == 4 else "CTb"])
        if STOP == "inproj":
            return finish()
        S.op("dve", lambda: dve.tensor_copy(u[:, :, 0:3], u[:, :, TT:TT + 3]), r=[("u", j) for j in range(6)], w=["u_halo"] + [("u", j) for j in range(6)])
        if debug and tt == 1:
            S.dma(lambda: sp.dma_start(out=d_xc.ap(), in_=acc[:]), r=[("acc", j) for j in range(6)], w=["d_xc"])
        for s in range(4):
            for k in range(8):
                S.op("pe", (lambda s=s, k=k: pe.matmul(bank(1, 264 + s * 8, 272 + s * 8), hT[:, k, s * 128:(s + 1) * 128], w1[:, k, 1280:1288], start=(k == 0), stop=(k == 7))),
                     r=W1K + [("hT", k, s)], w=[("ps", 1)])
        S.op("dve", lambda: dve.tensor_tensor(dtb[:], bank(1, 264, 296).rearrange("p (s h) -> p s h", s=4),
                                              v8[:, 0, :].unsqueeze(1).to_broadcast([128, 4, 8]), ALU.add),
             r=[("ps", 1), ("v8", 0)], w=["dtb"])
        S.op("act", lambda: act.activation(dtb[:], dtb[:], AF.Exp), r=["dtb"], w=["dtb"])
        S.op("act", lambda: act.activation(dtb[:], dtb[:], AF.Ln, bias=1.0), r=["dtb"], w=["dtb"])
        if STOP == "dt":
            return finish()
        def stA(ci):
            c0, c1 = ci * 128, (ci + 1) * 128
            dtc = dtb[:, ci, :]
            for j in range(4):
                S.op("pe", (lambda j=j: pe.transpose(bank(0, j * 128, (j + 1) * 128), acc[:, j, c0:c1], ident)),
                     r=[("acc", j), "cst"], w=[("ps", 0)])
            S.op("pe", (lambda: pe.transpose(bank(1, 0, 128), acc[:, 4, c0:c1], ident)), r=[("acc", 4), "cst"], w=[("ps", 1)])
            S.op("dve", (lambda: dve.tensor_tensor(xg4[:, ci, :].rearrange("p (h e) -> p h e", h=8), bank(0).rearrange("p (h e) -> p h e", h=8),
                                                   dtc.unsqueeze(2).to_broadcast([128, 8, 64]), ALU.mult)),
                 r=[("ps", 0), "dtb"], w=[("xg", ci)])
            S.op("act", (lambda: act.copy(Btok4[:, ci, :], bank(1, 0, 128))), r=[("ps", 1)], w=[("Btok", ci)])

        def stB1(ci):
            dtc = dtb[:, ci, :]
            rs = rseg2[ci % 2]
            S.op("dve", (lambda: dve.tensor_tensor(adt4[:, ci, :], dtc, Aneg[:], ALU.mult)), r=["dtb", "Aneg"], w=[("adt", ci)])
            S.op("dve", (lambda: dve.tensor_tensor(rs[:], UI.unsqueeze(1).to_broadcast([128, 8, 128]),
                                                   adt4[:, ci, :].unsqueeze(2).to_broadcast([128, 8, 128]), ALU.mult)),
                 r=[("adt", ci), "cst"], w=[("rseg", ci % 2)])

        def stB2(ci):
            rs = rseg2[ci % 2]
            dc = dec2[ci % 2]
            for hf in range(2):
                rv = rs[:, hf * 4:(hf + 1) * 4, :].rearrange("p h l -> p (h l)")
                S.op("pe", (lambda hf=hf, rv=rv: pe.matmul(bank(2 + hf), LS, rv, start=True, stop=True)), r=[("rseg", ci % 2), "cst"], w=[("ps", 2 + hf)])
                S.op("pe", (lambda hf=hf, rv=rv: pe.matmul(bank(4 + hf), ONESF, rv, start=True, stop=True)), r=[("rseg", ci % 2), "cst"], w=[("ps", 4 + hf)])
            S.op("pe", (lambda: pe.matmul(bank(1, 256, 264), LS, adt4[:, ci, :], start=True, stop=True)), r=[("adt", ci), "cst"], w=[("ps", 1)])
            S.op("act", (lambda: act.activation(dst4[:, ci, :], bank(1, 256, 264), AF.Exp)), r=[("ps", 1)], w=[("dst", ci)])
            for hf in range(2):
                S.op("act", (lambda hf=hf: act.activation(dc[:, hf * 4:(hf + 1) * 4, :].rearrange("p h l -> p (h l)"), bank(2 + hf), AF.Exp)),
                     r=[("ps", 2 + hf)], w=[("dec", ci % 2, hf)])
            acb = ps[:, 4 * 512: 6 * 512].rearrange("p (pr two l) -> p pr two l", pr=4, two=2)
            S.op("act", (lambda: act.activation(eac4[0:64, ci, :, :], acb[0:64, :, 0, :], AF.Exp)), r=[("ps", 4), ("ps", 5)], w=[("eac", ci, 0)])
            S.op("act", (lambda: act.activation(eac4[64:128, ci, :, :], acb[64:128, :, 1, :], AF.Exp)), r=[("ps", 4), ("ps", 5)], w=[("eac", ci, 1)])
            acl = ps[:, 4 * 512: 6 * 512].rearrange("p (h l) -> p h l", h=8)
            S.op("act", (lambda: act.activation(cdec4[:, ci, :], acl[:, :, 127], AF.Exp)), r=[("ps", 4), ("ps", 5)], w=[("cdec", ci)])

        def stC(ci):
            c0, c1 = ci * 128, (ci + 1) * 128
            dc = dec2[ci % 2]
            sm = sctm2[ci % 2]
            S.op("pe", (lambda: pe.matmul(bank(1, 128, 256), BTb[:, c0:c1], CTb[:, c0:c1], start=True, stop=True)),
                 r=["BTb", "CTb"], w=[("ps", 1)])
            S.op("dve", (lambda: dve.tensor_tensor(sm[:], bank(1, 128, 256), UI, ALU.mult)), r=[("ps", 1), "cst"], w=[("sctm", ci % 2)])
            S.op("dve", (lambda: dve.tensor_tensor(xgd4[:, ci, :].rearrange("p (h e) -> p h e", h=8), xg4[:, ci, :].rearrange("p (h e) -> p h e", h=8),
                                                   dst4[:, ci, :].unsqueeze(2).to_broadcast([128, 8, 64]), ALU.mult)),
                 r=[("xg", ci), ("dst", ci)], w=[("xgd", ci)])
            S.op("dve", (lambda: dve.tensor_tensor(G4[:, ci, :, :], dc[:], sm[:].unsqueeze(1).to_broadcast([128, 8, 128]), ALU.mult)),
                 r=[("dec", ci % 2, 0), ("dec", ci % 2, 1), ("sctm", ci % 2)], w=[("G", ci)])

        def stD(ci):
            for h in range(8):
                pr, hh = h // 2, h % 2
                S.op("pe", (lambda h=h, pr=pr, hh=hh: pe.matmul(ps[hh * 64:(hh + 1) * 64, 6 * 512 + pr * 128: 6 * 512 + (pr + 1) * 128],
                                                               xg4[:, ci, h * 64:(h + 1) * 64], G4[:, ci, h, :], start=True, stop=True)),
                     r=[("xg", ci), ("G", ci)], w=[("ps", 6)])
            S.op("act", (lambda: act.copy(ydg4[:, ci, :], bank(6))), r=[("ps", 6)], w=[("ydg", ci)])

        def stII(ci):
            c0, c1 = ci * 128, (ci + 1) * 128
            for pr in range(4):
                S.op("pe", (lambda pr=pr: pe.matmul(bank(7, pr * 128, (pr + 1) * 128), prevb[:, pr * 128:(pr + 1) * 128], CTb[:, c0:c1], start=True, stop=True)),
                     r=["prevb", "CTb"], w=[("ps", 7)])
            S.op("dve", (lambda: dve.tensor_tensor(tmpA[:], bank(7), eac4[:, ci, :, :].rearrange("p a l -> p (a l)"), ALU.mult)),
                 r=[("ps", 7), ("eac", ci, 0), ("eac", ci, 1)], w=["tmpA"])
            S.op("pe", (lambda: pe.matmul(bank(7), Btok4[:, ci, :], xgd4[:, ci, :], start=True, stop=True)), r=[("Btok", ci), ("xgd", ci)], w=[("ps", 7)])
            S.op("dve", (lambda: dve.tensor_tensor(prevT[:], prevT[:], cdec4[:, ci, :].unsqueeze(2).to_broadcast([128, 8, 64]), ALU.mult)),
                 r=["prevT", ("cdec", ci)], w=["prevT"])
            S.op("dve", (lambda: dve.tensor_tensor(ytile[:, :, c0:c1], ydg4[:, ci, :].rearrange("p (a l) -> p a l", a=4),
                                                   tmpA[:].rearrange("p (a l) -> p a l", a=4), ALU.add)),
                 r=[("ydg", ci), "tmpA"], w=[("ytile", ci)])
            S.op("dve", (lambda: dve.tensor_tensor(prevT[:], prevT[:], bank(7).rearrange("p (h e) -> p h e", h=8), ALU.add)),
                 r=["prevT", ("ps", 7)], w=["prevT"])
            S.op("dve", (lambda: dve.tensor_copy(prevb[:], prevT[:].rearrange("p h e -> p (h e)"))), r=["prevT"], w=["prevb"])

        for ci in range(4):
            stA(ci)
        stB1(0); stB1(1); stB2(0); stB1(2); stB2(1); stC(0); stB1(3); stB2(2); stC(1); stB2(3); stC(2); stC(3)
        for ci in range(4):
            stD(ci)
        for ci in range(4):
            stII(ci)
        if STOP == "chunks":
            return finish()
        if tt + 1 < ntiles:
            hT_part1(tt + 1)
        YK = [("ytile", ci) for ci in range(4)]
        S.op("dve", lambda: dve.tensor_tensor(ysq[:], acc[:, 0:4, :], dsk[:].unsqueeze(2).to_broadcast([128, 4, TT]), ALU.mult),
             r=[("acc", j) for j in range(4)] + ["dsk"], w=["ysq"])
        S.op("dve", lambda: dve.tensor_tensor(ytile[:], ytile[:], ysq[:], ALU.add), r=YK + ["ysq"], w=YK)
        S.op("dve", lambda: dve.tensor_tensor(ytile[:], ytile[:], zs[:], ALU.mult), r=YK + [("zs", j) for j in range(4)], w=YK)
        if debug and tt == 1:
            S.dma(lambda: sp.dma_start(out=d_yt.ap(), in_=ytile[:]), r=YK, w=["d_yt"])
        S.op("act", lambda: act.activation(ysq[:], ytile[:], AF.Square), r=YK, w=["ysq"])
        for a in range(4):
            S.op("pe", (lambda a=a: pe.matmul(bank(2), ONESF, ysq[:, a, :], start=(a == 0), stop=(a == 3))), r=["ysq", "cst"], w=[("ps", 2)])
        S.op("act", lambda: act.activation(rbc[:], bank(2), AF.Ln, bias=epsc[:], scale=1.0 / 512), r=[("ps", 2), "epsc"], w=["rbc"])
        S.op("act", lambda: act.activation(rbc[:], rbc[:], AF.Exp, scale=-0.5), r=["rbc"], w=["rbc"])
        S.op("dve", lambda: dve.tensor_tensor(ytile[:], ytile[:], rbc[:].unsqueeze(1).to_broadcast([128, 4, TT]), ALU.mult), r=YK + ["rbc"], w=YK)
        S.op("dve", lambda: dve.tensor_tensor(yT[:], ytile[:], snw[:].unsqueeze(2).to_broadcast([128, 4, TT]), ALU.mult), r=YK + ["snw"], w=["yT"])
        S.dma((lambda tt=tt: sp.dma_start(out=yl_ssd_v[tt // 4][:, :, (tt % 4) * TT:(tt % 4 + 1) * TT], in_=yT[:])), r=["yT"], w=[("yl_ssd", tt)], sk=("dma", "yT"))
        if tt + 1 < ntiles:
            hT_part2(tt + 1)
        if tt == 3:
            prep_advance(1000)
            gather(yl_ssd, ya_ssd, 0, "yl_ssd", 0)

    prep_advance(1000)
    if debug and not skip1a:
        d_yssd = dbg_out("d_yssd", [512, SEQ], BF16)
        for hf_ in range(2):
            nt_ = min(ntiles, hf_ * 4 + 4) - hf_ * 4
            if nt_ > 0:
                S.dma((lambda hf_=hf_, nt_=nt_: sp.dma_start(out=d_yssd.ap()[:, hf_ * HS: hf_ * HS + nt_ * TT], in_=yl_ssd[hf_].ap()[:, 0:nt_ * TT])),
                      r=[("yl_ssd", t) for t in range(ntiles)], w=[("d_yssd", hf_)])
    S.fence()
    A.release(m1)
    if upto < 2:
        S.emit()
        return nc, S, dbg
    if not skip1a:
        gather(yl_ssd, ya_ssd, 1, "yl_ssd", 1)

    m2 = A.mark()
    W2 = 1536
    w2 = A([128, 8, W2], BF16, "w2")
    S.dma(lambda: sp.dma_start(out=w2[:], in_=winb_v[:, :, C_Q:C_Q + W2]), r=WINK, w=["w2"])
    kT = A([128, 4, SEQ], BF16, "kT")
    vb = A([128, 32, 512], BF16, "vb")
    hT2 = [A([128, 8, TT], BF16, "hT2_%d" % i) for i in range(2)]
    qT = A([128, 4, TT], BF16, "qT")
    sqf = [A([128, TT], F32, "sqf%d" % i) for i in range(2)]
    rq = [A([128, TT], F32, "rq%d" % i) for i in range(2)]
    ebuf = [A([128, 512], F32, "e%d" % i) for i in range(2)]
    spb2 = [[A([128, 512], BF16, "sp%d_%d" % (i, q)) for q in range(2)] for i in range(2)]
    spm = [A([128, 512], BF16, "spm%d" % i) for i in range(2)]
    wb2 = [[A([128, 512], BF16, "w%d_%d" % (i, q)) for q in range(2)] for i in range(2)]
    wm = [A([128, 512], BF16, "wm%d" % i) for i in range(2)]
    cbb = [A([128, 512], BF16, "cb%d" % i) for i in range(2)]
    ysb = A([128, 4, TT], BF16, "ysb")
    yl_sb_v = [t_.ap().rearrange("(c p) t -> p c t", p=128) for t_ in yl_sb]
    US4 = US_b.unsqueeze(1).to_broadcast([128, 4, 128])
    ZBP = ((0, 1), (2, 3))
    CBK = (4, 5)
    OB = 6
    if debug and not only3:
        d_q = dbg_out("d_q", [128, 4, TT], BF16)

    for tt in range(0 if only3 else ntiles):
        hb = hT2[tt % 2]
        kh = ("hT2", tt % 2)
        S.dma((lambda hb=hb, tt=tt: sp.dma_start(out=hb[:], in_=hT_d.ap()[:, :, tt * TT:(tt + 1) * TT])), r=[("hTd", tt)], w=[kh])
        rot = [7, 6, 5]
        ri = 0
        for j in range(8):
            isq = j < 4
            c = j % 4
            bq = rot[ri % 3]
            bs = rot[(ri + 1) % 3]
            ri += 2
            col0 = (0 if isq else 512) + c * 128
            for k in range(8):
                S.op("pe", (lambda k=k, bq=bq, col0=col0: pe.matmul(bank(bq), w2[:, k, col0:col0 + 128], hb[:, k, :], start=(k == 0), stop=(k == 7))),
                     r=["w2", kh], w=[("ps", bq)])
            sb_ = sqf[j % 2]
            rb_ = rq[j % 2]
            S.op("act", (lambda bq=bq, sb_=sb_: act.activation(sb_[:], bank(bq), AF.Square)), r=[("ps", bq)], w=[("sqf", j % 2)])
            S.op("pe", (lambda bs=bs, sb_=sb_: pe.matmul(bank(bs), BD, sb_[:], start=True, stop=True)), r=[("sqf", j % 2), "cst"], w=[("ps", bs)])
            S.op("act", (lambda bs=bs, rb_=rb_: act.activation(rb_[:], bank(bs), AF.Ln, bias=epsc[:], scale=1.0 / 64)), r=[("ps", bs), "epsc"], w=[("rq", j % 2)])
            S.op("act", (lambda rb_=rb_: act.activation(rb_[:], rb_[:], AF.Exp, scale=-0.5)), r=[("rq", j % 2)], w=[("rq", j % 2)])
            if isq:
                S.op("dve", (lambda c=c, bq=bq, rb_=rb_: dve.scalar_tensor_tensor(qT[:, c, :], bank(bq), qkw[:, 0:1], rb_[:], ALU.mult, ALU.mult)),
                     r=[("ps", bq), ("rq", j % 2), "qkw"], w=[("qT", c)])
            else:
                S.op("dve", (lambda c=c, bq=bq, rb_=rb_, tt=tt: dve.scalar_tensor_tensor(kT[:, c, tt * TT:(tt + 1) * TT], bank(bq), qkw[:, 1:2], rb_[:], ALU.mult, ALU.mult)),
                     r=[("ps", bq), ("rq", j % 2), "qkw"], w=[("kT", c, tt)])
        for s in range(4):
            bq = rot[ri % 3]
            ri += 1
            for k in range(8):
                S.op("pe", (lambda k=k, bq=bq, s=s: pe.matmul(bank(bq), hb[:, k, s * 128:(s + 1) * 128], w2[:, k, 1024:1536], start=(k == 0), stop=(k == 7))),
                     r=["w2", kh], w=[("ps", bq)])
            S.op("dve", (lambda bq=bq, s=s, tt=tt: dve.tensor_copy(vb[:, tt * 4 + s, :], bank(bq))), r=[("ps", bq)], w=[("vb", tt * 4 + s)])
        if debug and tt == 1:
            S.dma(lambda: sp.dma_start(out=d_q.ap(), in_=qT[:]), r=[("qT", c) for c in range(4)], w=["d_q"])

        if STOP == "qkv":
            return finish()
        steps = [("d", d) for d in range(4)] + [("o", j) for j in range(4 * tt - 1, -1, -1)]
        ns = len(steps)
        for c in range(4):
            def lo_of(n):
                kind, v = steps[n]
                return v * 128 if kind == "d" else 0

            def zmm(n, hh, c=c, tt=tt):
                kind, v = steps[n]
                p0, p1 = hh * 64, (hh + 1) * 64
                zb = ZBP[hh][n % 2]
                if kind == "d":
                    for a in range(v, 4):
                        j = 4 * tt + a - v
                        S.op("pe", (lambda a=a, j=j, v=v: pe.matmul(bank(zb, a * 128, (a + 1) * 128), kT[p0:p1, c, j * 128:(j + 1) * 128],
                                                                qT[p0:p1, c, a * 128:(a + 1) * 128], start=(a == v), stop=False, skip_group_check=True)),
                             r=[("kT", c, j // 4), ("qT", c)], w=[("ps", zb)])
                else:
                    j = v
                    S.op("pe", (lambda j=j: pe.matmul(bank(zb), kT[p0:p1, c, j * 128:(j + 1) * 128], qT[p0:p1, c, :], start=True, stop=False, skip_group_check=True)),
                         r=[("kT", c, j // 4), ("qT", c)], w=[("ps", zb)])

            def act_e(n, hh):
                lo = lo_of(n)
                zb = ZBP[hh][n % 2]
                S.op("act", (lambda: act.activation(ebuf[hh][:, lo:512], bank(zb, lo, 512), AF.Exp)), r=[("ps", zb)], w=[("e", hh)])

            def act_sp(n, hh):
                lo = lo_of(n)
                pq = n % 2
                spt = spb2[hh][pq]
                S.op("act", (lambda: act.activation(spt[:, lo:512], ebuf[hh][:, lo:512], AF.Ln, bias=1.0)), r=[("e", hh)], w=[("sp", hh, pq)])
                if n == 0:
                    S.op("dve", (lambda: dve.tensor_tensor(spm[hh][:].rearrange("p (a l) -> p a l", a=4), spt[:].rearrange("p (a l) -> p a l", a=4), US4, ALU.mult)),
                         r=[("sp", hh, pq)] + CB, w=[("spm", hh)])

            def pe_tio(n, hh):
                lo = lo_of(n)
                pq = n % 2
                zb = ZBP[hh][pq]
                first = (n == 0)
                src = spm[hh] if first else spb2[hh][pq]
                skey = ("spm", hh) if first else ("sp", hh, pq)
                S.op("pe", (lambda: pe.matmul(bank(zb, lo, 512), negLI_b, src[:, lo:512], start=False, stop=first, skip_group_check=True)),
                     r=[skey] + CB, w=[("ps", zb)])
                if not first:
                    S.op("pe", (lambda: pe.matmul(bank(zb, lo, 512), negI_b, cbb[hh][:, lo:512], start=False, stop=True, skip_group_check=True)),
                         r=[("cb", hh)] + CB, w=[("ps", zb)])
                if n < ns - 1:
                    S.op("pe", (lambda: pe.matmul(bank(CBK[hh], lo, 512), ones_b, src[:, lo:512], start=first, stop=False, skip_group_check=True)),
                         r=[skey] + CB, w=[("ps", CBK[hh])])
                    S.op("dve", (lambda: dve.tensor_copy(cbb[hh][:], bank(CBK[hh]))), r=[("ps", CBK[hh])], w=[("cb", hh)])

            def act_w(n, hh):
                lo = lo_of(n)
                pq = n % 2
                zb = ZBP[hh][pq]
                wt = wb2[hh][pq]
                S.op("act", (lambda: act.activation(wt[:, lo:512], bank(zb, lo, 512), AF.Exp)), r=[("ps", zb)], w=[("w", hh, pq)])
                if n == 0:
                    S.op("dve", (lambda: dve.tensor_tensor(wm[hh][:].rearrange("p (a l) -> p a l", a=4), wt[:].rearrange("p (a l) -> p a l", a=4), US4, ALU.mult)),
                         r=[("w", hh, pq)] + CB, w=[("wm", hh)])

            def pe_pv(n, hh):
                kind, v = steps[n]
                pq = n % 2
                first = (n == 0)
                last = (n == ns - 1)
                p0, p1 = hh * 64, (hh + 1) * 64
                vc0 = c * 128 + hh * 64
                wsrc = wm[hh] if first else wb2[hh][pq]
                wkey = ("wm", hh) if first else ("w", hh, pq)
                if kind == "d":
                    for a in range(v, 4):
                        j = 4 * tt + a - v
                        S.op("pe", (lambda a=a, j=j: pe.matmul(ps[p0:p1, OB * 512 + a * 128: OB * 512 + (a + 1) * 128], vb[:, j, vc0:vc0 + 64],
                                                                wsrc[:, a * 128:(a + 1) * 128], start=(first and a == 0), stop=False, skip_group_check=True)),
                             r=[("vb", j), wkey], w=[("ps", OB)])
                else:
                    j = v
                    S.op("pe", (lambda j=j: pe.matmul(ps[p0:p1, OB * 512: OB * 512 + 512], vb[:, j, vc0:vc0 + 64], wsrc[:, :], start=False, stop=last,
                                                      skip_group_check=True)),
                         r=[("vb", j), wkey], w=[("ps", OB)])

            for n0 in range(min(2, ns)):
                for hh in range(2):
                    zmm(n0, hh)
            for hh in range(2):
                act_e(0, hh)
            for hh in range(2):
                act_sp(0, hh)
                pe_tio(0, hh)
            for n in range(ns):
                for hh in range(2):
                    if n + 1 < ns:
                        act_e(n + 1, hh)
                    act_w(n, hh)
                    if n + 1 < ns:
                        act_sp(n + 1, hh)
                        pe_tio(n + 1, hh)
                    if n + 2 < ns:
                        zmm(n + 2, hh)
                for hh in range(2):
                    pe_pv(n, hh)
                if STOP == "step0":
                    return finish()
            S.op("dve", (lambda c=c: dve.tensor_copy(ysb[:, c, :], bank(OB))), r=[("ps", OB)], w=[("ysb", c)])
            if STOP == "chunk0":
                return finish()
        S.dma((lambda tt=tt: sp.dma_start(out=yl_sb_v[tt // 4][:, :, (tt % 4) * TT:(tt % 4 + 1) * TT], in_=ysb[:])), r=[("ysb", c) for c in range(4)], w=[("yl_sb", tt)], sk=("dma", "ysb"))
        if tt == 3:
            gather(yl_sb, ya_sb, 0, "yl_sb", 2)

    if debug and not only3:
        d_ysb = dbg_out("d_ysb", [512, SEQ], BF16)
        for hf_ in range(2):
            nt_ = min(ntiles, hf_ * 4 + 4) - hf_ * 4
            if nt_ > 0:
                S.dma((lambda hf_=hf_, nt_=nt_: sp.dma_start(out=d_ysb.ap()[:, hf_ * HS: hf_ * HS + nt_ * TT], in_=yl_sb[hf_].ap()[:, 0:nt_ * TT])),
                      r=[("yl_sb", t) for t in range(ntiles)], w=[("d_ysb", hf_)])
    S.fence()
    A.release(mP)
    if upto < 3:
        S.emit()
        return nc, S, dbg
    if not only3:
        gather(yl_sb, ya_sb, 1, "yl_sb", 3)

    ya_ssd_v = [t_.ap().rearrange("(c p) t -> p c t", p=128) for t_ in ya_ssd]
    ya_sb_v = [t_.ap().rearrange("(c p) t -> p c t", p=128) for t_ in ya_sb]
    woutb_v = woutb_d.ap().rearrange("(k p) c -> p k c", p=128)
    wgb_v = wgb_d.ap().rearrange("(k p) c -> p k c", p=128)
    wub_v = wub_d.ap().rearrange("(k p) c -> p k c", p=128)
    wdb_v = wdb_d.ap().rearrange("(k p) c -> p k c", p=128)
    WOK = prep_keys("woutb", 2 * D, D)
    WGK = prep_keys("wgb", D, DFF)
    WUK = prep_keys("wub", D, DFF)
    WDK = prep_keys("wdb", DFF, D)
    yh = [A([128, 16, TT], BF16, "yh%d" % i) for i in range(2)]
    xt = A([128, 4, D], F32, "xt")
    x1 = A([128, 4, D], F32, "x1")
    xn = [A([128, D], F32, "xn%d" % i) for i in range(2)]
    h2T = A([128, 8, TT], BF16, "h2T")
    actT = A([128, NCH, TT], BF16, "actT")
    sg = [A([128, TT], F32, "sg%d" % i) for i in range(2)]
    tmpo = [A([128, 512], F32, "tmpo%d" % i) for i in range(2)]
    ob = [A([128, 512], F32, "ob%d" % i) for i in range(4)]
    ssq3 = A([128, 4], F32, "ssq3")
    sqj3 = A([128, D], BF16, "sqj3")
    NRING = 8
    ring = [A([128, 4096], BF16, "ring%d" % i) for i in range(NRING)]
    ring_i = [0]

    def ring_load(fn_src, rkeys):
        i = ring_i[0] % NRING
        ring_i[0] += 1
        t = ring[i]
        o, i_ = fn_src(t)
        S.dma((lambda o=o, i_=i_: sp.dma_start(out=o, in_=i_)), r=rkeys, w=[("ring", i)])
        return t, ("ring", i)

    obi = 0
    def load_inputs(ut):
        for hlf in range(2):
            t0 = ut * TT
            S.dma((lambda hlf=hlf, t0=t0: sp.dma_start(out=yh[hlf][:, 0:8, :], in_=ya_ssd_v[hlf][:, :, t0:t0 + TT])), r=[("ya", "yl_ssd", hlf)], w=[("yh", hlf, 0)])
            S.dma((lambda hlf=hlf, t0=t0: sp.dma_start(out=yh[hlf][:, 8:16, :], in_=ya_sb_v[hlf][:, :, t0:t0 + TT])), r=[("ya", "yl_sb", hlf)], w=[("yh", hlf, 1)])
        for q4 in range(4):
            pt = q4 // 2
            v0 = yh[0][:, q4 * 4:(q4 + 1) * 4, :].rearrange("p c t -> p (c t)")
            v1 = yh[1][:, q4 * 4:(q4 + 1) * 4, :].rearrange("p c t -> p (c t)")
            S.op("dve", (lambda v0=v0: dve.tensor_scalar_mul(v0, v0, flags[:, 0:1])), r=[("yh", 0, pt), "flags"], w=[("yh", 0, pt)])
            S.op("dve", (lambda v0=v0, v1=v1: dve.scalar_tensor_tensor(v0, v1, flags[:, 1:2], v0, ALU.mult, ALU.add)),
                 r=[("yh", 0, pt), ("yh", 1, pt), "flags"], w=[("yh", 0, pt)])
        S.dma((lambda ut=ut: sp.dma_start(out=xt[:], in_=xh_d.ap()[ut * TT:(ut + 1) * TT, :].rearrange("(s p) d -> p s d", p=128))), w=["xt"])

    def load_oproj():
        return [ring_load(lambda t, g4=g4: (t[:].rearrange("p (k c) -> p k c", k=4), woutb_v[:, g4 * 4:(g4 + 1) * 4, :]), WOK) for g4 in range(4)]

    load_inputs(0)
    opw = load_oproj()
    for ut in range(4):
        for g4 in range(4):
            wt, wkey = opw[g4]
            wv = wt[:].rearrange("p (k c) -> p k c", k=4)
            for kk in range(4):
                kc = g4 * 4 + kk
                for s in range(4):
                    for hf in range(2):
                        S.op("pe", (lambda kc=kc, kk=kk, s=s, hf=hf, wv=wv: pe.matmul(bank(s * 2 + hf), yh[0][:, kc, s * 128:(s + 1) * 128], wv[:, kk, hf * 512:(hf + 1) * 512],
                                                                                      start=(kc == 0), stop=(kc == 15))),
                             r=[("yh", 0, 0), ("yh", 0, 1), wkey], w=[("ps", s * 2 + hf)])
        for s in range(4):
            for hf in range(2):
                tb = tmpo[(s * 2 + hf) % 2]
                tk = ("tmpo", (s * 2 + hf) % 2)
                S.op("dve", (lambda s=s, hf=hf, tb=tb: dve.tensor_tensor(tb[:], bank(s * 2 + hf), g1bc[:, hf * 512:(hf + 1) * 512], ALU.mult)),
                     r=[("ps", s * 2 + hf), "g1bc"], w=[tk])
                S.op("dve", (lambda s=s, hf=hf, tb=tb: dve.tensor_tensor(x1[:, s, hf * 512:(hf + 1) * 512], tb[:], xt[:, s, hf * 512:(hf + 1) * 512], ALU.add)),
                     r=[tk, "xt"], w=[("x1", s, hf)])
        for s in range(4):
            sc = ssq3[:, s:s + 1]
            S.op("act", (lambda s=s, sc=sc: act.activation(sqj3[:], x1[:, s, :], AF.Square, accum_out=sc)), r=[("x1", s, 0), ("x1", s, 1)], w=["sqj3", ("ssq3", s)])
            rstd_from_ssq(sc, D, ("ssq3", s), ("ssq3", s))
            xb = xn[s % 2]
            kx = ("xn", s % 2)
            S.op("dve", (lambda s=s, sc=sc, xb=xb: dve.tensor_scalar_mul(xb[:], x1[:, s, :], sc)), r=[("x1", s, 0), ("x1", s, 1), ("ssq3", s)], w=[kx])
            for c in range(8):
                S.op("pe", (lambda c=c, s=s, xb=xb: pe.transpose(bank(c, s * 128, (s + 1) * 128), xb[:, c * 128:(c + 1) * 128], ident)),
                     r=[kx, "cst"], w=[("ps", c)])
        for c in range(8):
            S.op("dve", (lambda c=c: dve.tensor_scalar(h2T[:, c, :], bank(c), s2[:, c:c + 1], t2[:, c:c + 1], ALU.mult, ALU.add)),
                 r=[("ps", c), "s2", "modpp"], w=[("h2T", c)])
        H2K = [("h2T", c) for c in range(8)]
        if ut + 1 < 4:
            load_inputs(ut + 1)
        fi = 0
        for fg in range(6):
            nf = 4 if fg < 5 else 2
            ncol = nf * 128
            gt, gkey = ring_load(lambda t, fg=fg, ncol=ncol: (t[:].rearrange("p (k c) -> p k c", k=8)[:, :, 0:ncol], wgb_v[:, :, fg * 512: fg * 512 + ncol]), WGK)
            utile, ukey = ring_load(lambda t, fg=fg, ncol=ncol: (t[:].rearrange("p (k c) -> p k c", k=8)[:, :, 0:ncol], wub_v[:, :, fg * 512: fg * 512 + ncol]), WUK)
            gv = gt[:].rearrange("p (k c) -> p k c", k=8)
            uv = utile[:].rearrange("p (k c) -> p k c", k=8)
            for f in range(nf):
                bg_ = (fi % 4) * 2
                bu_ = bg_ + 1
                for k in range(8):
                    S.op("pe", (lambda k=k, f=f, bg_=bg_, gv=gv: pe.matmul(bank(bg_), gv[:, k, f * 128:(f + 1) * 128], h2T[:, k, :], start=(k == 0), stop=(k == 7))),
                         r=H2K + [gkey], w=[("ps", bg_)])
                for k in range(8):
                    S.op("pe", (lambda k=k, f=f, bu_=bu_, uv=uv: pe.matmul(bank(bu_), uv[:, k, f * 128:(f + 1) * 128], h2T[:, k, :], start=(k == 0), stop=(k == 7))),
                         r=H2K + [ukey], w=[("ps", bu_)])
                sgb = sg[fi % 2]
                S.op("act", (lambda bg_=bg_, sgb=sgb: act.activation(sgb[:], bank(bg_), AF.Silu)), r=[("ps", bg_)], w=[("sg", fi % 2)])
                fch = fg * 4 + f
                S.op("dve", (lambda bu_=bu_, sgb=sgb, fch=fch: dve.tensor_tensor(actT[:, fch, :], bank(bu_), sgb[:], ALU.mult)),
                     r=[("ps", bu_), ("sg", fi % 2)], w=[("actT", fch)])
                fi += 1
        for g in range(6):
            nk = 4 if g < 5 else 2
            wt, wkey = ring_load(lambda t, g=g, nk=nk: (t[:].rearrange("p (k c) -> p k c", k=4)[:, 0:nk, :], wdb_v[:, g * 4: g * 4 + nk, :]), WDK)
            wv = wt[:].rearrange("p (k c) -> p k c", k=4)
            for kk in range(nk):
                kc = g * 4 + kk
                for s in range(4):
                    for hf in range(2):
                        S.op("pe", (lambda kc=kc, kk=kk, s=s, hf=hf, wv=wv: pe.matmul(bank(s * 2 + hf), actT[:, kc, s * 128:(s + 1) * 128], wv[:, kk, hf * 512:(hf + 1) * 512],
                                                                                      start=(kc == 0), stop=(kc == NCH - 1))),
                             r=[("actT", kc), wkey], w=[("ps", s * 2 + hf)])
        if ut + 1 < 4:
            opw = load_oproj()
        for s in range(4):
            for hf in range(2):
                tb = tmpo[(s * 2 + hf) % 2]
                tk = ("tmpo", (s * 2 + hf) % 2)
                o = ob[obi % 4]
                ok = ("ob", obi % 4)
                obi += 1
                S.op("dve", (lambda s=s, hf=hf, tb=tb: dve.tensor_tensor(tb[:], bank(s * 2 + hf), g2bc[:, hf * 512:(hf + 1) * 512], ALU.mult)),
                     r=[("ps", s * 2 + hf), "g2bc"], w=[tk])
                S.op("dve", (lambda s=s, hf=hf, tb=tb, o=o: dve.tensor_tensor(o[:], tb[:], x1[:, s, hf * 512:(hf + 1) * 512], ALU.add)),
                     r=[tk, ("x1", s, hf)], w=[ok])
                r0 = ut * TT + s * 128
                S.dma((lambda o=o, r0=r0, hf=hf: sp.dma_start(out=out_d.ap()[r0:r0 + 128, hf * 512:(hf + 1) * 512], in_=o[:])), r=[ok], w=[("out", ut, s, hf)], sk=("dma",) + ok)
    S.fence()
    S.emit()
    return nc, S, dbg


def _consts():
    i = np.arange(128)[:, None]
    j = np.arange(128)[None, :]
    c = np.zeros((128, 7, 128), np.float32)
    c[:, 0] = (i == j)
    c[:, 1] = (i <= j)
    c[:, 2] = (i < j)
    c[:, 3] = (i >= j)
    c[:, 4] = (i > j)
    c[:, 5] = 1.0
    c[:, 6] = ((i // 64) == (j // 64))
    return c


def _pp(v):
    v = np.asarray(v, np.float32)
    return np.ascontiguousarray(v.reshape(-1, 128).T)


def make_in_maps(x, c, w_ada, b_ada, norm1_w, w_in, conv_w, conv_b, dt_bias, a_log, d_skip,
                 ssd_norm_w, q_norm_w, k_norm_w, w_out, norm2_w, w_gate, w_up, w_down):
    f = lambda a: np.ascontiguousarray(np.asarray(a, np.float32))
    x, c = f(x), f(c)
    w_ada, b_ada = f(w_ada)[0], f(b_ada)[0]
    w_in, conv_w, conv_b = f(w_in)[0], f(conv_w)[0], f(conv_b)[0]
    dt_bias, a_log, d_skip = f(dt_bias)[0], f(a_log)[0], f(d_skip)[0]
    ssd_norm_w, q_norm_w, k_norm_w = f(ssd_norm_w)[0], f(q_norm_w)[0], f(k_norm_w)[0]
    w_out, w_gate, w_up, w_down = f(w_out)[0], f(w_gate)[0], f(w_up)[0], f(w_down)[0]
    n1, n2 = f(norm1_w)[0], f(norm2_w)[0]
    consts = _consts()
    b_pp = np.concatenate([_pp(b_ada[0:1024]), _pp(b_ada[1024:2048]), _pp(b_ada[3072:4096]), _pp(b_ada[4096:5120])], axis=1)
    b_g = np.ascontiguousarray(np.stack([b_ada[2048:3072], b_ada[5120:6144]]))
    maps = []
    for core in range(8):
        b, g = core // 2, core % 2
        cols = np.concatenate([
            np.arange(g * 512, (g + 1) * 512),
            1024 + np.arange(g * 512, (g + 1) * 512),
            2048 + np.arange(g * 128, (g + 1) * 128),
            2304 + np.arange(g * 128, (g + 1) * 128),
            2576 + np.arange(g * 512, (g + 1) * 512),
            3600 + np.arange(g * 512, (g + 1) * 512),
            4624 + np.arange(g * 512, (g + 1) * 512),
            2560 + np.arange(g * 8, (g + 1) * 8),
        ])
        cch = np.concatenate([np.arange(g * 512, (g + 1) * 512), 1024 + np.arange(g * 128, (g + 1) * 128),
                              1280 + np.arange(g * 128, (g + 1) * 128)])
        cw = np.ascontiguousarray(conv_w[:, cch].T.reshape(6, 128, 4).transpose(1, 0, 2))
        cb = _pp(conv_b[cch])
        vec8 = np.zeros((3, 8), np.float32)
        vec8[0] = dt_bias[g * 8:(g + 1) * 8]
        vec8[1] = a_log[g * 8:(g + 1) * 8]
        dsk = np.ascontiguousarray(np.repeat(d_skip[g * 8:(g + 1) * 8], 64).reshape(4, 128).T)
        snw = _pp(ssd_norm_w[g * 512:(g + 1) * 512])
        qkw = np.ascontiguousarray(np.stack([np.tile(q_norm_w, 2), np.tile(k_norm_w, 2)], axis=1))
        flags = np.zeros((128, 2), np.float32)
        flags[:, g] = 1.0
        maps.append({
            "x": x[b], "xh": np.ascontiguousarray(x[b, g * 2048:(g + 1) * 2048]), "cT": _pp(c[b]),
            "w_ada": w_ada, "b_pp": b_pp, "b_g": b_g, "n1w": _pp(n1), "n2w": _pp(n2),
            "w_in": np.ascontiguousarray(w_in[:, cols]), "conv_w": cw, "conv_b": cb, "vec8": vec8,
            "dsk": dsk, "snw": snw, "qkw": qkw, "w_out": w_out, "w_gate": w_gate, "w_up": w_up, "w_down": w_down,
            "consts": consts, "flags": flags,
        })
    return maps


_CACHE = {}


def kernel(**inputs):
    if "nc" not in _CACHE:
        _CACHE["nc"] = build(False)[0]
    nc = _CACHE["nc"]
    maps = make_in_maps(**inputs)
    res = run_bass_kernel_spmd(nc, maps, core_ids=list(range(8)))
    out = np.empty((NB, SEQ, D), np.float32)
    for core in range(8):
        b, g = core // 2, core % 2
        out[b, g * 2048:(g + 1) * 2048] = res.results[core]["out"]
    return out
```

```python
import types
import numpy as np
import concourse.bass as bass
import concourse.mybir as mybir
from concourse.bass_utils import run_bass_kernel_spmd

F32 = mybir.dt.float32
BF16 = mybir.dt.bfloat16
AF = mybir.ActivationFunctionType
ALU = mybir.AluOpType

D = 1024
SEQ = 4096
NB = 4
DFF = 2816
NCH = 22
EPS = 1e-6
WIN = 2824
C_Z, C_X, C_B, C_C, C_Q, C_K, C_V, C_DT = 0, 512, 1024, 1152, 1280, 1792, 2304, 2816
TT = 512
NT = SEQ // TT
SB_BASE = 20480
SB_END = 229376
NO_ALIAS = False
STOP = None


def _freeze(fn):
    if fn is None or fn.__closure__ is None:
        return fn
    cells = []
    for c in fn.__closure__:
        try:
            cells.append(types.CellType(c.cell_contents))
        except ValueError:
            cells.append(c)
    return types.FunctionType(fn.__code__, fn.__globals__, fn.__name__, fn.__defaults__, tuple(cells))


class Sched:
    ENG = ("pe", "act", "dve", "pool", "sp")

    def __init__(self, nc):
        self.nc = nc
        self.e = {"pe": nc.tensor, "act": nc.scalar, "dve": nc.vector, "pool": nc.gpsimd, "sp": nc.sync}
        self.ops = []
        self.last_w = {}
        self.readers = {}
        self.fence_idx = None
        self.last_on = {}
        self.last_dma = {}

    def _add(self, eng, fn, r, w, kind, sk=None):
        idx = len(self.ops)
        deps = {}
        for k in r:
            p = self.last_w.get(k)
            if p is not None:
                deps[p] = True
        for k in w:
            p = self.last_w.get(k)
            if p is not None:
                deps.setdefault(p, False)
            for p in self.readers.get(k, ()):
                if p != idx:
                    deps.setdefault(p, False)
        if self.fence_idx is not None:
            deps[self.fence_idx] = True
        op = dict(eng=eng, fn=_freeze(fn), kind=kind, deps=deps, sk=sk, sig=False)
        for p, raw in deps.items():
            po = self.ops[p]
            need = po["kind"] != "c" or kind != "c" or po["eng"] != eng or raw or eng != "pe"
            if need:
                po["sig"] = True
        for k in r:
            self.readers.setdefault(k, []).append(idx)
        for k in w:
            self.last_w[k] = idx
            self.readers[k] = []
        self.ops.append(op)
        if kind == "c":
            self.last_on[eng] = idx
        else:
            self.last_dma[sk] = idx
        return idx

    def op(self, eng, fn, r=(), w=()):
        r = tuple(r)
        w = tuple(w) + tuple(k for k in r if isinstance(k, tuple) and k and k[0] in ("ps", "ps0") and k not in w)
        return self._add(eng, fn, r, w, "c")

    def dma(self, fn, r=(), w=(), q="sp", sk=None):
        w = tuple(w)
        if sk is None:
            sk = ("dma",) + tuple(w[:1])
        return self._add(q, fn, tuple(r), w, "d", sk)

    def cc(self, fn, r=(), w=(), sk=None):
        return self._add("pool", fn, tuple(r), tuple(w), "cc", sk)

    def fence(self):
        deps = {}
        for e, i in self.last_on.items():
            deps[i] = True
        for sk, i in self.last_dma.items():
            deps[i] = True
        idx = len(self.ops)
        op = dict(eng="sp", fn=None, kind="f", deps=deps, sk=None, sig=True)
        for p in deps:
            self.ops[p]["sig"] = True
        self.ops.append(op)
        self.fence_idx = idx
        self.last_on = {"sp": idx}
        self.last_dma = {}

    def emit(self):
        nc = self.nc
        sems = {}

        def sem_for(name):
            if name not in sems:
                sems[name] = nc.alloc_semaphore("s%d" % len(sems))
            return sems[name]

        cnt = {}
        for op in self.ops:
            if op["kind"] in ("c", "f"):
                key = ("eng", op["eng"])
                inc = 1
            elif op["kind"] == "d":
                key = op["sk"]
                inc = 16
                op["sig"] = True
            else:
                key = op["sk"]
                inc = 1
                op["sig"] = True
            if op["sig"]:
                cnt[key] = cnt.get(key, 0) + inc
                op["sem"] = key
                op["cnt"] = cnt[key]
                op["inc"] = inc
        seen = {e: {} for e in self.ENG}
        nwait = 0
        for op in self.ops:
            eng = op["eng"]
            E = self.e[eng]
            sn = seen[eng]
            need = {}
            for p, raw in op["deps"].items():
                po = self.ops[p]
                if po["kind"] == "c" and op["kind"] == "c" and po["eng"] == eng and not raw and eng == "pe":
                    continue
                k, c = po["sem"], po["cnt"]
                if sn.get(k, 0) >= c:
                    continue
                if need.get(k, 0) < c:
                    need[k] = c
            for k, c in need.items():
                E.wait_ge(sem_for(k), c)
                nwait += 1
                sn[k] = c
            own = ("eng", eng)
            for p, raw in op["deps"].items():
                po = self.ops[p]
                snap = po.get("snap")
                if snap:
                    skipped = po["kind"] == "c" and op["kind"] == "c" and po["eng"] == eng and not raw and eng == "pe"
                    for k, c in snap.items():
                        if skipped and k == own:
                            continue
                        if sn.get(k, 0) < c:
                            sn[k] = c
            if op["kind"] == "f":
                E.sem_inc(sem_for(op["sem"]), 1)
            else:
                ins = op["fn"]()
                if op["sig"]:
                    ins.then_inc(sem_for(op["sem"]), op["inc"])
            if op["sig"]:
                op["snap"] = dict(sn)
                op["snap"][op["sem"]] = op["cnt"]
            op["fn"] = None
        self.stats = (len(self.ops), nwait, len(sems))


class Alloc:
    def __init__(self, nc):
        self.nc = nc
        self.off = SB_BASE
        self.n = 0

    def __call__(self, shape, dt, name=None):
        nbytes = int(np.prod(shape[1:])) * (4 if dt == F32 else 2)
        nbytes = (nbytes + 63) // 64 * 64
        assert NO_ALIAS or self.off + nbytes <= SB_END, ("SBUF overflow", name, self.off, nbytes)
        self.n += 1
        t = self.nc.alloc_sbuf_tensor_at("t%d_%s" % (self.n, name or "x"), list(shape), dt, offset=self.off)
        self.off += nbytes
        return t

    def mark(self):
        return self.off

    def release(self, m):
        if not NO_ALIAS:
            self.off = m


def build(debug=False, upto=3, ntiles=NT, fake_cc=False, skip1a=False, only3=False):
    nc = bass.Bass("TRN2", target_bir_lowering=False)
    S = Sched(nc)
    A = Alloc(nc)
    pe, act, dve, pool, sp = nc.tensor, nc.scalar, nc.vector, nc.gpsimd, nc.sync

    def din(name, shape, dt=F32):
        return nc.dram_tensor(name, list(shape), dt, kind="ExternalInput")

    x_d = din("x", [SEQ, D])
    xh_d = din("xh", [SEQ // 2, D])
    cT_d = din("cT", [128, 8])
    wada_d = din("w_ada", [D, 6 * D])
    bpp_d = din("b_pp", [128, 32])
    bg_d = din("b_g", [2, D])
    n1w_d = din("n1w", [128, 8])
    n2w_d = din("n2w", [128, 8])
    win_d = din("w_in", [D, WIN])
    cw_d = din("conv_w", [128, 6, 4])
    cb_d = din("conv_b", [128, 6])
    vec8_d = din("vec8", [3, 8])
    dsk_d = din("dsk", [128, 4])
    snw_d = din("snw", [128, 4])
    qkw_d = din("qkw", [128, 2])
    wout_d = din("w_out", [2 * D, D])
    wg_d = din("w_gate", [D, DFF])
    wu_d = din("w_up", [D, DFF])
    wd_d = din("w_down", [DFF, D])
    consts_d = din("consts", [128, 7, 128])
    flags_d = din("flags", [128, 2])
    out_d = nc.dram_tensor("out", [SEQ // 2, D], F32, kind="ExternalOutput")

    winb_d = nc.dram_tensor("winb", [D, WIN], BF16, kind="ExternalOutput")
    woutb_d = nc.dram_tensor("woutb", [2 * D, D], BF16, kind="ExternalOutput")
    wgb_d = nc.dram_tensor("wgb", [D, DFF], BF16, kind="ExternalOutput")
    wub_d = nc.dram_tensor("wub", [D, DFF], BF16, kind="ExternalOutput")
    wdb_d = nc.dram_tensor("wdb", [DFF, D], BF16, kind="ExternalOutput")
    hT_d2 = nc.dram_tensor("hTs", [128, 8 * SEQ], BF16, kind="ExternalInput") if skip1a else nc.dram_tensor("hTs", [128, 8 * SEQ], BF16, kind="ExternalOutput")

    class _HT:
        def ap(self):
            return hT_d2.ap().rearrange("p (k t) -> p k t", k=8)
    hT_d = _HT()
    HS = SEQ // 2
    yl_ssd = [nc.dram_tensor("yl_ssd%d" % i, [512, HS], BF16) for i in range(2)]
    yl_sb = [nc.dram_tensor("yl_sb%d" % i, [512, HS], BF16) for i in range(2)]
    if only3:
        skip1a = True
        ya_ssd = [nc.dram_tensor("ya_ssd%d" % i, [1024, HS], BF16, kind="ExternalInput") for i in range(2)]
        ya_sb = [nc.dram_tensor("ya_sb%d" % i, [1024, HS], BF16, kind="ExternalInput") for i in range(2)]
    else:
        ya_ssd = [nc.dram_tensor("ya_ssd%d" % i, [1024, HS], BF16) for i in range(2)]
        ya_sb = [nc.dram_tensor("ya_sb%d" % i, [1024, HS], BF16) for i in range(2)]

    def gather(src, dst, hf, name, ci):
        keys = [(name, t) for t in range(hf * 4, min(ntiles, hf * 4 + 4))]
        if not keys:
            return
        if fake_cc:
            ncol = (min(ntiles, hf * 4 + 4) - hf * 4) * TT
            for hh_ in range(2):
                S.dma((lambda hh_=hh_: sp.dma_start(out=dst[hf].ap()[hh_ * 512:(hh_ + 1) * 512, 0:ncol], in_=src[hf].ap()[:, 0:ncol])),
                      r=keys, w=[("ya", name, hf)], sk=("dma", "fcc", name, hf, hh_))
        else:
            S.cc(lambda: pool.collective_compute("AllGather", ALU.bypass, replica_groups=[[0, 1], [2, 3], [4, 5], [6, 7]],
                                                 ins=[src[hf].ap().opt()], outs=[dst[hf].ap().opt()]),
                 r=keys, w=[("ya", name, hf)], sk=("cc", ci))

    dbg = {}

    def dbg_out(name, shape, dt=F32):
        if debug:
            dbg[name] = nc.dram_tensor(name, list(shape), dt, kind="ExternalOutput")
            return dbg[name]
        return None

    ps = nc.alloc_psum_tensor("ps", [128, 8 * 512], F32)

    def bank(b, c0=0, c1=512):
        return ps[:, b * 512 + c0: b * 512 + c1]

    cst = A([128, 7, 128], F32, "cst")
    ident = cst[:, 0, :]
    UI = cst[:, 1, :]
    US = cst[:, 2, :]
    LI = cst[:, 3, :]
    LS = cst[:, 4, :]
    ONESF = cst[:, 5, :]
    BD = cst[:, 6, :]
    cbf = A([128, 4, 128], BF16, "cbf")
    negLI_b, negI_b, ones_b, US_b = cbf[:, 0, :], cbf[:, 1, :], cbf[:, 2, :], cbf[:, 3, :]
    flags = A([128, 2], F32, "flags")
    sv = A([128, 96], F32, "sv")
    cT = sv[:, 0:8]
    condT = sv[:, 8:16]
    n1w = sv[:, 16:24]
    n2w = sv[:, 24:32]
    modpp = sv[:, 32:64]
    s1 = sv[:, 64:72]
    s2 = sv[:, 72:80]
    bpp = A([128, 32], F32, "bpp")
    g1bc = A([128, D], F32, "g1bc")
    g2bc = A([128, D], F32, "g2bc")
    cw = A([128, 6, 4], F32, "cw")
    cbv = A([128, 6], F32, "cbv")
    v8 = A([128, 3, 8], F32, "v8")
    Aneg = A([128, 8], F32, "Aneg")
    dsk = A([128, 4], F32, "dsk")
    snw = A([128, 4], F32, "snw")
    qkw = A([128, 2], F32, "qkw")
    epsc = A([128, 1], F32, "epsc")

    S.dma(lambda: sp.dma_start(out=cst[:], in_=consts_d.ap()), w=["cst"])
    for (t, d, k) in ((flags, flags_d, "flags"), (bpp, bpp_d, "bpp"), (cw, cw_d, "cw"), (cbv, cbv_d if False else cb_d, "cbv"),
                      (dsk, dsk_d, "dsk"), (snw, snw_d, "snw"), (qkw, qkw_d, "qkw")):
        S.dma((lambda t=t, d=d: sp.dma_start(out=t[:], in_=d.ap())), w=[k])
    S.dma(lambda: sp.dma_start(out=sv[:, 0:8], in_=cT_d.ap()), w=["cT"])
    S.dma(lambda: sp.dma_start(out=sv[:, 16:24], in_=n1w_d.ap()), w=["n1w"])
    S.dma(lambda: sp.dma_start(out=sv[:, 24:32], in_=n2w_d.ap()), w=["n2w"])
    for r in range(2):
        S.dma((lambda r=r: sp.dma_start(out=v8[:, r, :], in_=vec8_d.ap()[r:r + 1, :].partition_broadcast(128))),
              w=[("v8", r)])
    S.dma(lambda: sp.dma_start(out=g1bc[:], in_=bg_d.ap()[0:1, :].partition_broadcast(128)), w=["g1bc"])
    S.dma(lambda: sp.dma_start(out=g2bc[:], in_=bg_d.ap()[1:2, :].partition_broadcast(128)), w=["g2bc"])

    S.op("dve", lambda: dve.memset(epsc[:], EPS), w=["epsc"])
    S.op("dve", lambda: dve.tensor_scalar_mul(cbf[:, 0, :], LI, -1.0), r=["cst"], w=["cbf0"])
    S.op("dve", lambda: dve.tensor_scalar_mul(cbf[:, 1, :], ident, -1.0), r=["cst"], w=["cbf1"])
    S.op("dve", lambda: dve.tensor_copy(cbf[:, 2, :], ONESF), r=["cst"], w=["cbf2"])
    S.op("dve", lambda: dve.tensor_copy(cbf[:, 3, :], US), r=["cst"], w=["cbf3"])
    CB = ["cbf0", "cbf1", "cbf2", "cbf3"]
    S.op("dve", lambda: dve.tensor_scalar_mul(qkw[:, 0:1], qkw[:, 0:1], 0.125), r=["qkw"], w=["qkw"])
    S.op("act", lambda: act.activation(Aneg[:], v8[:, 1, :], AF.Exp), r=[("v8", 1)], w=["Aneg"])
    S.op("dve", lambda: dve.tensor_scalar_mul(Aneg[:], Aneg[:], -1.0), r=["Aneg"], w=["Aneg"])
    S.op("act", lambda: act.activation(condT, cT, AF.Silu), r=["cT"], w=["condT"])

    PW = 2048
    mP = A.mark()
    stf = [A([128, PW], F32, "stf%d" % i) for i in range(2)]
    stb = [A([128, PW], BF16, "stb%d" % i) for i in range(2)]
    prep_list = []

    def prep_matrix(src, dst, rows, cols, key):
        nr = rows // 128
        ncp = (cols + PW - 1) // PW
        cw_ = (cols + ncp - 1) // ncp
        for r in range(nr):
            for c in range(ncp):
                c0, c1 = c * cw_, min(cols, (c + 1) * cw_)
                prep_list.append((src, dst, r, c0, c1, (key, r)))

    prep_matrix(win_d, winb_d, D, WIN, "winb")
    prep_matrix(wout_d, woutb_d, 2 * D, D, "woutb")
    prep_matrix(wg_d, wgb_d, D, DFF, "wgb")
    prep_matrix(wu_d, wub_d, D, DFF, "wub")
    prep_matrix(wd_d, wdb_d, DFF, D, "wdb")
    prep_state = {"next_load": 0, "next_cast": 0}

    def prep_load(i):
        src, dst, r, c0, c1, key = prep_list[i]
        b = i % 2
        S.dma((lambda: pool.dma_start(out=stf[b][:, 0:c1 - c0], in_=src.ap()[r * 128:(r + 1) * 128, c0:c1])),
              w=[("stf", b)], q="pool")

    def prep_cast_store(i):
        src, dst, r, c0, c1, key = prep_list[i]
        b = i % 2
        S.op("pool", (lambda: pool.tensor_copy(stb[b][:, 0:c1 - c0], stf[b][:, 0:c1 - c0])),
             r=[("stf", b)], w=[("stb", b)])
        S.dma((lambda: pool.dma_start(out=dst.ap()[r * 128:(r + 1) * 128, c0:c1], in_=stb[b][:, 0:c1 - c0])),
              r=[("stb", b)], w=[key + (c0,)], q="pool", sk=("dma", "stbo", b))

    def prep_advance(n):
        for _ in range(n):
            i = prep_state["next_cast"]
            if i >= len(prep_list):
                return
            if prep_state["next_load"] == 0:
                prep_load(0)
                prep_state["next_load"] = 1
            if prep_state["next_load"] < len(prep_list) and prep_state["next_load"] == i + 1:
                prep_load(i + 1)
                prep_state["next_load"] = i + 2
            prep_cast_store(i)
            prep_state["next_cast"] = i + 1

    def prep_keys(key, rows, cols):
        nr = rows // 128
        ncp = (cols + PW - 1) // PW
        cw_ = (cols + ncp - 1) // ncp
        return [(key, r, c * cw_) for r in range(nr) for c in range(ncp)]

    def finish():
        prep_advance(1000)
        S.fence()
        S.emit()
        return nc, S, dbg

    prep_advance(16)

    m0 = A.mark()
    cbc = A([128, 8, 128], F32, "cbc")
    NWST = 6
    wst = [A([128, 2048], F32, "wst%d" % i) for i in range(NWST)]
    S.op("dve", lambda: dve.tensor_copy(cbc[:], condT.unsqueeze(2).to_broadcast([128, 8, 128])), r=["condT"], w=["cbc"])
    pc = 0
    for cg in range(3):
        for k in range(8):
            b = pc % NWST
            qn = "sp" if pc % 2 == 0 else "act"
            pc += 1
            S.dma((lambda b=b, k=k, cg=cg, qn=qn: S.e[qn].dma_start(out=wst[b][:], in_=wada_d.ap()[k * 128:(k + 1) * 128, cg * 2048:(cg + 1) * 2048])),
                  w=[("wst", b)], q=qn)
            st, sp_ = (k == 0), (k == 7)

            def ppmm(colbase, ncols, wofs, b=b, k=k, st=st, sp_=sp_):
                for cc in range(ncols):
                    S.op("pe", (lambda cc=cc: pe.matmul(bank(0, colbase + cc, colbase + cc + 1),
                                                        wst[b][:, wofs + cc * 128: wofs + (cc + 1) * 128],
                                                        condT[:, k:k + 1], start=(st and colbase == 0 and cc == 0), stop=sp_,
                                                        skip_group_check=True)),
                         r=[("wst", b), "condT"], w=[("ps0", colbase + cc)])

            def bcmm(bk, wofs, b=b, k=k, st=st, sp_=sp_):
                for h in range(2):
                    S.op("pe", (lambda h=h: pe.matmul(bank(bk + h), cbc[:, k, :],
                                                      wst[b][:, wofs + h * 512: wofs + (h + 1) * 512], start=st, stop=sp_)),
                         r=[("wst", b), "cbc"], w=[("ps", bk + h)])
            if cg == 0:
                ppmm(0, 16, 0)
            elif cg == 1:
                bcmm(1, 0)
                ppmm(16, 8, 1024)
            else:
                ppmm(24, 8, 0)
                bcmm(3, 1024)
    S.op("dve", lambda: dve.tensor_tensor(modpp, bank(0, 0, 32), bpp[:], ALU.add),
         r=[("ps0", c) for c in range(32)] + ["bpp"], w=["modpp"])
    for h in range(2):
        S.op("dve", (lambda h=h: dve.tensor_tensor(g1bc[:, h * 512:(h + 1) * 512], bank(1 + h), g1bc[:, h * 512:(h + 1) * 512], ALU.add)),
             r=[("ps", 1 + h), "g1bc"], w=["g1bc"])
        S.op("dve", (lambda h=h: dve.tensor_tensor(g2bc[:, h * 512:(h + 1) * 512], bank(3 + h), g2bc[:, h * 512:(h + 1) * 512], ALU.add)),
             r=[("ps", 3 + h), "g2bc"], w=["g2bc"])
    S.op("dve", lambda: dve.scalar_tensor_tensor(s1, modpp[:, 8:16], 1.0, n1w, ALU.add, ALU.mult), r=["modpp", "n1w"], w=["s1"])
    S.op("dve", lambda: dve.scalar_tensor_tensor(s2, modpp[:, 24:32], 1.0, n2w, ALU.add, ALU.mult), r=["modpp", "n2w"], w=["s2"])
    t1 = modpp[:, 0:8]
    t2 = modpp[:, 16:24]
    if debug:
        d_mod = dbg_out("d_mod", [128, 96])
        S.dma(lambda: sp.dma_start(out=d_mod.ap()[:, 0:80], in_=sv[:, 0:80]), r=["modpp", "s1", "s2", "condT", "cT", "n1w", "n2w"], w=["d_mod"])
        d_g = dbg_out("d_g", [128, 2 * D])
        S.dma(lambda: sp.dma_start(out=d_g.ap()[:, 0:D], in_=g1bc[:]), r=["g1bc"], w=["d_g1"])
        S.dma(lambda: sp.dma_start(out=d_g.ap()[:, D:2 * D], in_=g2bc[:]), r=["g2bc"], w=["d_g2"])
    S.fence()
    A.release(m0)
    if upto < 1:
        prep_advance(1000)
        S.fence()
        S.emit()
        return nc, S, dbg

    WINK = prep_keys("winb", D, WIN)
    winb_v = winb_d.ap().rearrange("(k p) c -> p k c", p=128)

    def rstd_from_ssq(ssq, n, keyr, keyw):
        S.op("act", lambda: act.activation(ssq, ssq, AF.Ln, bias=epsc[:], scale=1.0 / n), r=[keyr, "epsc"], w=[keyw])
        S.op("act", lambda: act.activation(ssq, ssq, AF.Exp, scale=-0.5), r=[keyw], w=[keyw])

    m1 = A.mark()
    W1 = 1288
    w1 = A([128, 8, W1], BF16, "w1")
    S.dma(lambda: sp.dma_start(out=w1[:, :, 0:1280], in_=winb_v[:, :, 0:1280]), r=WINK, w=["w1a"])
    S.dma(lambda: sp.dma_start(out=w1[:, :, 1280:1288], in_=winb_v[:, :, C_DT:C_DT + 8]), r=WINK, w=["w1b"])
    W1K = ["w1a", "w1b"]
    xs = [A([128, D], F32, "xs%d" % i) for i in range(4)]
    sq_junk = A([128, D], BF16, "sqj")
    ssq = A([128, 8], F32, "ssq")
    hT = A([128, 8, TT], BF16, "hT")
    zs = A([128, 4, TT], F32, "zs")
    u = A([128, 6, TT + 3], F32, "u")
    acc = A([128, 6, TT], F32, "acc")
    BTb = A([128, TT], BF16, "BTb")
    CTb = A([128, TT], BF16, "CTb")
    dtb = A([128, 4, 8], F32, "dtb")
    adt4 = A([128, 4, 8], F32, "adt")
    rseg2 = [A([128, 8, 128], F32, "rseg%d" % i) for i in range(2)]
    dec2 = [A([128, 8, 128], F32, "dec%d" % i) for i in range(2)]
    eac4 = A([128, 4, 4, 128], F32, "eac")
    dst4 = A([128, 4, 8], F32, "dst")
    cdec4 = A([128, 4, 8], F32, "cdec")
    xg4 = A([128, 4, 512], BF16, "xg")
    xgd4 = A([128, 4, 512], BF16, "xgd")
    Btok4 = A([128, 4, 128], BF16, "Btok")
    sctm2 = [A([128, 128], F32, "sctm%d" % i) for i in range(2)]
    G4 = A([128, 4, 8, 128], BF16, "G")
    ydg4 = A([128, 4, 512], F32, "ydg")
    prevT = A([128, 8, 64], F32, "prevT")
    prevb = A([128, 512], BF16, "prevb")
    tmpA = A([128, 512], F32, "tmpA")
    ytile = A([128, 4, TT], F32, "ytile")
    ysq = A([128, 4, TT], F32, "ysq")
    rbc = A([128, TT], F32, "rbc")
    yT = A([128, 4, TT], BF16, "yT")
    yl_ssd_v = [t_.ap().rearrange("(c p) t -> p c t", p=128) for t_ in yl_ssd]

    S.op("dve", lambda: dve.memset(u[:], 0.0), w=["u_halo"] + [("u", j) for j in range(6)])
    S.op("dve", lambda: dve.memset(prevT[:], 0.0), w=["prevT"])
    S.op("dve", lambda: dve.memset(prevb[:], 0.0), w=["prevb"])
    if debug and not skip1a:
        d_hT = dbg_out("d_hT", [128, 8, TT], BF16)
        d_xc = dbg_out("d_xc", [128, 6, TT])
        d_yt = dbg_out("d_yt", [128, 4, TT])

    if STOP == "w1":
        return finish()
    HTK = [("hT", c, s) for c in range(8) for s in range(4)]

    def hT_part1(tt):
        for s in range(4):
            r0 = (tt * 4 + s) * 128
            S.dma((lambda s=s, r0=r0: sp.dma_start(out=xs[s][:], in_=x_d.ap()[r0:r0 + 128, :])), w=[("xs", s)])
        for s in range(4):
            S.op("act", (lambda s=s: act.activation(sq_junk[:], xs[s][:], AF.Square, accum_out=ssq[:, s:s + 1])), r=[("xs", s)], w=["sqj", ("ssq", s)])
        for s in range(4):
            S.op("act", (lambda s=s: act.activation(ssq[:, s:s + 1], ssq[:, s:s + 1], AF.Ln, bias=epsc[:], scale=1.0 / D)), r=[("ssq", s), "epsc"], w=[("ssq", s)])
        for s in range(4):
            S.op("act", (lambda s=s: act.activation(ssq[:, s:s + 1], ssq[:, s:s + 1], AF.Exp, scale=-0.5)), r=[("ssq", s)], w=[("ssq", s)])
        for s in range(4):
            S.op("dve", (lambda s=s: dve.tensor_scalar_mul(xs[s][:], xs[s][:], ssq[:, s:s + 1])), r=[("xs", s), ("ssq", s)], w=[("xs", s)])

    def hT_tr(tt, s):
        for half in range(2):
            bk = 4 + (s % 2) * 2 + half
            for c4 in range(4):
                c = half * 4 + c4
                S.op("pe", (lambda c=c, c4=c4, bk=bk: pe.transpose(bank(bk, c4 * 128, (c4 + 1) * 128), xs[s][:, c * 128:(c + 1) * 128], ident)),
                     r=[("xs", s), "cst"], w=[("ps", bk)])

    def hT_ev(tt, s):
        for half in range(2):
            bk = 4 + (s % 2) * 2 + half
            for c4 in range(4):
                c = half * 4 + c4
                S.op("dve", (lambda c=c, c4=c4, bk=bk: dve.tensor_scalar(hT[:, c, s * 128:(s + 1) * 128], bank(bk, c4 * 128, (c4 + 1) * 128),
                                                                      s1[:, c:c + 1], t1[:, c:c + 1], ALU.mult, ALU.add)),
                     r=[("ps", bk), "s1", "modpp"], w=[("hT", c, s)])

    def hT_part2(tt):
        hT_tr(tt, 0); hT_tr(tt, 1); hT_ev(tt, 0); hT_tr(tt, 2); hT_ev(tt, 1); hT_tr(tt, 3); hT_ev(tt, 2); hT_ev(tt, 3)
        S.dma((lambda: sp.dma_start(out=hT_d.ap()[:, :, tt * TT:(tt + 1) * TT], in_=hT[:])), r=HTK, w=[("hTd", tt)], sk=("dma", "hTd"))
        if debug and tt == 1:
            S.dma(lambda: sp.dma_start(out=d_hT.ap(), in_=hT[:]), r=HTK, w=["d_hT"])

    if not skip1a:
        hT_part1(0)
        hT_part2(0)
    for tt in range(0 if skip1a else ntiles):
        prep_advance(6)
        for j in range(10):
            bk = 4 + (j % 4)
            for k in range(8):
                S.op("pe", (lambda j=j, k=k, bk=bk: pe.matmul(bank(bk), w1[:, k, j * 128:(j + 1) * 128], hT[:, k, :], start=(k == 0), stop=(k == 7))),
                     r=W1K + [("hT", k, s) for s in range(4)], w=[("ps", bk)])
            if j < 4:
                S.op("act", (lambda j=j, bk=bk: act.activation(zs[:, j, :], bank(bk), AF.Silu)), r=[("ps", bk)], w=[("zs", j)])
            else:
                jj = j - 4
                S.op("act", (lambda jj=jj, bk=bk: act.activation(acc[:, jj, :], bank(bk), AF.Identity, bias=cbv[:, jj:jj + 1], scale=cw[:, jj, 3:4])),
                     r=[("ps", bk), "cw", "cbv"], w=[("acc", jj)])
                S.op("act", (lambda jj=jj, bk=bk: act.copy(u[:, jj, 3:TT + 3], bank(bk))), r=[("ps", bk), "u_halo"], w=[("u", jj)])
                for kk in (2, 1, 0):
                    S.op("dve", (lambda jj=jj, kk=kk: dve.scalar_tensor_tensor(acc[:, jj, :], u[:, jj, kk:kk + TT], cw[:, jj, kk:kk + 1], acc[:, jj, :], ALU.mult, ALU.add)),
                         r=[("u", jj), ("acc", jj), "cw"], w=[("acc", jj)])
                if jj < 4:
                    S.op("act", (lambda jj=jj: act.activation(acc[:, jj, :], acc[:, jj, :], AF.Silu)), r=[("acc", jj)], w=[("acc", jj)])
                else:
                    S.op("act", (lambda jj=jj: act.activation(acc[:, jj, :], acc[:, jj, :], AF.Silu)), r=[("acc", jj)], w=[("acc", jj)])
                    tb = BTb if jj == 4 else CTb
                    S.op("dve", (lambda jj=jj, tb=tb: dve.tensor_copy(tb[:], acc[:, jj, :])), r=[("acc", jj)], w=["BTb" if jj == 4 else "CTb"])
        if STOP == "inproj":
            return finish()
        S.op("dve", lambda: dve.tensor_copy(u[:, :, 0:3], u[:, :, TT:TT + 3]), r=[("u", j) for j in range(6)], w=["u_halo"] + [("u", j) for j in range(6)])
        if debug and tt == 1:
            S.dma(lambda: sp.dma_start(out=d_xc.ap(), in_=acc[:]), r=[("acc", j) for j in range(6)], w=["d_xc"])
        for s in range(4):
            for k in range(8):
                S.op("pe", (lambda s=s, k=k: pe.matmul(bank(1, 264 + s * 8, 272 + s * 8), hT[:, k, s * 128:(s + 1) * 128], w1[:, k, 1280:1288], start=(k == 0), stop=(k == 7))),
                     r=W1K + [("hT", k, s)], w=[("ps", 1)])
        S.op("dve", lambda: dve.tensor_tensor(dtb[:], bank(1, 264, 296).rearrange("p (s h) -> p s h", s=4),
                                              v8[:, 0, :].unsqueeze(1).to_broadcast([128, 4, 8]), ALU.add),
             r=[("ps", 1), ("v8", 0)], w=["dtb"])
        S.op("act", lambda: act.activation(dtb[:], dtb[:], AF.Exp), r=["dtb"], w=["dtb"])
        S.op("act", lambda: act.activation(dtb[:], dtb[:], AF.Ln, bias=1.0), r=["dtb"], w=["dtb"])
        if STOP == "dt":
            return finish()
        def stA(ci):
            c0, c1 = ci * 128, (ci + 1) * 128
            dtc = dtb[:, ci, :]
            for j in range(4):
                S.op("pe", (lambda j=j: pe.transpose(bank(0, j * 128, (j + 1) * 128), acc[:, j, c0:c1], ident)),
                     r=[("acc", j), "cst"], w=[("ps", 0)])
            S.op("pe", (lambda: pe.transpose(bank(1, 0, 128), acc[:, 4, c0:c1], ident)), r=[("acc", 4), "cst"], w=[("ps", 1)])
            S.op("dve", (lambda: dve.tensor_tensor(xg4[:, ci, :].rearrange("p (h e) -> p h e", h=8), bank(0).rearrange("p (h e) -> p h e", h=8),
                                                   dtc.unsqueeze(2).to_broadcast([128, 8, 64]), ALU.mult)),
                 r=[("ps", 0), "dtb"], w=[("xg", ci)])
            S.op("act", (lambda: act.copy(Btok4[:, ci, :], bank(1, 0, 128))), r=[("ps", 1)], w=[("Btok", ci)])

        def stB1(ci):
            dtc = dtb[:, ci, :]
            rs = rseg2[ci % 2]
            S.op("dve", (lambda: dve.tensor_tensor(adt4[:, ci, :], dtc, Aneg[:], ALU.mult)), r=["dtb", "Aneg"], w=[("adt", ci)])
            S.op("dve", (lambda: dve.tensor_tensor(rs[:], UI.unsqueeze(1).to_broadcast([128, 8, 128]),
                                                   adt4[:, ci, :].unsqueeze(2).to_broadcast([128, 8, 128]), ALU.mult)),
                 r=[("adt", ci), "cst"], w=[("rseg", ci % 2)])

        def stB2(ci):
            rs = rseg2[ci % 2]
            dc = dec2[ci % 2]
            for hf in range(2):
                rv = rs[:, hf * 4:(hf + 1) * 4, :].rearrange("p h l -> p (h l)")
                S.op("pe", (lambda hf=hf, rv=rv: pe.matmul(bank(2 + hf), LS, rv, start=True, stop=True)), r=[("rseg", ci % 2), "cst"], w=[("ps", 2 + hf)])
                S.op("pe", (lambda hf=hf, rv=rv: pe.matmul(bank(4 + hf), ONESF, rv, start=True, stop=True)), r=[("rseg", ci % 2), "cst"], w=[("ps", 4 + hf)])
            S.op("pe", (lambda: pe.matmul(bank(1, 256, 264), LS, adt4[:, ci, :], start=True, stop=True)), r=[("adt", ci), "cst"], w=[("ps", 1)])
            S.op("act", (lambda: act.activation(dst4[:, ci, :], bank(1, 256, 264), AF.Exp)), r=[("ps", 1)], w=[("dst", ci)])
            for hf in range(2):
                S.op("act", (lambda hf=hf: act.activation(dc[:, hf * 4:(hf + 1) * 4, :].rearrange("p h l -> p (h l)"), bank(2 + hf), AF.Exp)),
                     r=[("ps", 2 + hf)], w=[("dec", ci % 2, hf)])
            acb = ps[:, 4 * 512: 6 * 512].rearrange("p (pr two l) -> p pr two l", pr=4, two=2)
            S.op("act", (lambda: act.activation(eac4[0:64, ci, :, :], acb[0:64, :, 0, :], AF.Exp)), r=[("ps", 4), ("ps", 5)], w=[("eac", ci, 0)])
            S.op("act", (lambda: act.activation(eac4[64:128, ci, :, :], acb[64:128, :, 1, :], AF.Exp)), r=[("ps", 4), ("ps", 5)], w=[("eac", ci, 1)])
            acl = ps[:, 4 * 512: 6 * 512].rearrange("p (h l) -> p h l", h=8)
            S.op("act", (lambda: act.activation(cdec4[:, ci, :], acl[:, :, 127], AF.Exp)), r=[("ps", 4), ("ps", 5)], w=[("cdec", ci)])

        def stC(ci):
            c0, c1 = ci * 128, (ci + 1) * 128
            dc = dec2[ci % 2]
            sm = sctm2[ci % 2]
            S.op("pe", (lambda: pe.matmul(bank(1, 128, 256), BTb[:, c0:c1], CTb[:, c0:c1], start=True, stop=True)),
                 r=["BTb", "CTb"], w=[("ps", 1)])
            S.op("dve", (lambda: dve.tensor_tensor(sm[:], bank(1, 128, 256), UI, ALU.mult)), r=[("ps", 1), "cst"], w=[("sctm", ci % 2)])
            S.op("dve", (lambda: dve.tensor_tensor(xgd4[:, ci, :].rearrange("p (h e) -> p h e", h=8), xg4[:, ci, :].rearrange("p (h e) -> p h e", h=8),
                                                   dst4[:, ci, :].unsqueeze(2).to_broadcast([128, 8, 64]), ALU.mult)),
                 r=[("xg", ci), ("dst", ci)], w=[("xgd", ci)])
            S.op("dve", (lambda: dve.tensor_tensor(G4[:, ci, :, :], dc[:], sm[:].unsqueeze(1).to_broadcast([128, 8, 128]), ALU.mult)),
                 r=[("dec", ci % 2, 0), ("dec", ci % 2, 1), ("sctm", ci % 2)], w=[("G", ci)])

        def stD(ci):
            for h in range(8):
                pr, hh = h // 2, h % 2
                S.op("pe", (lambda h=h, pr=pr, hh=hh: pe.matmul(ps[hh * 64:(hh + 1) * 64, 6 * 512 + pr * 128: 6 * 512 + (pr + 1) * 128],
                                                               xg4[:, ci, h * 64:(h + 1) * 64], G4[:, ci, h, :], start=True, stop=True)),
                     r=[("xg", ci), ("G", ci)], w=[("ps", 6)])
            S.op("act", (lambda: act.copy(ydg4[:, ci, :], bank(6))), r=[("ps", 6)], w=[("ydg", ci)])

        def stII(ci):
            c0, c1 = ci * 128, (ci + 1) * 128
            for pr in range(4):
                S.op("pe", (lambda pr=pr: pe.matmul(bank(7, pr * 128, (pr + 1) * 128), prevb[:, pr * 128:(pr + 1) * 128], CTb[:, c0:c1], start=True, stop=True)),
                     r=["prevb", "CTb"], w=[("ps", 7)])
            S.op("dve", (lambda: dve.tensor_tensor(tmpA[:], bank(7), eac4[:, ci, :, :].rearrange("p a l -> p (a l)"), ALU.mult)),
                 r=[("ps", 7), ("eac", ci, 0), ("eac", ci, 1)], w=["tmpA"])
            S.op("pe", (lambda: pe.matmul(bank(7), Btok4[:, ci, :], xgd4[:, ci, :], start=True, stop=True)), r=[("Btok", ci), ("xgd", ci)], w=[("ps", 7)])
            S.op("dve", (lambda: dve.tensor_tensor(prevT[:], prevT[:], cdec4[:, ci, :].unsqueeze(2).to_broadcast([128, 8, 64]), ALU.mult)),
                 r=["prevT", ("cdec", ci)], w=["prevT"])
            S.op("dve", (lambda: dve.tensor_tensor(ytile[:, :, c0:c1], ydg4[:, ci, :].rearrange("p (a l) -> p a l", a=4),
                                                   tmpA[:].rearrange("p (a l) -> p a l", a=4), ALU.add)),
                 r=[("ydg", ci), "tmpA"], w=[("ytile", ci)])
            S.op("dve", (lambda: dve.tensor_tensor(prevT[:], prevT[:], bank(7).rearrange("p (h e) -> p h e", h=8), ALU.add)),
                 r=["prevT", ("ps", 7)], w=["prevT"])
            S.op("dve", (lambda: dve.tensor_copy(prevb[:], prevT[:].rearrange("p h e -> p (h e)"))), r=["prevT"], w=["prevb"])

        for ci in range(4):
            stA(ci)
        stB1(0); stB1(1); stB2(0); stB1(2); stB2(1); stC(0); stB1(3); stB2(2); stC(1); stB2(3); stC(2); stC(3)
        for ci in range(4):
            stD(ci)
        for ci in range(4):
            stII(ci)
        if STOP == "chunks":
            return finish()
        if tt + 1 < ntiles:
            hT_part1(tt + 1)
        YK = [("ytile", ci) for ci in range(4)]
        S.op("dve", lambda: dve.tensor_tensor(ysq[:], acc[:, 0:4, :], dsk[:].unsqueeze(2).to_broadcast([128, 4, TT]), ALU.mult),
             r=[("acc", j) for j in range(4)] + ["dsk"], w=["ysq"])
        S.op("dve", lambda: dve.tensor_tensor(ytile[:], ytile[:], ysq[:], ALU.add), r=YK + ["ysq"], w=YK)
        S.op("dve", lambda: dve.tensor_tensor(ytile[:], ytile[:], zs[:], ALU.mult), r=YK + [("zs", j) for j in range(4)], w=YK)
        if debug and tt == 1:
            S.dma(lambda: sp.dma_start(out=d_yt.ap(), in_=ytile[:]), r=YK, w=["d_yt"])
        S.op("act", lambda: act.activation(ysq[:], ytile[:], AF.Square), r=YK, w=["ysq"])
        for a in range(4):
            S.op("pe", (lambda a=a: pe.matmul(bank(2), ONESF, ysq[:, a, :], start=(a == 0), stop=(a == 3))), r=["ysq", "cst"], w=[("ps", 2)])
        S.op("act", lambda: act.activation(rbc[:], bank(2), AF.Ln, bias=epsc[:], scale=1.0 / 512), r=[("ps", 2), "epsc"], w=["rbc"])
        S.op("act", lambda: act.activation(rbc[:], rbc[:], AF.Exp, scale=-0.5), r=["rbc"], w=["rbc"])
        S.op("dve", lambda: dve.tensor_tensor(ytile[:], ytile[:], rbc[:].unsqueeze(1).to_broadcast([128, 4, TT]), ALU.mult), r=YK + ["rbc"], w=YK)
        S.op("dve", lambda: dve.tensor_tensor(yT[:], ytile[:], snw[:].unsqueeze(2).to_broadcast([128, 4, TT]), ALU.mult), r=YK + ["snw"], w=["yT"])
        S.dma((lambda tt=tt: sp.dma_start(out=yl_ssd_v[tt // 4][:, :, (tt % 4) * TT:(tt % 4 + 1) * TT], in_=yT[:])), r=["yT"], w=[("yl_ssd", tt)], sk=("dma", "yT"))
        if tt + 1 < ntiles:
            hT_part2(tt + 1)
        if tt == 3:
            prep_advance(1000)
            gather(yl_ssd, ya_ssd, 0, "yl_ssd", 0)

    prep_advance(1000)
    if debug and not skip1a:
        d_yssd = dbg_out("d_yssd", [512, SEQ], BF16)
        for hf_ in range(2):
            nt_ = min(ntiles, hf_ * 4 + 4) - hf_ * 4
            if nt_ > 0:
                S.dma((lambda hf_=hf_, nt_=nt_: sp.dma_start(out=d_yssd.ap()[:, hf_ * HS: hf_ * HS + nt_ * TT], in_=yl_ssd[hf_].ap()[:, 0:nt_ * TT])),
                      r=[("yl_ssd", t) for t in range(ntiles)], w=[("d_yssd", hf_)])
    S.fence()
    A.release(m1)
    if upto < 2:
        S.emit()
        return nc, S, dbg
    if not skip1a:
        gather(yl_ssd, ya_ssd, 1, "yl_ssd", 1)

    m2 = A.mark()
    W2 = 1536
    w2 = A([128, 8, W2], BF16, "w2")
    S.dma(lambda: sp.dma_start(out=w2[:], in_=winb_v[:, :, C_Q:C_Q + W2]), r=WINK, w=["w2"])
    kT = A([128, 4, SEQ], BF16, "kT")
    vb = A([128, 32, 512], BF16, "vb")
    hT2 = [A([128, 8, TT], BF16, "hT2_%d" % i) for i in range(2)]
    qT = A([128, 4, TT], BF16, "qT")
    sqf = [A([128, TT], F32, "sqf%d" % i) for i in range(2)]
    rq = [A([128, TT], F32, "rq%d" % i) for i in range(2)]
    ebuf = [A([128, 512], F32, "e%d" % i) for i in range(2)]
    spb2 = [[A([128, 512], BF16, "sp%d_%d" % (i, q)) for q in range(2)] for i in range(2)]
    spm = [A([128, 512], BF16, "spm%d" % i) for i in range(2)]
    wb2 = [[A([128, 512], BF16, "w%d_%d" % (i, q)) for q in range(2)] for i in range(2)]
    wm = [A([128, 512], BF16, "wm%d" % i) for i in range(2)]
    cbb = [A([128, 512], BF16, "cb%d" % i) for i in range(2)]
    ysb = A([128, 4, TT], BF16, "ysb")
    yl_sb_v = [t_.ap().rearrange("(c p) t -> p c t", p=128) for t_ in yl_sb]
    US4 = US_b.unsqueeze(1).to_broadcast([128, 4, 128])
    ZBP = ((0, 1), (2, 3))
    CBK = (4, 5)
    OB = 6
    if debug and not only3:
        d_q = dbg_out("d_q", [128, 4, TT], BF16)

    for tt in range(0 if only3 else ntiles):
        hb = hT2[tt % 2]
        kh = ("hT2", tt % 2)

        def load_h(t_):
            S.dma((lambda: sp.dma_start(out=hT2[t_ % 2][:], in_=hT_d.ap()[:, :, t_ * TT:(t_ + 1) * TT])), r=[("hTd", t_)], w=[("hT2", t_ % 2)])
        if tt == 0:
            load_h(0)
        rot = [7, 6, 5]
        ri = 0
        for j in range(8):
            isq = j < 4
            c = j % 4
            bq = rot[ri % 3]
            bs = rot[(ri + 1) % 3]
            ri += 2
            col0 = (0 if isq else 512) + c * 128
            for k in range(8):
                S.op("pe", (lambda k=k, bq=bq, col0=col0: pe.matmul(bank(bq), w2[:, k, col0:col0 + 128], hb[:, k, :], start=(k == 0), stop=(k == 7))),
                     r=["w2", kh], w=[("ps", bq)])
            sb_ = sqf[j % 2]
            rb_ = rq[j % 2]
            S.op("act", (lambda bq=bq, sb_=sb_: act.activation(sb_[:], bank(bq), AF.Square)), r=[("ps", bq)], w=[("sqf", j % 2)])
            S.op("pe", (lambda bs=bs, sb_=sb_: pe.matmul(bank(bs), BD, sb_[:], start=True, stop=True)), r=[("sqf", j % 2), "cst"], w=[("ps", bs)])
            S.op("act", (lambda bs=bs, rb_=rb_: act.activation(rb_[:], bank(bs), AF.Ln, bias=epsc[:], scale=1.0 / 64)), r=[("ps", bs), "epsc"], w=[("rq", j % 2)])
            S.op("act", (lambda rb_=rb_: act.activation(rb_[:], rb_[:], AF.Exp, scale=-0.5)), r=[("rq", j % 2)], w=[("rq", j % 2)])
            if isq:
                S.op("dve", (lambda c=c, bq=bq, rb_=rb_: dve.scalar_tensor_tensor(qT[:, c, :], bank(bq), qkw[:, 0:1], rb_[:], ALU.mult, ALU.mult)),
                     r=[("ps", bq), ("rq", j % 2), "qkw"], w=[("qT", c)])
            else:
                S.op("dve", (lambda c=c, bq=bq, rb_=rb_, tt=tt: dve.scalar_tensor_tensor(kT[:, c, tt * TT:(tt + 1) * TT], bank(bq), qkw[:, 1:2], rb_[:], ALU.mult, ALU.mult)),
                     r=[("ps", bq), ("rq", j % 2), "qkw"], w=[("kT", c, tt)])
        for s in range(4):
            bq = rot[ri % 3]
            ri += 1
            for k in range(8):
                S.op("pe", (lambda k=k, bq=bq, s=s: pe.matmul(bank(bq), hb[:, k, s * 128:(s + 1) * 128], w2[:, k, 1024:1536], start=(k == 0), stop=(k == 7))),
                     r=["w2", kh], w=[("ps", bq)])
            S.op("dve", (lambda bq=bq, s=s, tt=tt: dve.tensor_copy(vb[:, tt * 4 + s, :], bank(bq))), r=[("ps", bq)], w=[("vb", tt * 4 + s)])
        if debug and tt == 1:
            S.dma(lambda: sp.dma_start(out=d_q.ap(), in_=qT[:]), r=[("qT", c) for c in range(4)], w=["d_q"])

        if tt + 1 < ntiles:
            load_h(tt + 1)
        if STOP == "qkv":
            return finish()
        steps = [("d", d) for d in range(4)] + [("o", j) for j in range(4 * tt - 1, -1, -1)]
        ns = len(steps)
        for c in range(4):
            def lo_of(n):
                kind, v = steps[n]
                return v * 128 if kind == "d" else 0

            def zmm(n, hh, c=c, tt=tt):
                kind, v = steps[n]
                p0, p1 = hh * 64, (hh + 1) * 64
                zb = ZBP[hh][n % 2]
                if kind == "d":
                    for a in range(v, 4):
                        j = 4 * tt + a - v
                        S.op("pe", (lambda a=a, j=j, v=v: pe.matmul(bank(zb, a * 128, (a + 1) * 128), kT[p0:p1, c, j * 128:(j + 1) * 128],
                                                                qT[p0:p1, c, a * 128:(a + 1) * 128], start=(a == v), stop=False, skip_group_check=True)),
                             r=[("kT", c, j // 4), ("qT", c)], w=[("ps", zb)])
                else:
                    j = v
                    S.op("pe", (lambda j=j: pe.matmul(bank(zb), kT[p0:p1, c, j * 128:(j + 1) * 128], qT[p0:p1, c, :], start=True, stop=False, skip_group_check=True)),
                         r=[("kT", c, j // 4), ("qT", c)], w=[("ps", zb)])

            def act_e(n, hh):
                lo = lo_of(n)
                zb = ZBP[hh][n % 2]
                S.op("act", (lambda: act.activation(ebuf[hh][:, lo:512], bank(zb, lo, 512), AF.Exp)), r=[("ps", zb)], w=[("e", hh)])

            def act_sp(n, hh):
                lo = lo_of(n)
                pq = n % 2
                spt = spb2[hh][pq]
                S.op("act", (lambda: act.activation(spt[:, lo:512], ebuf[hh][:, lo:512], AF.Ln, bias=1.0)), r=[("e", hh)], w=[("sp", hh, pq)])
                if n == 0:
                    S.op("dve", (lambda: dve.tensor_tensor(spm[hh][:].rearrange("p (a l) -> p a l", a=4), spt[:].rearrange("p (a l) -> p a l", a=4), US4, ALU.mult)),
                         r=[("sp", hh, pq)] + CB, w=[("spm", hh)])

            def pe_tio(n, hh):
                lo = lo_of(n)
                pq = n % 2
                zb = ZBP[hh][pq]
                first = (n == 0)
                src = spm[hh] if first else spb2[hh][pq]
                skey = ("spm", hh) if first else ("sp", hh, pq)
                S.op("pe", (lambda: pe.matmul(bank(zb, lo, 512), negLI_b, src[:, lo:512], start=False, stop=first, skip_group_check=True)),
                     r=[skey] + CB, w=[("ps", zb)])
                if not first:
                    S.op("pe", (lambda: pe.matmul(bank(zb, lo, 512), negI_b, cbb[hh][:, lo:512], start=False, stop=True, skip_group_check=True)),
                         r=[("cb", hh)] + CB, w=[("ps", zb)])
                if n < ns - 1:
                    S.op("pe", (lambda: pe.matmul(bank(CBK[hh], lo, 512), ones_b, src[:, lo:512], start=first, stop=False, skip_group_check=True)),
                         r=[skey] + CB, w=[("ps", CBK[hh])])
                    S.op("dve", (lambda: dve.tensor_copy(cbb[hh][:], bank(CBK[hh]))), r=[("ps", CBK[hh])], w=[("cb", hh)])

            def act_w(n, hh):
                lo = lo_of(n)
                pq = n % 2
                zb = ZBP[hh][pq]
                wt = wb2[hh][pq]
                S.op("act", (lambda: act.activation(wt[:, lo:512], bank(zb, lo, 512), AF.Exp)), r=[("ps", zb)], w=[("w", hh, pq)])
                if n == 0:
                    S.op("dve", (lambda: dve.tensor_tensor(wm[hh][:].rearrange("p (a l) -> p a l", a=4), wt[:].rearrange("p (a l) -> p a l", a=4), US4, ALU.mult)),
                         r=[("w", hh, pq)] + CB, w=[("wm", hh)])

            def pe_pv(n, hh):
                kind, v = steps[n]
                pq = n % 2
                first = (n == 0)
                last = (n == ns - 1)
                p0, p1 = hh * 64, (hh + 1) * 64
                vc0 = c * 128 + hh * 64
                wsrc = wm[hh] if first else wb2[hh][pq]
                wkey = ("wm", hh) if first else ("w", hh, pq)
                if kind == "d":
                    for a in range(v, 4):
                        j = 4 * tt + a - v
                        S.op("pe", (lambda a=a, j=j: pe.matmul(ps[p0:p1, OB * 512 + a * 128: OB * 512 + (a + 1) * 128], vb[:, j, vc0:vc0 + 64],
                                                                wsrc[:, a * 128:(a + 1) * 128], start=(first and a == 0), stop=False, skip_group_check=True)),
                             r=[("vb", j), wkey], w=[("ps", OB)])
                else:
                    j = v
                    S.op("pe", (lambda j=j: pe.matmul(ps[p0:p1, OB * 512: OB * 512 + 512], vb[:, j, vc0:vc0 + 64], wsrc[:, :], start=False, stop=last,
                                                      skip_group_check=True)),
                         r=[("vb", j), wkey], w=[("ps", OB)])

            for n0 in range(min(2, ns)):
                for hh in range(2):
                    zmm(n0, hh)
            for hh in range(2):
                act_e(0, hh)
            for hh in range(2):
                act_sp(0, hh)
                pe_tio(0, hh)
            for n in range(ns):
                for hh in range(2):
                    if n + 1 < ns:
                        act_e(n + 1, hh)
                    act_w(n, hh)
                    if n + 1 < ns:
                        act_sp(n + 1, hh)
                        pe_tio(n + 1, hh)
                    if n + 2 < ns:
                        zmm(n + 2, hh)
                for hh in range(2):
                    pe_pv(n, hh)
                if STOP == "step0":
                    return finish()
            S.op("dve", (lambda c=c: dve.tensor_copy(ysb[:, c, :], bank(OB))), r=[("ps", OB)], w=[("ysb", c)])
            if STOP == "chunk0":
                return finish()
        S.dma((lambda tt=tt: sp.dma_start(out=yl_sb_v[tt // 4][:, :, (tt % 4) * TT:(tt % 4 + 1) * TT], in_=ysb[:])), r=[("ysb", c) for c in range(4)], w=[("yl_sb", tt)], sk=("dma", "ysb"))
        if tt == 3:
            gather(yl_sb, ya_sb, 0, "yl_sb", 2)

    if debug and not only3:
        d_ysb = dbg_out("d_ysb", [512, SEQ], BF16)
        for hf_ in range(2):
            nt_ = min(ntiles, hf_ * 4 + 4) - hf_ * 4
            if nt_ > 0:
                S.dma((lambda hf_=hf_, nt_=nt_: sp.dma_start(out=d_ysb.ap()[:, hf_ * HS: hf_ * HS + nt_ * TT], in_=yl_sb[hf_].ap()[:, 0:nt_ * TT])),
                      r=[("yl_sb", t) for t in range(ntiles)], w=[("d_ysb", hf_)])
    S.fence()
    A.release(mP)
    if upto < 3:
        S.emit()
        return nc, S, dbg
    if not only3:
        gather(yl_sb, ya_sb, 1, "yl_sb", 3)

    ya_ssd_v = [t_.ap().rearrange("(c p) t -> p c t", p=128) for t_ in ya_ssd]
    ya_sb_v = [t_.ap().rearrange("(c p) t -> p c t", p=128) for t_ in ya_sb]
    woutb_v = woutb_d.ap().rearrange("(k p) c -> p k c", p=128)
    wgb_v = wgb_d.ap().rearrange("(k p) c -> p k c", p=128)
    wub_v = wub_d.ap().rearrange("(k p) c -> p k c", p=128)
    wdb_v = wdb_d.ap().rearrange("(k p) c -> p k c", p=128)
    WOK = prep_keys("woutb", 2 * D, D)
    WGK = prep_keys("wgb", D, DFF)
    WUK = prep_keys("wub", D, DFF)
    WDK = prep_keys("wdb", DFF, D)
    yh = [A([128, 16, TT], BF16, "yh%d" % i) for i in range(2)]
    xt = A([128, 4, D], F32, "xt")
    x1 = A([128, 4, D], F32, "x1")
    xn = [A([128, D], F32, "xn%d" % i) for i in range(2)]
    h2T = A([128, 8, TT], BF16, "h2T")
    actT = A([128, NCH, TT], BF16, "actT")
    sg = [A([128, TT], F32, "sg%d" % i) for i in range(2)]
    tmpo = [A([128, 512], F32, "tmpo%d" % i) for i in range(2)]
    ob = [A([128, 512], F32, "ob%d" % i) for i in range(4)]
    ssq3 = A([128, 4], F32, "ssq3")
    sqj3 = A([128, D], BF16, "sqj3")
    NRING = 6
    ring = [A([128, 4096], BF16, "ring%d" % i) for i in range(NRING)]
    ring_i = [0]

    def ring_load(fn_src, rkeys):
        i = ring_i[0] % NRING
        ring_i[0] += 1
        t = ring[i]
        o, i_ = fn_src(t)
        S.dma((lambda o=o, i_=i_: sp.dma_start(out=o, in_=i_)), r=rkeys, w=[("ring", i)])
        return t, ("ring", i)

    obi = 0
    def load_inputs(ut):
        for hlf in range(2):
            t0 = ut * TT
            S.dma((lambda hlf=hlf, t0=t0: sp.dma_start(out=yh[hlf][:, 0:8, :], in_=ya_ssd_v[hlf][:, :, t0:t0 + TT])), r=[("ya", "yl_ssd", hlf)], w=[("yh", hlf, 0)])
            S.dma((lambda hlf=hlf, t0=t0: sp.dma_start(out=yh[hlf][:, 8:16, :], in_=ya_sb_v[hlf][:, :, t0:t0 + TT])), r=[("ya", "yl_sb", hlf)], w=[("yh", hlf, 1)])
        for q4 in range(4):
            pt = q4 // 2
            v0 = yh[0][:, q4 * 4:(q4 + 1) * 4, :].rearrange("p c t -> p (c t)")
            v1 = yh[1][:, q4 * 4:(q4 + 1) * 4, :].rearrange("p c t -> p (c t)")
            S.op("dve", (lambda v0=v0: dve.tensor_scalar_mul(v0, v0, flags[:, 0:1])), r=[("yh", 0, pt), "flags"], w=[("yh", 0, pt)])
            S.op("dve", (lambda v0=v0, v1=v1: dve.scalar_tensor_tensor(v0, v1, flags[:, 1:2], v0, ALU.mult, ALU.add)),
                 r=[("yh", 0, pt), ("yh", 1, pt), "flags"], w=[("yh", 0, pt)])
        S.dma((lambda ut=ut: sp.dma_start(out=xt[:], in_=xh_d.ap()[ut * TT:(ut + 1) * TT, :].rearrange("(s p) d -> p s d", p=128))), w=["xt"])

    load_inputs(0)
    for ut in range(4):
        for g4 in range(4):
            wt, wkey = ring_load(lambda t, g4=g4: (t[:].rearrange("p (k c) -> p k c", k=4), woutb_v[:, g4 * 4:(g4 + 1) * 4, :]), WOK)
            wv = wt[:].rearrange("p (k c) -> p k c", k=4)
            for kk in range(4):
                kc = g4 * 4 + kk
                for s in range(4):
                    for hf in range(2):
                        S.op("pe", (lambda kc=kc, kk=kk, s=s, hf=hf, wv=wv: pe.matmul(bank(s * 2 + hf), yh[0][:, kc, s * 128:(s + 1) * 128], wv[:, kk, hf * 512:(hf + 1) * 512],
                                                                                      start=(kc == 0), stop=(kc == 15))),
                             r=[("yh", 0, 0), ("yh", 0, 1), wkey], w=[("ps", s * 2 + hf)])
        for s in range(4):
            for hf in range(2):
                tb = tmpo[(s * 2 + hf) % 2]
                tk = ("tmpo", (s * 2 + hf) % 2)
                S.op("dve", (lambda s=s, hf=hf, tb=tb: dve.tensor_tensor(tb[:], bank(s * 2 + hf), g1bc[:, hf * 512:(hf + 1) * 512], ALU.mult)),
                     r=[("ps", s * 2 + hf), "g1bc"], w=[tk])
                S.op("dve", (lambda s=s, hf=hf, tb=tb: dve.tensor_tensor(x1[:, s, hf * 512:(hf + 1) * 512], tb[:], xt[:, s, hf * 512:(hf + 1) * 512], ALU.add)),
                     r=[tk, "xt"], w=[("x1", s, hf)])
        for s in range(4):
            sc = ssq3[:, s:s + 1]
            S.op("act", (lambda s=s, sc=sc: act.activation(sqj3[:], x1[:, s, :], AF.Square, accum_out=sc)), r=[("x1", s, 0), ("x1", s, 1)], w=["sqj3", ("ssq3", s)])
            rstd_from_ssq(sc, D, ("ssq3", s), ("ssq3", s))
            xb = xn[s % 2]
            kx = ("xn", s % 2)
            S.op("dve", (lambda s=s, sc=sc, xb=xb: dve.tensor_scalar_mul(xb[:], x1[:, s, :], sc)), r=[("x1", s, 0), ("x1", s, 1), ("ssq3", s)], w=[kx])
            for c in range(8):
                S.op("pe", (lambda c=c, s=s, xb=xb: pe.transpose(bank(c, s * 128, (s + 1) * 128), xb[:, c * 128:(c + 1) * 128], ident)),
                     r=[kx, "cst"], w=[("ps", c)])
        for c in range(8):
            S.op("dve", (lambda c=c: dve.tensor_scalar(h2T[:, c, :], bank(c), s2[:, c:c + 1], t2[:, c:c + 1], ALU.mult, ALU.add)),
                 r=[("ps", c), "s2", "modpp"], w=[("h2T", c)])
        H2K = [("h2T", c) for c in range(8)]
        if ut + 1 < 4:
            load_inputs(ut + 1)
        fi = 0
        for fg in range(6):
            nf = 4 if fg < 5 else 2
            ncol = nf * 128
            gt, gkey = ring_load(lambda t, fg=fg, ncol=ncol: (t[:].rearrange("p (k c) -> p k c", k=8)[:, :, 0:ncol], wgb_v[:, :, fg * 512: fg * 512 + ncol]), WGK)
            utile, ukey = ring_load(lambda t, fg=fg, ncol=ncol: (t[:].rearrange("p (k c) -> p k c", k=8)[:, :, 0:ncol], wub_v[:, :, fg * 512: fg * 512 + ncol]), WUK)
            gv = gt[:].rearrange("p (k c) -> p k c", k=8)
            uv = utile[:].rearrange("p (k c) -> p k c", k=8)
            for f in range(nf):
                bg_ = (fi % 4) * 2
                bu_ = bg_ + 1
                for k in range(8):
                    S.op("pe", (lambda k=k, f=f, bg_=bg_, gv=gv: pe.matmul(bank(bg_), gv[:, k, f * 128:(f + 1) * 128], h2T[:, k, :], start=(k == 0), stop=(k == 7))),
                         r=H2K + [gkey], w=[("ps", bg_)])
                for k in range(8):
                    S.op("pe", (lambda k=k, f=f, bu_=bu_, uv=uv: pe.matmul(bank(bu_), uv[:, k, f * 128:(f + 1) * 128], h2T[:, k, :], start=(k == 0), stop=(k == 7))),
                         r=H2K + [ukey], w=[("ps", bu_)])
                sgb = sg[fi % 2]
                S.op("act", (lambda bg_=bg_, sgb=sgb: act.activation(sgb[:], bank(bg_), AF.Silu)), r=[("ps", bg_)], w=[("sg", fi % 2)])
                fch = fg * 4 + f
                S.op("dve", (lambda bu_=bu_, sgb=sgb, fch=fch: dve.tensor_tensor(actT[:, fch, :], bank(bu_), sgb[:], ALU.mult)),
                     r=[("ps", bu_), ("sg", fi % 2)], w=[("actT", fch)])
                fi += 1
        for g in range(6):
            nk = 4 if g < 5 else 2
            wt, wkey = ring_load(lambda t, g=g, nk=nk: (t[:].rearrange("p (k c) -> p k c", k=4)[:, 0:nk, :], wdb_v[:, g * 4: g * 4 + nk, :]), WDK)
            wv = wt[:].rearrange("p (k c) -> p k c", k=4)
            for kk in range(nk):
                kc = g * 4 + kk
                for s in range(4):
                    for hf in range(2):
                        S.op("pe", (lambda kc=kc, kk=kk, s=s, hf=hf, wv=wv: pe.matmul(bank(s * 2 + hf), actT[:, kc, s * 128:(s + 1) * 128], wv[:, kk, hf * 512:(hf + 1) * 512],
                                                                                      start=(kc == 0), stop=(kc == NCH - 1))),
                             r=[("actT", kc), wkey], w=[("ps", s * 2 + hf)])
        for s in range(4):
            for hf in range(2):
                tb = tmpo[(s * 2 + hf) % 2]
                tk = ("tmpo", (s * 2 + hf) % 2)
                o = ob[obi % 4]
                ok = ("ob", obi % 4)
                obi += 1
                S.op("dve", (lambda s=s, hf=hf, tb=tb: dve.tensor_tensor(tb[:], bank(s * 2 + hf), g2bc[:, hf * 512:(hf + 1) * 512], ALU.mult)),
                     r=[("ps", s * 2 + hf), "g2bc"], w=[tk])
                S.op("dve", (lambda s=s, hf=hf, tb=tb, o=o: dve.tensor_tensor(o[:], tb[:], x1[:, s, hf * 512:(hf + 1) * 512], ALU.add)),
                     r=[tk, ("x1", s, hf)], w=[ok])
                r0 = ut * TT + s * 128
                S.dma((lambda o=o, r0=r0, hf=hf: sp.dma_start(out=out_d.ap()[r0:r0 + 128, hf * 512:(hf + 1) * 512], in_=o[:])), r=[ok], w=[("out", ut, s, hf)], sk=("dma",) + ok)
    S.fence()
    S.emit()
    return nc, S, dbg


def _consts():
    i = np.arange(128)[:, None]
    j = np.arange(128)[None, :]
    c = np.zeros((128, 7, 128), np.float32)
    c[:, 0] = (i == j)
    c[:, 1] = (i <= j)
    c[:, 2] = (i < j)
    c[:, 3] = (i >= j)
    c[:, 4] = (i > j)
    c[:, 5] = 1.0
    c[:, 6] = ((i // 64) == (j // 64))
    return c


def _pp(v):
    v = np.asarray(v, np.float32)
    return np.ascontiguousarray(v.reshape(-1, 128).T)


def make_in_maps(x, c, w_ada, b_ada, norm1_w, w_in, conv_w, conv_b, dt_bias, a_log, d_skip,
                 ssd_norm_w, q_norm_w, k_norm_w, w_out, norm2_w, w_gate, w_up, w_down):
    f = lambda a: np.ascontiguousarray(np.asarray(a, np.float32))
    x, c = f(x), f(c)
    w_ada, b_ada = f(w_ada)[0], f(b_ada)[0]
    w_in, conv_w, conv_b = f(w_in)[0], f(conv_w)[0], f(conv_b)[0]
    dt_bias, a_log, d_skip = f(dt_bias)[0], f(a_log)[0], f(d_skip)[0]
    ssd_norm_w, q_norm_w, k_norm_w = f(ssd_norm_w)[0], f(q_norm_w)[0], f(k_norm_w)[0]
    w_out, w_gate, w_up, w_down = f(w_out)[0], f(w_gate)[0], f(w_up)[0], f(w_down)[0]
    n1, n2 = f(norm1_w)[0], f(norm2_w)[0]
    consts = _consts()
    b_pp = np.concatenate([_pp(b_ada[0:1024]), _pp(b_ada[1024:2048]), _pp(b_ada[3072:4096]), _pp(b_ada[4096:5120])], axis=1)
    b_g = np.ascontiguousarray(np.stack([b_ada[2048:3072], b_ada[5120:6144]]))
    maps = []
    for core in range(8):
        b, g = core // 2, core % 2
        cols = np.concatenate([
            np.arange(g * 512, (g + 1) * 512),
            1024 + np.arange(g * 512, (g + 1) * 512),
            2048 + np.arange(g * 128, (g + 1) * 128),
            2304 + np.arange(g * 128, (g + 1) * 128),
            2576 + np.arange(g * 512, (g + 1) * 512),
            3600 + np.arange(g * 512, (g + 1) * 512),
            4624 + np.arange(g * 512, (g + 1) * 512),
            2560 + np.arange(g * 8, (g + 1) * 8),
        ])
        cch = np.concatenate([np.arange(g * 512, (g + 1) * 512), 1024 + np.arange(g * 128, (g + 1) * 128),
                              1280 + np.arange(g * 128, (g + 1) * 128)])
        cw = np.ascontiguousarray(conv_w[:, cch].T.reshape(6, 128, 4).transpose(1, 0, 2))
        cb = _pp(conv_b[cch])
        vec8 = np.zeros((3, 8), np.float32)
        vec8[0] = dt_bias[g * 8:(g + 1) * 8]
        vec8[1] = a_log[g * 8:(g + 1) * 8]
        dsk = np.ascontiguousarray(np.repeat(d_skip[g * 8:(g + 1) * 8], 64).reshape(4, 128).T)
        snw = _pp(ssd_norm_w[g * 512:(g + 1) * 512])
        qkw = np.ascontiguousarray(np.stack([np.tile(q_norm_w, 2), np.tile(k_norm_w, 2)], axis=1))
        flags = np.zeros((128, 2), np.float32)
        flags[:, g] = 1.0
        maps.append({
            "x": x[b], "xh": np.ascontiguousarray(x[b, g * 2048:(g + 1) * 2048]), "cT": _pp(c[b]),
            "w_ada": w_ada, "b_pp": b_pp, "b_g": b_g, "n1w": _pp(n1), "n2w": _pp(n2),
            "w_in": np.ascontiguousarray(w_in[:, cols]), "conv_w": cw, "conv_b": cb, "vec8": vec8,
            "dsk": dsk, "snw": snw, "qkw": qkw, "w_out": w_out, "w_gate": w_gate, "w_up": w_up, "w_down": w_down,
            "consts": consts, "flags": flags,
        })
    return maps


_CACHE = {}


def kernel(**inputs):
    if "nc" not in _CACHE:
        _CACHE["nc"] = build(False)[0]
    nc = _CACHE["nc"]
    maps = make_in_maps(**inputs)
    res = run_bass_kernel_spmd(nc, maps, core_ids=list(range(8)))
    out = np.empty((NB, SEQ, D), np.float32)
    for core in range(8):
        b, g = core // 2, core % 2
        out[b, g * 2048:(g + 1) * 2048] = res.results[core]["out"]
    return out
```

```python
import types
import numpy as np
import concourse.bass as bass
import concourse.mybir as mybir
from concourse.bass_utils import run_bass_kernel_spmd

F32 = mybir.dt.float32
BF16 = mybir.dt.bfloat16
AF = mybir.ActivationFunctionType
ALU = mybir.AluOpType

D = 1024
SEQ = 4096
NB = 4
DFF = 2816
NCH = 22
EPS = 1e-6
WIN = 2824
C_Z, C_X, C_B, C_C, C_Q, C_K, C_V, C_DT = 0, 512, 1024, 1152, 1280, 1792, 2304, 2816
TT = 512
NT = SEQ // TT
SB_BASE = 20480
SB_END = 229376
NO_ALIAS = False
STOP = None


def _freeze(fn):
    if fn is None or fn.__closure__ is None:
        return fn
    cells = []
    for c in fn.__closure__:
        try:
            cells.append(types.CellType(c.cell_contents))
        except ValueError:
            cells.append(c)
    return types.FunctionType(fn.__code__, fn.__globals__, fn.__name__, fn.__defaults__, tuple(cells))


class Sched:
    ENG = ("pe", "act", "dve", "pool", "sp")

    def __init__(self, nc):
        self.nc = nc
        self.e = {"pe": nc.tensor, "act": nc.scalar, "dve": nc.vector, "pool": nc.gpsimd, "sp": nc.sync}
        self.ops = []
        self.last_w = {}
        self.readers = {}
        self.fence_idx = None
        self.last_on = {}
        self.last_dma = {}

    def _add(self, eng, fn, r, w, kind, sk=None):
        idx = len(self.ops)
        deps = {}
        for k in r:
            p = self.last_w.get(k)
            if p is not None:
                deps[p] = True
        for k in w:
            p = self.last_w.get(k)
            if p is not None:
                deps.setdefault(p, False)
            for p in self.readers.get(k, ()):
                if p != idx:
                    deps.setdefault(p, False)
        if self.fence_idx is not None:
            deps[self.fence_idx] = True
        op = dict(eng=eng, fn=_freeze(fn), kind=kind, deps=deps, sk=sk, sig=False)
        for p, raw in deps.items():
            po = self.ops[p]
            need = po["kind"] != "c" or kind != "c" or po["eng"] != eng or raw or eng != "pe"
            if need:
                po["sig"] = True
        for k in r:
            self.readers.setdefault(k, []).append(idx)
        for k in w:
            self.last_w[k] = idx
            self.readers[k] = []
        self.ops.append(op)
        if kind == "c":
            self.last_on[eng] = idx
        else:
            self.last_dma[sk] = idx
        return idx

    def op(self, eng, fn, r=(), w=()):
        r = tuple(r)
        w = tuple(w) + tuple(k for k in r if isinstance(k, tuple) and k and k[0] in ("ps", "ps0") and k not in w)
        return self._add(eng, fn, r, w, "c")

    def dma(self, fn, r=(), w=(), q="sp", sk=None):
        w = tuple(w)
        if sk is None:
            sk = ("dma",) + tuple(w[:1])
        return self._add(q, fn, tuple(r), w, "d", sk)

    def cc(self, fn, r=(), w=(), sk=None):
        return self._add("pool", fn, tuple(r), tuple(w), "cc", sk)

    def fence(self):
        deps = {}
        for e, i in self.last_on.items():
            deps[i] = True
        for sk, i in self.last_dma.items():
            deps[i] = True
        idx = len(self.ops)
        op = dict(eng="sp", fn=None, kind="f", deps=deps, sk=None, sig=True)
        for p in deps:
            self.ops[p]["sig"] = True
        self.ops.append(op)
        self.fence_idx = idx
        self.last_on = {"sp": idx}
        self.last_dma = {}

    def emit(self):
        nc = self.nc
        sems = {}

        def sem_for(name):
            if name not in sems:
                sems[name] = nc.alloc_semaphore("s%d" % len(sems))
            return sems[name]

        cnt = {}
        for op in self.ops:
            if op["kind"] in ("c", "f"):
                key = ("eng", op["eng"])
                inc = 1
            elif op["kind"] == "d":
                key = op["sk"]
                inc = 16
                op["sig"] = True
            else:
                key = op["sk"]
                inc = 1
                op["sig"] = True
            if op["sig"]:
                cnt[key] = cnt.get(key, 0) + inc
                op["sem"] = key
                op["cnt"] = cnt[key]
                op["inc"] = inc
        seen = {e: {} for e in self.ENG}
        nwait = 0
        for op in self.ops:
            eng = op["eng"]
            E = self.e[eng]
            sn = seen[eng]
            need = {}
            for p, raw in op["deps"].items():
                po = self.ops[p]
                if po["kind"] == "c" and op["kind"] == "c" and po["eng"] == eng and not raw and eng == "pe":
                    continue
                k, c = po["sem"], po["cnt"]
                if sn.get(k, 0) >= c:
                    continue
                if need.get(k, 0) < c:
                    need[k] = c
            for k, c in need.items():
                E.wait_ge(sem_for(k), c)
                nwait += 1
                sn[k] = c
            own = ("eng", eng)
            for p, raw in op["deps"].items():
                po = self.ops[p]
                snap = po.get("snap")
                if snap:
                    skipped = po["kind"] == "c" and op["kind"] == "c" and po["eng"] == eng and not raw and eng == "pe"
                    for k, c in snap.items():
                        if skipped and k == own:
                            continue
                        if sn.get(k, 0) < c:
                            sn[k] = c
            if op["kind"] == "f":
                E.sem_inc(sem_for(op["sem"]), 1)
            else:
                ins = op["fn"]()
                if op["sig"]:
                    ins.then_inc(sem_for(op["sem"]), op["inc"])
            if op["sig"]:
                op["snap"] = dict(sn)
                op["snap"][op["sem"]] = op["cnt"]
            op["fn"] = None
        self.stats = (len(self.ops), nwait, len(sems))


class Alloc:
    def __init__(self, nc):
        self.nc = nc
        self.off = SB_BASE
        self.n = 0

    def __call__(self, shape, dt, name=None):
        nbytes = int(np.prod(shape[1:])) * (4 if dt == F32 else 2)
        nbytes = (nbytes + 63) // 64 * 64
        assert NO_ALIAS or self.off + nbytes <= SB_END, ("SBUF overflow", name, self.off, nbytes)
        self.n += 1
        t = self.nc.alloc_sbuf_tensor_at("t%d_%s" % (self.n, name or "x"), list(shape), dt, offset=self.off)
        self.off += nbytes
        return t

    def mark(self):
        return self.off

    def release(self, m):
        if not NO_ALIAS:
            self.off = m


def build(debug=False, upto=3, ntiles=NT, fake_cc=False, skip1a=False, only3=False):
    nc = bass.Bass("TRN2", target_bir_lowering=False)
    S = Sched(nc)
    A = Alloc(nc)
    pe, act, dve, pool, sp = nc.tensor, nc.scalar, nc.vector, nc.gpsimd, nc.sync

    def din(name, shape, dt=F32):
        return nc.dram_tensor(name, list(shape), dt, kind="ExternalInput")

    x_d = din("x", [SEQ, D])
    xh_d = din("xh", [SEQ // 2, D])
    cT_d = din("cT", [128, 8])
    wada_d = din("w_ada", [D, 6 * D])
    bpp_d = din("b_pp", [128, 32])
    bg_d = din("b_g", [2, D])
    n1w_d = din("n1w", [128, 8])
    n2w_d = din("n2w", [128, 8])
    win_d = din("w_in", [D, WIN])
    cw_d = din("conv_w", [128, 6, 4])
    cb_d = din("conv_b", [128, 6])
    vec8_d = din("vec8", [3, 8])
    dsk_d = din("dsk", [128, 4])
    snw_d = din("snw", [128, 4])
    qkw_d = din("qkw", [128, 2])
    wout_d = din("w_out", [2 * D, D])
    wg_d = din("w_gate", [D, DFF])
    wu_d = din("w_up", [D, DFF])
    wd_d = din("w_down", [DFF, D])
    consts_d = din("consts", [128, 7, 128])
    flags_d = din("flags", [128, 2])
    out_d = nc.dram_tensor("out", [SEQ // 2, D], F32, kind="ExternalOutput")

    winb_d = nc.dram_tensor("winb", [D, WIN], BF16, kind="ExternalOutput")
    woutb_d = nc.dram_tensor("woutb", [2 * D, D], BF16, kind="ExternalOutput")
    wgb_d = nc.dram_tensor("wgb", [D, DFF], BF16, kind="ExternalOutput")
    wub_d = nc.dram_tensor("wub", [D, DFF], BF16, kind="ExternalOutput")
    wdb_d = nc.dram_tensor("wdb", [DFF, D], BF16, kind="ExternalOutput")
    hT_d2 = nc.dram_tensor("hTs", [128, 8 * SEQ], BF16, kind="ExternalInput") if skip1a else nc.dram_tensor("hTs", [128, 8 * SEQ], BF16, kind="ExternalOutput")

    class _HT:
        def ap(self):
            return hT_d2.ap().rearrange("p (k t) -> p k t", k=8)
    hT_d = _HT()
    HS = SEQ // 2
    yl_ssd = [nc.dram_tensor("yl_ssd%d" % i, [512, HS], BF16) for i in range(2)]
    yl_sb = [nc.dram_tensor("yl_sb%d" % i, [512, HS], BF16) for i in range(2)]
    if only3:
        skip1a = True
        ya_ssd = [nc.dram_tensor("ya_ssd%d" % i, [1024, HS], BF16, kind="ExternalInput") for i in range(2)]
        ya_sb = [nc.dram_tensor("ya_sb%d" % i, [1024, HS], BF16, kind="ExternalInput") for i in range(2)]
    else:
        ya_ssd = [nc.dram_tensor("ya_ssd%d" % i, [1024, HS], BF16) for i in range(2)]
        ya_sb = [nc.dram_tensor("ya_sb%d" % i, [1024, HS], BF16) for i in range(2)]

    def gather(src, dst, hf, name, ci):
        keys = [(name, t) for t in range(hf * 4, min(ntiles, hf * 4 + 4))]
        if not keys:
            return
        if fake_cc:
            ncol = (min(ntiles, hf * 4 + 4) - hf * 4) * TT
            for hh_ in range(2):
                S.dma((lambda hh_=hh_: sp.dma_start(out=dst[hf].ap()[hh_ * 512:(hh_ + 1) * 512, 0:ncol], in_=src[hf].ap()[:, 0:ncol])),
                      r=keys, w=[("ya", name, hf)], sk=("dma", "fcc", name, hf, hh_))
        else:
            S.cc(lambda: pool.collective_compute("AllGather", ALU.bypass, replica_groups=[[0, 1], [2, 3], [4, 5], [6, 7]],
                                                 ins=[src[hf].ap().opt()], outs=[dst[hf].ap().opt()]),
                 r=keys, w=[("ya", name, hf)], sk=("cc", ci))

    dbg = {}

    def dbg_out(name, shape, dt=F32):
        if debug:
            dbg[name] = nc.dram_tensor(name, list(shape), dt, kind="ExternalOutput")
            return dbg[name]
        return None

    ps = nc.alloc_psum_tensor("ps", [128, 8 * 512], F32)

    def bank(b, c0=0, c1=512):
        return ps[:, b * 512 + c0: b * 512 + c1]

    cst = A([128, 7, 128], F32, "cst")
    ident = cst[:, 0, :]
    UI = cst[:, 1, :]
    US = cst[:, 2, :]
    LI = cst[:, 3, :]
    LS = cst[:, 4, :]
    ONESF = cst[:, 5, :]
    BD = cst[:, 6, :]
    cbf = A([128, 4, 128], BF16, "cbf")
    negLI_b, negI_b, ones_b, US_b = cbf[:, 0, :], cbf[:, 1, :], cbf[:, 2, :], cbf[:, 3, :]
    flags = A([128, 2], F32, "flags")
    sv = A([128, 96], F32, "sv")
    cT = sv[:, 0:8]
    condT = sv[:, 8:16]
    n1w = sv[:, 16:24]
    n2w = sv[:, 24:32]
    modpp = sv[:, 32:64]
    s1 = sv[:, 64:72]
    s2 = sv[:, 72:80]
    bpp = A([128, 32], F32, "bpp")
    g1bc = A([128, D], F32, "g1bc")
    g2bc = A([128, D], F32, "g2bc")
    cw = A([128, 6, 4], F32, "cw")
    cbv = A([128, 6], F32, "cbv")
    v8 = A([128, 3, 8], F32, "v8")
    Aneg = A([128, 8], F32, "Aneg")
    dsk = A([128, 4], F32, "dsk")
    snw = A([128, 4], F32, "snw")
    qkw = A([128, 2], F32, "qkw")
    epsc = A([128, 1], F32, "epsc")

    S.dma(lambda: sp.dma_start(out=cst[:], in_=consts_d.ap()), w=["cst"])
    for (t, d, k) in ((flags, flags_d, "flags"), (bpp, bpp_d, "bpp"), (cw, cw_d, "cw"), (cbv, cbv_d if False else cb_d, "cbv"),
                      (dsk, dsk_d, "dsk"), (snw, snw_d, "snw"), (qkw, qkw_d, "qkw")):
        S.dma((lambda t=t, d=d: sp.dma_start(out=t[:], in_=d.ap())), w=[k])
    S.dma(lambda: sp.dma_start(out=sv[:, 0:8], in_=cT_d.ap()), w=["cT"])
    S.dma(lambda: sp.dma_start(out=sv[:, 16:24], in_=n1w_d.ap()), w=["n1w"])
    S.dma(lambda: sp.dma_start(out=sv[:, 24:32], in_=n2w_d.ap()), w=["n2w"])
    for r in range(2):
        S.dma((lambda r=r: sp.dma_start(out=v8[:, r, :], in_=vec8_d.ap()[r:r + 1, :].partition_broadcast(128))),
              w=[("v8", r)])
    S.dma(lambda: sp.dma_start(out=g1bc[:], in_=bg_d.ap()[0:1, :].partition_broadcast(128)), w=["g1bc"])
    S.dma(lambda: sp.dma_start(out=g2bc[:], in_=bg_d.ap()[1:2, :].partition_broadcast(128)), w=["g2bc"])

    S.op("dve", lambda: dve.memset(epsc[:], EPS), w=["epsc"])
    S.op("dve", lambda: dve.tensor_scalar_mul(cbf[:, 0, :], LI, -1.0), r=["cst"], w=["cbf0"])
    S.op("dve", lambda: dve.tensor_scalar_mul(cbf[:, 1, :], ident, -1.0), r=["cst"], w=["cbf1"])
    S.op("dve", lambda: dve.tensor_copy(cbf[:, 2, :], ONESF), r=["cst"], w=["cbf2"])
    S.op("dve", lambda: dve.tensor_copy(cbf[:, 3, :], US), r=["cst"], w=["cbf3"])
    CB = ["cbf0", "cbf1", "cbf2", "cbf3"]
    S.op("dve", lambda: dve.tensor_scalar_mul(qkw[:, 0:1], qkw[:, 0:1], 0.125), r=["qkw"], w=["qkw"])
    S.op("act", lambda: act.activation(Aneg[:], v8[:, 1, :], AF.Exp), r=[("v8", 1)], w=["Aneg"])
    S.op("dve", lambda: dve.tensor_scalar_mul(Aneg[:], Aneg[:], -1.0), r=["Aneg"], w=["Aneg"])
    S.op("act", lambda: act.activation(condT, cT, AF.Silu), r=["cT"], w=["condT"])

    PW = 2048
    mP = A.mark()
    stf = [A([128, PW], F32, "stf%d" % i) for i in range(2)]
    stb = [A([128, PW], BF16, "stb%d" % i) for i in range(2)]
    prep_list = []

    def prep_matrix(src, dst, rows, cols, key):
        nr = rows // 128
        ncp = (cols + PW - 1) // PW
        cw_ = (cols + ncp - 1) // ncp
        for r in range(nr):
            for c in range(ncp):
                c0, c1 = c * cw_, min(cols, (c + 1) * cw_)
                prep_list.append((src, dst, r, c0, c1, (key, r)))

    prep_matrix(win_d, winb_d, D, WIN, "winb")
    prep_matrix(wout_d, woutb_d, 2 * D, D, "woutb")
    prep_matrix(wg_d, wgb_d, D, DFF, "wgb")
    prep_matrix(wu_d, wub_d, D, DFF, "wub")
    prep_matrix(wd_d, wdb_d, DFF, D, "wdb")
    prep_state = {"next_load": 0, "next_cast": 0}

    def prep_load(i):
        src, dst, r, c0, c1, key = prep_list[i]
        b = i % 2
        S.dma((lambda: pool.dma_start(out=stf[b][:, 0:c1 - c0], in_=src.ap()[r * 128:(r + 1) * 128, c0:c1])),
              w=[("stf", b)], q="pool")

    def prep_cast_store(i):
        src, dst, r, c0, c1, key = prep_list[i]
        b = i % 2
        S.op("pool", (lambda: pool.tensor_copy(stb[b][:, 0:c1 - c0], stf[b][:, 0:c1 - c0])),
             r=[("stf", b)], w=[("stb", b)])
        S.dma((lambda: pool.dma_start(out=dst.ap()[r * 128:(r + 1) * 128, c0:c1], in_=stb[b][:, 0:c1 - c0])),
              r=[("stb", b)], w=[key + (c0,)], q="pool", sk=("dma", "stbo", b))

    def prep_advance(n):
        for _ in range(n):
            i = prep_state["next_cast"]
            if i >= len(prep_list):
                return
            if prep_state["next_load"] == 0:
                prep_load(0)
                prep_state["next_load"] = 1
            if prep_state["next_load"] < len(prep_list) and prep_state["next_load"] == i + 1:
                prep_load(i + 1)
                prep_state["next_load"] = i + 2
            prep_cast_store(i)
            prep_state["next_cast"] = i + 1

    def prep_keys(key, rows, cols):
        nr = rows // 128
        ncp = (cols + PW - 1) // PW
        cw_ = (cols + ncp - 1) // ncp
        return [(key, r, c * cw_) for r in range(nr) for c in range(ncp)]

    def finish():
        prep_advance(1000)
        S.fence()
        S.emit()
        return nc, S, dbg

    prep_advance(16)

    WINK = prep_keys("winb", D, WIN)
    winb_v = winb_d.ap().rearrange("(k p) c -> p k c", p=128)
    W1 = 1288
    w1 = A([128, 8, W1], BF16, "w1")
    m0 = A.mark()
    cbc = A([128, 8, 128], F32, "cbc")
    NWST = 6
    wst = [A([128, 2048], F32, "wst%d" % i) for i in range(NWST)]
    S.op("dve", lambda: dve.tensor_copy(cbc[:], condT.unsqueeze(2).to_broadcast([128, 8, 128])), r=["condT"], w=["cbc"])
    pc = 0
    for cg in range(3):
        for k in range(8):
            b = pc % NWST
            qn = "sp" if pc % 2 == 0 else "act"
            pc += 1
            S.dma((lambda b=b, k=k, cg=cg, qn=qn: S.e[qn].dma_start(out=wst[b][:], in_=wada_d.ap()[k * 128:(k + 1) * 128, cg * 2048:(cg + 1) * 2048])),
                  w=[("wst", b)], q=qn)
            st, sp_ = (k == 0), (k == 7)

            def ppmm(colbase, ncols, wofs, b=b, k=k, st=st, sp_=sp_):
                for cc in range(ncols):
                    S.op("pe", (lambda cc=cc: pe.matmul(bank(0, colbase + cc, colbase + cc + 1),
                                                        wst[b][:, wofs + cc * 128: wofs + (cc + 1) * 128],
                                                        condT[:, k:k + 1], start=(st and colbase == 0 and cc == 0), stop=sp_,
                                                        skip_group_check=True)),
                         r=[("wst", b), "condT"], w=[("ps0", colbase + cc)])

            def bcmm(bk, wofs, b=b, k=k, st=st, sp_=sp_):
                for h in range(2):
                    S.op("pe", (lambda h=h: pe.matmul(bank(bk + h), cbc[:, k, :],
                                                      wst[b][:, wofs + h * 512: wofs + (h + 1) * 512], start=st, stop=sp_)),
                         r=[("wst", b), "cbc"], w=[("ps", bk + h)])
            if cg == 0:
                ppmm(0, 16, 0)
            elif cg == 1:
                bcmm(1, 0)
                ppmm(16, 8, 1024)
            else:
                ppmm(24, 8, 0)
                bcmm(3, 1024)
    S.op("dve", lambda: dve.tensor_tensor(modpp, bank(0, 0, 32), bpp[:], ALU.add),
         r=[("ps0", c) for c in range(32)] + ["bpp"], w=["modpp"])
    for h in range(2):
        S.op("dve", (lambda h=h: dve.tensor_tensor(g1bc[:, h * 512:(h + 1) * 512], bank(1 + h), g1bc[:, h * 512:(h + 1) * 512], ALU.add)),
             r=[("ps", 1 + h), "g1bc"], w=["g1bc"])
        S.op("dve", (lambda h=h: dve.tensor_tensor(g2bc[:, h * 512:(h + 1) * 512], bank(3 + h), g2bc[:, h * 512:(h + 1) * 512], ALU.add)),
             r=[("ps", 3 + h), "g2bc"], w=["g2bc"])
    S.op("dve", lambda: dve.scalar_tensor_tensor(s1, modpp[:, 8:16], 1.0, n1w, ALU.add, ALU.mult), r=["modpp", "n1w"], w=["s1"])
    S.op("dve", lambda: dve.scalar_tensor_tensor(s2, modpp[:, 24:32], 1.0, n2w, ALU.add, ALU.mult), r=["modpp", "n2w"], w=["s2"])
    t1 = modpp[:, 0:8]
    t2 = modpp[:, 16:24]
    if not skip1a:
        S.dma(lambda: sp.dma_start(out=w1[:, :, 0:1280], in_=winb_v[:, :, 0:1280]), r=WINK, w=["w1a"])
        S.dma(lambda: sp.dma_start(out=w1[:, :, 1280:1288], in_=winb_v[:, :, C_DT:C_DT + 8]), r=WINK, w=["w1b"])
    if debug:
        d_mod = dbg_out("d_mod", [128, 96])
        S.dma(lambda: sp.dma_start(out=d_mod.ap()[:, 0:80], in_=sv[:, 0:80]), r=["modpp", "s1", "s2", "condT", "cT", "n1w", "n2w"], w=["d_mod"])
        d_g = dbg_out("d_g", [128, 2 * D])
        S.dma(lambda: sp.dma_start(out=d_g.ap()[:, 0:D], in_=g1bc[:]), r=["g1bc"], w=["d_g1"])
        S.dma(lambda: sp.dma_start(out=d_g.ap()[:, D:2 * D], in_=g2bc[:]), r=["g2bc"], w=["d_g2"])
    S.fence()
    A.release(m0)
    if upto < 1:
        prep_advance(1000)
        S.fence()
        S.emit()
        return nc, S, dbg

    def rstd_from_ssq(ssq, n, keyr, keyw):
        S.op("act", lambda: act.activation(ssq, ssq, AF.Ln, bias=epsc[:], scale=1.0 / n), r=[keyr, "epsc"], w=[keyw])
        S.op("act", lambda: act.activation(ssq, ssq, AF.Exp, scale=-0.5), r=[keyw], w=[keyw])

    m1 = A.mark()
    W1K = ["w1a", "w1b"]
    xs = [A([128, D], F32, "xs%d" % i) for i in range(4)]
    sq_junk = A([128, D], BF16, "sqj")
    ssq = A([128, 8], F32, "ssq")
    hT = A([128, 8, TT], BF16, "hT")
    zs = A([128, 4, TT], F32, "zs")
    u = A([128, 6, TT + 3], F32, "u")
    acc = A([128, 6, TT], F32, "acc")
    BTb = A([128, TT], BF16, "BTb")
    CTb = A([128, TT], BF16, "CTb")
    dtb = A([128, 4, 8], F32, "dtb")
    adt4 = A([128, 4, 8], F32, "adt")
    rseg2 = [A([128, 8, 128], F32, "rseg%d" % i) for i in range(2)]
    dec2 = [A([128, 8, 128], F32, "dec%d" % i) for i in range(2)]
    eac4 = A([128, 4, 4, 128], F32, "eac")
    dst4 = A([128, 4, 8], F32, "dst")
    cdec4 = A([128, 4, 8], F32, "cdec")
    xg4 = A([128, 4, 512], BF16, "xg")
    xgd4 = A([128, 4, 512], BF16, "xgd")
    Btok4 = A([128, 4, 128], BF16, "Btok")
    sctm2 = [A([128, 128], F32, "sctm%d" % i) for i in range(2)]
    G4 = A([128, 4, 8, 128], BF16, "G")
    ydg4 = A([128, 4, 512], F32, "ydg")
    prevT = A([128, 8, 64], F32, "prevT")
    prevb = A([128, 512], BF16, "prevb")
    tmpA = A([128, 512], F32, "tmpA")
    ytile = A([128, 4, TT], F32, "ytile")
    ysq = A([128, 4, TT], F32, "ysq")
    rbc = A([128, TT], F32, "rbc")
    yT = A([128, 4, TT], BF16, "yT")
    yl_ssd_v = [t_.ap().rearrange("(c p) t -> p c t", p=128) for t_ in yl_ssd]

    S.op("dve", lambda: dve.memset(u[:], 0.0), w=["u_halo"] + [("u", j) for j in range(6)])
    S.op("dve", lambda: dve.memset(prevT[:], 0.0), w=["prevT"])
    S.op("dve", lambda: dve.memset(prevb[:], 0.0), w=["prevb"])
    if debug and not skip1a:
        d_hT = dbg_out("d_hT", [128, 8, TT], BF16)
        d_xc = dbg_out("d_xc", [128, 6, TT])
        d_yt = dbg_out("d_yt", [128, 4, TT])

    if STOP == "w1":
        return finish()
    HTK = [("hT", c, s) for c in range(8) for s in range(4)]

    def hT_part1(tt):
        for s in range(4):
            r0 = (tt * 4 + s) * 128
            S.dma((lambda s=s, r0=r0: sp.dma_start(out=xs[s][:], in_=x_d.ap()[r0:r0 + 128, :])), w=[("xs", s)])
        for s in range(4):
            S.op("act", (lambda s=s: act.activation(sq_junk[:], xs[s][:], AF.Square, accum_out=ssq[:, s:s + 1])), r=[("xs", s)], w=["sqj", ("ssq", s)])
        for s in range(4):
            S.op("act", (lambda s=s: act.activation(ssq[:, s:s + 1], ssq[:, s:s + 1], AF.Ln, bias=epsc[:], scale=1.0 / D)), r=[("ssq", s), "epsc"], w=[("ssq", s)])
        for s in range(4):
            S.op("act", (lambda s=s: act.activation(ssq[:, s:s + 1], ssq[:, s:s + 1], AF.Exp, scale=-0.5)), r=[("ssq", s)], w=[("ssq", s)])
        for s in range(4):
            S.op("dve", (lambda s=s: dve.tensor_scalar_mul(xs[s][:], xs[s][:], ssq[:, s:s + 1])), r=[("xs", s), ("ssq", s)], w=[("xs", s)])

    def hT_tr(tt, s):
        for half in range(2):
            bk = 4 + (s % 2) * 2 + half
            for c4 in range(4):
                c = half * 4 + c4
                S.op("pe", (lambda c=c, c4=c4, bk=bk: pe.transpose(bank(bk, c4 * 128, (c4 + 1) * 128), xs[s][:, c * 128:(c + 1) * 128], ident)),
                     r=[("xs", s), "cst"], w=[("ps", bk)])

    def hT_ev(tt, s):
        for half in range(2):
            bk = 4 + (s % 2) * 2 + half
            for c4 in range(4):
                c = half * 4 + c4
                S.op("dve", (lambda c=c, c4=c4, bk=bk: dve.tensor_scalar(hT[:, c, s * 128:(s + 1) * 128], bank(bk, c4 * 128, (c4 + 1) * 128),
                                                                      s1[:, c:c + 1], t1[:, c:c + 1], ALU.mult, ALU.add)),
                     r=[("ps", bk), "s1", "modpp"], w=[("hT", c, s)])

    def hT_part2(tt):
        hT_tr(tt, 0); hT_tr(tt, 1); hT_ev(tt, 0); hT_tr(tt, 2); hT_ev(tt, 1); hT_tr(tt, 3); hT_ev(tt, 2); hT_ev(tt, 3)
        S.dma((lambda: sp.dma_start(out=hT_d.ap()[:, :, tt * TT:(tt + 1) * TT], in_=hT[:])), r=HTK, w=[("hTd", tt)], sk=("dma", "hTd"))
        if debug and tt == 1:
            S.dma(lambda: sp.dma_start(out=d_hT.ap(), in_=hT[:]), r=HTK, w=["d_hT"])

    if not skip1a:
        hT_part1(0)
        hT_part2(0)
    for tt in range(0 if skip1a else ntiles):
        prep_advance(6)
        for j in range(10):
            bk = 4 + (j % 4)
            for k in range(8):
                S.op("pe", (lambda j=j, k=k, bk=bk: pe.matmul(bank(bk), w1[:, k, j * 128:(j + 1) * 128], hT[:, k, :], start=(k == 0), stop=(k == 7))),
                     r=W1K + [("hT", k, s) for s in range(4)], w=[("ps", bk)])
            if j < 4:
                S.op("act", (lambda j=j, bk=bk: act.activation(zs[:, j, :], bank(bk), AF.Silu)), r=[("ps", bk)], w=[("zs", j)])
            else:
                jj = j - 4
                S.op("act", (lambda jj=jj, bk=bk: act.activation(acc[:, jj, :], bank(bk), AF.Identity, bias=cbv[:, jj:jj + 1], scale=cw[:, jj, 3:4])),
                     r=[("ps", bk), "cw", "cbv"], w=[("acc", jj)])
                S.op("act", (lambda jj=jj, bk=bk: act.copy(u[:, jj, 3:TT + 3], bank(bk))), r=[("ps", bk), "u_halo"], w=[("u", jj)])
                for kk in (2, 1, 0):
                    S.op("dve", (lambda jj=jj, kk=kk: dve.scalar_tensor_tensor(acc[:, jj, :], u[:, jj, kk:kk + TT], cw[:, jj, kk:kk + 1], acc[:, jj, :], ALU.mult, ALU.add)),
                         r=[("u", jj), ("acc", jj), "cw"], w=[("acc", jj)])
                if jj < 4:
                    S.op("act", (lambda jj=jj: act.activation(acc[:, jj, :], acc[:, jj, :], AF.Silu)), r=[("acc", jj)], w=[("acc", jj)])
                else:
                    S.op("act", (lambda jj=jj: act.activation(acc[:, jj, :], acc[:, jj, :], AF.Silu)), r=[("acc", jj)], w=[("acc", jj)])
                    tb = BTb if jj == 4 else CTb
                    S.op("dve", (lambda jj=jj, tb=tb: dve.tensor_copy(tb[:], acc[:, jj, :])), r=[("acc", jj)], w=["BTb" if jj == 4 else "CTb"])
        if STOP == "inproj":
            return finish()
        S.op("dve", lambda: dve.tensor_copy(u[:, :, 0:3], u[:, :, TT:TT + 3]), r=[("u", j) for j in range(6)], w=["u_halo"] + [("u", j) for j in range(6)])
        if debug and tt == 1:
            S.dma(lambda: sp.dma_start(out=d_xc.ap(), in_=acc[:]), r=[("acc", j) for j in range(6)], w=["d_xc"])
        for s in range(4):
            for k in range(8):
                S.op("pe", (lambda s=s, k=k: pe.matmul(bank(1, 264 + s * 8, 272 + s * 8), hT[:, k, s * 128:(s + 1) * 128], w1[:, k, 1280:1288], start=(k == 0), stop=(k == 7))),
                     r=W1K + [("hT", k, s)], w=[("ps", 1)])
        S.op("dve", lambda: dve.tensor_tensor(dtb[:], bank(1, 264, 296).rearrange("p (s h) -> p s h", s=4),
                                              v8[:, 0, :].unsqueeze(1).to_broadcast([128, 4, 8]), ALU.add),
             r=[("ps", 1), ("v8", 0)], w=["dtb"])
        S.op("act", lambda: act.activation(dtb[:], dtb[:], AF.Exp), r=["dtb"], w=["dtb"])
        S.op("act", lambda: act.activation(dtb[:], dtb[:], AF.Ln, bias=1.0), r=["dtb"], w=["dtb"])
        if STOP == "dt":
            return finish()
        def stA(ci):
            c0, c1 = ci * 128, (ci + 1) * 128
            dtc = dtb[:, ci, :]
            for j in range(4):
                S.op("pe", (lambda j=j: pe.transpose(bank(0, j * 128, (j + 1) * 128), acc[:, j, c0:c1], ident)),
                     r=[("acc", j), "cst"], w=[("ps", 0)])
            S.op("pe", (lambda: pe.transpose(bank(1, 0, 128), acc[:, 4, c0:c1], ident)), r=[("acc", 4), "cst"], w=[("ps", 1)])
            S.op("dve", (lambda: dve.tensor_tensor(xg4[:, ci, :].rearrange("p (h e) -> p h e", h=8), bank(0).rearrange("p (h e) -> p h e", h=8),
                                                   dtc.unsqueeze(2).to_broadcast([128, 8, 64]), ALU.mult)),
                 r=[("ps", 0), "dtb"], w=[("xg", ci)])
            S.op("act", (lambda: act.copy(Btok4[:, ci, :], bank(1, 0, 128))), r=[("ps", 1)], w=[("Btok", ci)])

        def stB1(ci):
            dtc = dtb[:, ci, :]
            rs = rseg2[ci % 2]
            S.op("dve", (lambda: dve.tensor_tensor(adt4[:, ci, :], dtc, Aneg[:], ALU.mult)), r=["dtb", "Aneg"], w=[("adt", ci)])
            S.op("dve", (lambda: dve.tensor_tensor(rs[:], UI.unsqueeze(1).to_broadcast([128, 8, 128]),
                                                   adt4[:, ci, :].unsqueeze(2).to_broadcast([128, 8, 128]), ALU.mult)),
                 r=[("adt", ci), "cst"], w=[("rseg", ci % 2)])

        def stB2(ci):
            rs = rseg2[ci % 2]
            dc = dec2[ci % 2]
            for hf in range(2):
                rv = rs[:, hf * 4:(hf + 1) * 4, :].rearrange("p h l -> p (h l)")
                S.op("pe", (lambda hf=hf, rv=rv: pe.matmul(bank(2 + hf), LS, rv, start=True, stop=True)), r=[("rseg", ci % 2), "cst"], w=[("ps", 2 + hf)])
                S.op("pe", (lambda hf=hf, rv=rv: pe.matmul(bank(4 + hf), ONESF, rv, start=True, stop=True)), r=[("rseg", ci % 2), "cst"], w=[("ps", 4 + hf)])
            S.op("pe", (lambda: pe.matmul(bank(1, 256, 264), LS, adt4[:, ci, :], start=True, stop=True)), r=[("adt", ci), "cst"], w=[("ps", 1)])
            S.op("act", (lambda: act.activation(dst4[:, ci, :], bank(1, 256, 264), AF.Exp)), r=[("ps", 1)], w=[("dst", ci)])
            for hf in range(2):
                S.op("act", (lambda hf=hf: act.activation(dc[:, hf * 4:(hf + 1) * 4, :].rearrange("p h l -> p (h l)"), bank(2 + hf), AF.Exp)),
                     r=[("ps", 2 + hf)], w=[("dec", ci % 2, hf)])
            acb = ps[:, 4 * 512: 6 * 512].rearrange("p (pr two l) -> p pr two l", pr=4, two=2)
            S.op("act", (lambda: act.activation(eac4[0:64, ci, :, :], acb[0:64, :, 0, :], AF.Exp)), r=[("ps", 4), ("ps", 5)], w=[("eac", ci, 0)])
            S.op("act", (lambda: act.activation(eac4[64:128, ci, :, :], acb[64:128, :, 1, :], AF.Exp)), r=[("ps", 4), ("ps", 5)], w=[("eac", ci, 1)])
            acl = ps[:, 4 * 512: 6 * 512].rearrange("p (h l) -> p h l", h=8)
            S.op("act", (lambda: act.activation(cdec4[:, ci, :], acl[:, :, 127], AF.Exp)), r=[("ps", 4), ("ps", 5)], w=[("cdec", ci)])

        def stC(ci):
            c0, c1 = ci * 128, (ci + 1) * 128
            dc = dec2[ci % 2]
            sm = sctm2[ci % 2]
            S.op("pe", (lambda: pe.matmul(bank(1, 128, 256), BTb[:, c0:c1], CTb[:, c0:c1], start=True, stop=True)),
                 r=["BTb", "CTb"], w=[("ps", 1)])
            S.op("dve", (lambda: dve.tensor_tensor(sm[:], bank(1, 128, 256), UI, ALU.mult)), r=[("ps", 1), "cst"], w=[("sctm", ci % 2)])
            S.op("dve", (lambda: dve.tensor_tensor(xgd4[:, ci, :].rearrange("p (h e) -> p h e", h=8), xg4[:, ci, :].rearrange("p (h e) -> p h e", h=8),
                                                   dst4[:, ci, :].unsqueeze(2).to_broadcast([128, 8, 64]), ALU.mult)),
                 r=[("xg", ci), ("dst", ci)], w=[("xgd", ci)])
            S.op("dve", (lambda: dve.tensor_tensor(G4[:, ci, :, :], dc[:], sm[:].unsqueeze(1).to_broadcast([128, 8, 128]), ALU.mult)),
                 r=[("dec", ci % 2, 0), ("dec", ci % 2, 1), ("sctm", ci % 2)], w=[("G", ci)])

        def stD(ci):
            for h in range(8):
                pr, hh = h // 2, h % 2
                S.op("pe", (lambda h=h, pr=pr, hh=hh: pe.matmul(ps[hh * 64:(hh + 1) * 64, 6 * 512 + pr * 128: 6 * 512 + (pr + 1) * 128],
                                                               xg4[:, ci, h * 64:(h + 1) * 64], G4[:, ci, h, :], start=True, stop=True)),
                     r=[("xg", ci), ("G", ci)], w=[("ps", 6)])
            S.op("act", (lambda: act.copy(ydg4[:, ci, :], bank(6))), r=[("ps", 6)], w=[("ydg", ci)])

        def stII(ci):
            c0, c1 = ci * 128, (ci + 1) * 128
            for pr in range(4):
                S.op("pe", (lambda pr=pr: pe.matmul(bank(7, pr * 128, (pr + 1) * 128), prevb[:, pr * 128:(pr + 1) * 128], CTb[:, c0:c1], start=True, stop=True)),
                     r=["prevb", "CTb"], w=[("ps", 7)])
            S.op("dve", (lambda: dve.tensor_tensor(tmpA[:], bank(7), eac4[:, ci, :, :].rearrange("p a l -> p (a l)"), ALU.mult)),
                 r=[("ps", 7), ("eac", ci, 0), ("eac", ci, 1)], w=["tmpA"])
            S.op("pe", (lambda: pe.matmul(bank(7), Btok4[:, ci, :], xgd4[:, ci, :], start=True, stop=True)), r=[("Btok", ci), ("xgd", ci)], w=[("ps", 7)])
            S.op("dve", (lambda: dve.tensor_tensor(prevT[:], prevT[:], cdec4[:, ci, :].unsqueeze(2).to_broadcast([128, 8, 64]), ALU.mult)),
                 r=["prevT", ("cdec", ci)], w=["prevT"])
            S.op("dve", (lambda: dve.tensor_tensor(ytile[:, :, c0:c1], ydg4[:, ci, :].rearrange("p (a l) -> p a l", a=4),
                                                   tmpA[:].rearrange("p (a l) -> p a l", a=4), ALU.add)),
                 r=[("ydg", ci), "tmpA"], w=[("ytile", ci)])
            S.op("dve", (lambda: dve.tensor_tensor(prevT[:], prevT[:], bank(7).rearrange("p (h e) -> p h e", h=8), ALU.add)),
                 r=["prevT", ("ps", 7)], w=["prevT"])
            S.op("dve", (lambda: dve.tensor_copy(prevb[:], prevT[:].rearrange("p h e -> p (h e)"))), r=["prevT"], w=["prevb"])

        for ci in range(4):
            stA(ci)
        stB1(0); stB1(1); stB2(0); stB1(2); stB2(1); stC(0); stB1(3); stB2(2); stC(1); stB2(3); stC(2); stC(3)
        for ci in range(4):
            stD(ci)
        for ci in range(4):
            stII(ci)
        if STOP == "chunks":
            return finish()
        if tt + 1 < ntiles:
            hT_part1(tt + 1)
        YK = [("ytile", ci) for ci in range(4)]
        S.op("dve", lambda: dve.tensor_tensor(ysq[:], acc[:, 0:4, :], dsk[:].unsqueeze(2).to_broadcast([128, 4, TT]), ALU.mult),
             r=[("acc", j) for j in range(4)] + ["dsk"], w=["ysq"])
        S.op("dve", lambda: dve.tensor_tensor(ytile[:], ytile[:], ysq[:], ALU.add), r=YK + ["ysq"], w=YK)
        S.op("dve", lambda: dve.tensor_tensor(ytile[:], ytile[:], zs[:], ALU.mult), r=YK + [("zs", j) for j in range(4)], w=YK)
        if debug and tt == 1:
            S.dma(lambda: sp.dma_start(out=d_yt.ap(), in_=ytile[:]), r=YK, w=["d_yt"])
        S.op("act", lambda: act.activation(ysq[:], ytile[:], AF.Square), r=YK, w=["ysq"])
        for a in range(4):
            S.op("pe", (lambda a=a: pe.matmul(bank(2), ONESF, ysq[:, a, :], start=(a == 0), stop=(a == 3))), r=["ysq", "cst"], w=[("ps", 2)])
        S.op("act", lambda: act.activation(rbc[:], bank(2), AF.Ln, bias=epsc[:], scale=1.0 / 512), r=[("ps", 2), "epsc"], w=["rbc"])
        S.op("act", lambda: act.activation(rbc[:], rbc[:], AF.Exp, scale=-0.5), r=["rbc"], w=["rbc"])
        S.op("dve", lambda: dve.tensor_tensor(ytile[:], ytile[:], rbc[:].unsqueeze(1).to_broadcast([128, 4, TT]), ALU.mult), r=YK + ["rbc"], w=YK)
        S.op("dve", lambda: dve.tensor_tensor(yT[:], ytile[:], snw[:].unsqueeze(2).to_broadcast([128, 4, TT]), ALU.mult), r=YK + ["snw"], w=["yT"])
        S.dma((lambda tt=tt: sp.dma_start(out=yl_ssd_v[tt // 4][:, :, (tt % 4) * TT:(tt % 4 + 1) * TT], in_=yT[:])), r=["yT"], w=[("yl_ssd", tt)], sk=("dma", "yT"))
        if tt + 1 < ntiles:
            hT_part2(tt + 1)
        if tt == 3:
            prep_advance(1000)
            gather(yl_ssd, ya_ssd, 0, "yl_ssd", 0)

    prep_advance(1000)
    if debug and not skip1a:
        d_yssd = dbg_out("d_yssd", [512, SEQ], BF16)
        for hf_ in range(2):
            nt_ = min(ntiles, hf_ * 4 + 4) - hf_ * 4
            if nt_ > 0:
                S.dma((lambda hf_=hf_, nt_=nt_: sp.dma_start(out=d_yssd.ap()[:, hf_ * HS: hf_ * HS + nt_ * TT], in_=yl_ssd[hf_].ap()[:, 0:nt_ * TT])),
                      r=[("yl_ssd", t) for t in range(ntiles)], w=[("d_yssd", hf_)])
    S.fence()
    A.release(m1)
    if upto < 2:
        S.emit()
        return nc, S, dbg
    if not skip1a:
        gather(yl_ssd, ya_ssd, 1, "yl_ssd", 1)

    m2 = A.mark()
    W2 = 1536
    w2 = A([128, 8, W2], BF16, "w2")
    S.dma(lambda: sp.dma_start(out=w2[:], in_=winb_v[:, :, C_Q:C_Q + W2]), r=WINK, w=["w2"])
    kT = A([128, 4, SEQ], BF16, "kT")
    vb = A([128, 32, 512], BF16, "vb")
    hT2 = [A([128, 8, TT], BF16, "hT2_%d" % i) for i in range(2)]
    qT = A([128, 4, TT], BF16, "qT")
    sqf = [A([128, TT], F32, "sqf%d" % i) for i in range(2)]
    rq = [A([128, TT], F32, "rq%d" % i) for i in range(2)]
    ebuf = [A([128, 512], F32, "e%d" % i) for i in range(2)]
    spb2 = [[A([128, 512], BF16, "sp%d_%d" % (i, q)) for q in range(2)] for i in range(2)]
    spm = [A([128, 512], BF16, "spm%d" % i) for i in range(2)]
    wb2 = [[A([128, 512], BF16, "w%d_%d" % (i, q)) for q in range(2)] for i in range(2)]
    wm = [A([128, 512], BF16, "wm%d" % i) for i in range(2)]
    cbb = [A([128, 512], BF16, "cb%d" % i) for i in range(2)]
    ysb = A([128, 4, TT], BF16, "ysb")
    yl_sb_v = [t_.ap().rearrange("(c p) t -> p c t", p=128) for t_ in yl_sb]
    US4 = US_b.unsqueeze(1).to_broadcast([128, 4, 128])
    ZBP = ((0, 1), (2, 3))
    CBK = (4, 5)
    OB = 6
    if debug and not only3:
        d_q = dbg_out("d_q", [128, 4, TT], BF16)

    for tt in range(0 if only3 else ntiles):
        hb = hT2[tt % 2]
        kh = ("hT2", tt % 2)

        def load_h(t_):
            S.dma((lambda: sp.dma_start(out=hT2[t_ % 2][:], in_=hT_d.ap()[:, :, t_ * TT:(t_ + 1) * TT])), r=[("hTd", t_)], w=[("hT2", t_ % 2)])
        if tt == 0:
            load_h(0)
        rot = [7, 6, 5]
        ri = 0
        for j in range(8):
            isq = j < 4
            c = j % 4
            bq = rot[ri % 3]
            bs = rot[(ri + 1) % 3]
            ri += 2
            col0 = (0 if isq else 512) + c * 128
            for k in range(8):
                S.op("pe", (lambda k=k, bq=bq, col0=col0: pe.matmul(bank(bq), w2[:, k, col0:col0 + 128], hb[:, k, :], start=(k == 0), stop=(k == 7))),
                     r=["w2", kh], w=[("ps", bq)])
            sb_ = sqf[j % 2]
            rb_ = rq[j % 2]
            S.op("act", (lambda bq=bq, sb_=sb_: act.activation(sb_[:], bank(bq), AF.Square)), r=[("ps", bq)], w=[("sqf", j % 2)])
            S.op("pe", (lambda bs=bs, sb_=sb_: pe.matmul(bank(bs), BD, sb_[:], start=True, stop=True)), r=[("sqf", j % 2), "cst"], w=[("ps", bs)])
            S.op("act", (lambda bs=bs, rb_=rb_: act.activation(rb_[:], bank(bs), AF.Ln, bias=epsc[:], scale=1.0 / 64)), r=[("ps", bs), "epsc"], w=[("rq", j % 2)])
            S.op("act", (lambda rb_=rb_: act.activation(rb_[:], rb_[:], AF.Exp, scale=-0.5)), r=[("rq", j % 2)], w=[("rq", j % 2)])
            if isq:
                S.op("dve", (lambda c=c, bq=bq, rb_=rb_: dve.scalar_tensor_tensor(qT[:, c, :], bank(bq), qkw[:, 0:1], rb_[:], ALU.mult, ALU.mult)),
                     r=[("ps", bq), ("rq", j % 2), "qkw"], w=[("qT", c)])
            else:
                S.op("dve", (lambda c=c, bq=bq, rb_=rb_, tt=tt: dve.scalar_tensor_tensor(kT[:, c, tt * TT:(tt + 1) * TT], bank(bq), qkw[:, 1:2], rb_[:], ALU.mult, ALU.mult)),
                     r=[("ps", bq), ("rq", j % 2), "qkw"], w=[("kT", c, tt)])
        for s in range(4):
            bq = rot[ri % 3]
            ri += 1
            for k in range(8):
                S.op("pe", (lambda k=k, bq=bq, s=s: pe.matmul(bank(bq), hb[:, k, s * 128:(s + 1) * 128], w2[:, k, 1024:1536], start=(k == 0), stop=(k == 7))),
                     r=["w2", kh], w=[("ps", bq)])
            S.op("dve", (lambda bq=bq, s=s, tt=tt: dve.tensor_copy(vb[:, tt * 4 + s, :], bank(bq))), r=[("ps", bq)], w=[("vb", tt * 4 + s)])
        if debug and tt == 1:
            S.dma(lambda: sp.dma_start(out=d_q.ap(), in_=qT[:]), r=[("qT", c) for c in range(4)], w=["d_q"])

        if tt + 1 < ntiles:
            load_h(tt + 1)
        if STOP == "qkv":
            return finish()
        steps = [("d", d) for d in range(4)] + [("o", j) for j in range(4 * tt - 1, -1, -1)]
        ns = len(steps)
        for c in range(4):
            def lo_of(n):
                kind, v = steps[n]
                return v * 128 if kind == "d" else 0

            def zmm(n, hh, c=c, tt=tt):
                kind, v = steps[n]
                p0, p1 = hh * 64, (hh + 1) * 64
                zb = ZBP[hh][n % 2]
                if kind == "d":
                    for a in range(v, 4):
                        j = 4 * tt + a - v
                        S.op("pe", (lambda a=a, j=j, v=v: pe.matmul(bank(zb, a * 128, (a + 1) * 128), kT[p0:p1, c, j * 128:(j + 1) * 128],
                                                                qT[p0:p1, c, a * 128:(a + 1) * 128], start=(a == v), stop=False, skip_group_check=True)),
                             r=[("kT", c, j // 4), ("qT", c)], w=[("ps", zb)])
                else:
                    j = v
                    S.op("pe", (lambda j=j: pe.matmul(bank(zb), kT[p0:p1, c, j * 128:(j + 1) * 128], qT[p0:p1, c, :], start=True, stop=False, skip_group_check=True)),
                         r=[("kT", c, j // 4), ("qT", c)], w=[("ps", zb)])

            def act_e(n, hh):
                lo = lo_of(n)
                zb = ZBP[hh][n % 2]
                S.op("act", (lambda: act.activation(ebuf[hh][:, lo:512], bank(zb, lo, 512), AF.Exp)), r=[("ps", zb)], w=[("e", hh)])

            def act_sp(n, hh):
                lo = lo_of(n)
                pq = n % 2
                spt = spb2[hh][pq]
                S.op("act", (lambda: act.activation(spt[:, lo:512], ebuf[hh][:, lo:512], AF.Ln, bias=1.0)), r=[("e", hh)], w=[("sp", hh, pq)])
                if n == 0:
                    S.op("dve", (lambda: dve.tensor_tensor(spm[hh][:].rearrange("p (a l) -> p a l", a=4), spt[:].rearrange("p (a l) -> p a l", a=4), US4, ALU.mult)),
                         r=[("sp", hh, pq)] + CB, w=[("spm", hh)])

            def pe_tio(n, hh):
                lo = lo_of(n)
                pq = n % 2
                zb = ZBP[hh][pq]
                first = (n == 0)
                src = spm[hh] if first else spb2[hh][pq]
                skey = ("spm", hh) if first else ("sp", hh, pq)
                S.op("pe", (lambda: pe.matmul(bank(zb, lo, 512), negLI_b, src[:, lo:512], start=False, stop=first, skip_group_check=True)),
                     r=[skey] + CB, w=[("ps", zb)])
                if not first:
                    S.op("pe", (lambda: pe.matmul(bank(zb, lo, 512), negI_b, cbb[hh][:, lo:512], start=False, stop=True, skip_group_check=True)),
                         r=[("cb", hh)] + CB, w=[("ps", zb)])
                if n < ns - 1:
                    S.op("pe", (lambda: pe.matmul(bank(CBK[hh], lo, 512), ones_b, src[:, lo:512], start=first, stop=False, skip_group_check=True)),
                         r=[skey] + CB, w=[("ps", CBK[hh])])
                    S.op("dve", (lambda: dve.tensor_copy(cbb[hh][:], bank(CBK[hh]))), r=[("ps", CBK[hh])], w=[("cb", hh)])

            def act_w(n, hh):
                lo = lo_of(n)
                pq = n % 2
                zb = ZBP[hh][pq]
                wt = wb2[hh][pq]
                S.op("act", (lambda: act.activation(wt[:, lo:512], bank(zb, lo, 512), AF.Exp)), r=[("ps", zb)], w=[("w", hh, pq)])
                if n == 0:
                    S.op("dve", (lambda: dve.tensor_tensor(wm[hh][:].rearrange("p (a l) -> p a l", a=4), wt[:].rearrange("p (a l) -> p a l", a=4), US4, ALU.mult)),
                         r=[("w", hh, pq)] + CB, w=[("wm", hh)])

            def pe_pv(n, hh):
                kind, v = steps[n]
                pq = n % 2
                first = (n == 0)
                last = (n == ns - 1)
                p0, p1 = hh * 64, (hh + 1) * 64
                vc0 = c * 128 + hh * 64
                wsrc = wm[hh] if first else wb2[hh][pq]
                wkey = ("wm", hh) if first else ("w", hh, pq)
                if kind == "d":
                    for a in range(v, 4):
                        j = 4 * tt + a - v
                        S.op("pe", (lambda a=a, j=j: pe.matmul(ps[p0:p1, OB * 512 + a * 128: OB * 512 + (a + 1) * 128], vb[:, j, vc0:vc0 + 64],
                                                                wsrc[:, a * 128:(a + 1) * 128], start=(first and a == 0), stop=False, skip_group_check=True)),
                             r=[("vb", j), wkey], w=[("ps", OB)])
                else:
                    j = v
                    S.op("pe", (lambda j=j: pe.matmul(ps[p0:p1, OB * 512: OB * 512 + 512], vb[:, j, vc0:vc0 + 64], wsrc[:, :], start=False, stop=last,
                                                      skip_group_check=True)),
                         r=[("vb", j), wkey], w=[("ps", OB)])

            for n0 in range(min(2, ns)):
                for hh in range(2):
                    zmm(n0, hh)
            for hh in range(2):
                act_e(0, hh)
            for hh in range(2):
                act_sp(0, hh)
                pe_tio(0, hh)
            for n in range(ns):
                for hh in range(2):
                    if n + 1 < ns:
                        act_e(n + 1, hh)
                    act_w(n, hh)
                    if n + 1 < ns:
                        act_sp(n + 1, hh)
                        pe_tio(n + 1, hh)
                    if n + 2 < ns:
                        zmm(n + 2, hh)
                for hh in range(2):
                    pe_pv(n, hh)
                if STOP == "step0":
                    return finish()
            S.op("dve", (lambda c=c: dve.tensor_copy(ysb[:, c, :], bank(OB))), r=[("ps", OB)], w=[("ysb", c)])
            if STOP == "chunk0":
                return finish()
        S.dma((lambda tt=tt: sp.dma_start(out=yl_sb_v[tt // 4][:, :, (tt % 4) * TT:(tt % 4 + 1) * TT], in_=ysb[:])), r=[("ysb", c) for c in range(4)], w=[("yl_sb", tt)], sk=("dma", "ysb"))
        if tt == 3:
            gather(yl_sb, ya_sb, 0, "yl_sb", 2)

    if debug and not only3:
        d_ysb = dbg_out("d_ysb", [512, SEQ], BF16)
        for hf_ in range(2):
            nt_ = min(ntiles, hf_ * 4 + 4) - hf_ * 4
            if nt_ > 0:
                S.dma((lambda hf_=hf_, nt_=nt_: sp.dma_start(out=d_ysb.ap()[:, hf_ * HS: hf_ * HS + nt_ * TT], in_=yl_sb[hf_].ap()[:, 0:nt_ * TT])),
                      r=[("yl_sb", t) for t in range(ntiles)], w=[("d_ysb", hf_)])
    S.fence()
    A.release(mP)
    if upto < 3:
        S.emit()
        return nc, S, dbg
    if not only3:
        gather(yl_sb, ya_sb, 1, "yl_sb", 3)

    ya_ssd_v = [t_.ap().rearrange("(c p) t -> p c t", p=128) for t_ in ya_ssd]
    ya_sb_v = [t_.ap().rearrange("(c p) t -> p c t", p=128) for t_ in ya_sb]
    woutb_v = woutb_d.ap().rearrange("(k p) c -> p k c", p=128)
    wgb_v = wgb_d.ap().rearrange("(k p) c -> p k c", p=128)
    wub_v = wub_d.ap().rearrange("(k p) c -> p k c", p=128)
    wdb_v = wdb_d.ap().rearrange("(k p) c -> p k c", p=128)
    WOK = prep_keys("woutb", 2 * D, D)
    WGK = prep_keys("wgb", D, DFF)
    WUK = prep_keys("wub", D, DFF)
    WDK = prep_keys("wdb", DFF, D)
    yh = [A([128, 16, TT], BF16, "yh%d" % i) for i in range(2)]
    xt = A([128, 4, D], F32, "xt")
    x1 = A([128, 4, D], F32, "x1")
    xn = [A([128, D], F32, "xn%d" % i) for i in range(2)]
    h2T = A([128, 8, TT], BF16, "h2T")
    actT = A([128, NCH, TT], BF16, "actT")
    sg = [A([128, TT], F32, "sg%d" % i) for i in range(2)]
    tmpo = [A([128, 512], F32, "tmpo%d" % i) for i in range(2)]
    ob = [A([128, 512], F32, "ob%d" % i) for i in range(4)]
    ssq3 = A([128, 4], F32, "ssq3")
    sqj3 = A([128, D], BF16, "sqj3")
    NRING = 6
    ring = [A([128, 4096], BF16, "ring%d" % i) for i in range(NRING)]
    ring_i = [0]

    def ring_load(fn_src, rkeys):
        i = ring_i[0] % NRING
        ring_i[0] += 1
        t = ring[i]
        o, i_ = fn_src(t)
        S.dma((lambda o=o, i_=i_: sp.dma_start(out=o, in_=i_)), r=rkeys, w=[("ring", i)])
        return t, ("ring", i)

    obi = 0
    def load_inputs(ut):
        for hlf in range(2):
            t0 = ut * TT
            S.dma((lambda hlf=hlf, t0=t0: sp.dma_start(out=yh[hlf][:, 0:8, :], in_=ya_ssd_v[hlf][:, :, t0:t0 + TT])), r=[("ya", "yl_ssd", hlf)], w=[("yh", hlf, 0)])
            S.dma((lambda hlf=hlf, t0=t0: sp.dma_start(out=yh[hlf][:, 8:16, :], in_=ya_sb_v[hlf][:, :, t0:t0 + TT])), r=[("ya", "yl_sb", hlf)], w=[("yh", hlf, 1)])
        for q4 in range(4):
            pt = q4 // 2
            v0 = yh[0][:, q4 * 4:(q4 + 1) * 4, :].rearrange("p c t -> p (c t)")
            v1 = yh[1][:, q4 * 4:(q4 + 1) * 4, :].rearrange("p c t -> p (c t)")
            S.op("dve", (lambda v0=v0: dve.tensor_scalar_mul(v0, v0, flags[:, 0:1])), r=[("yh", 0, pt), "flags"], w=[("yh", 0, pt)])
            S.op("dve", (lambda v0=v0, v1=v1: dve.scalar_tensor_tensor(v0, v1, flags[:, 1:2], v0, ALU.mult, ALU.add)),
                 r=[("yh", 0, pt), ("yh", 1, pt), "flags"], w=[("yh", 0, pt)])
        S.dma((lambda ut=ut: sp.dma_start(out=xt[:], in_=xh_d.ap()[ut * TT:(ut + 1) * TT, :].rearrange("(s p) d -> p s d", p=128))), w=["xt"])

    load_inputs(0)
    for ut in range(4):
        for g4 in range(4):
            wt, wkey = ring_load(lambda t, g4=g4: (t[:].rearrange("p (k c) -> p k c", k=4), woutb_v[:, g4 * 4:(g4 + 1) * 4, :]), WOK)
            wv = wt[:].rearrange("p (k c) -> p k c", k=4)
            for kk in range(4):
                kc = g4 * 4 + kk
                for s in range(4):
                    for hf in range(2):
                        S.op("pe", (lambda kc=kc, kk=kk, s=s, hf=hf, wv=wv: pe.matmul(bank(s * 2 + hf), yh[0][:, kc, s * 128:(s + 1) * 128], wv[:, kk, hf * 512:(hf + 1) * 512],
                                                                                      start=(kc == 0), stop=(kc == 15))),
                             r=[("yh", 0, 0), ("yh", 0, 1), wkey], w=[("ps", s * 2 + hf)])
        for s in range(4):
            for hf in range(2):
                tb = tmpo[(s * 2 + hf) % 2]
                tk = ("tmpo", (s * 2 + hf) % 2)
                S.op("dve", (lambda s=s, hf=hf, tb=tb: dve.tensor_tensor(tb[:], bank(s * 2 + hf), g1bc[:, hf * 512:(hf + 1) * 512], ALU.mult)),
                     r=[("ps", s * 2 + hf), "g1bc"], w=[tk])
                S.op("dve", (lambda s=s, hf=hf, tb=tb: dve.tensor_tensor(x1[:, s, hf * 512:(hf + 1) * 512], tb[:], xt[:, s, hf * 512:(hf + 1) * 512], ALU.add)),
                     r=[tk, "xt"], w=[("x1", s, hf)])
        for s in range(4):
            sc = ssq3[:, s:s + 1]
            S.op("act", (lambda s=s, sc=sc: act.activation(sqj3[:], x1[:, s, :], AF.Square, accum_out=sc)), r=[("x1", s, 0), ("x1", s, 1)], w=["sqj3", ("ssq3", s)])
            rstd_from_ssq(sc, D, ("ssq3", s), ("ssq3", s))
            xb = xn[s % 2]
            kx = ("xn", s % 2)
            S.op("dve", (lambda s=s, sc=sc, xb=xb: dve.tensor_scalar_mul(xb[:], x1[:, s, :], sc)), r=[("x1", s, 0), ("x1", s, 1), ("ssq3", s)], w=[kx])
            for c in range(8):
                S.op("pe", (lambda c=c, s=s, xb=xb: pe.transpose(bank(c, s * 128, (s + 1) * 128), xb[:, c * 128:(c + 1) * 128], ident)),
                     r=[kx, "cst"], w=[("ps", c)])
        for c in range(8):
            S.op("dve", (lambda c=c: dve.tensor_scalar(h2T[:, c, :], bank(c), s2[:, c:c + 1], t2[:, c:c + 1], ALU.mult, ALU.add)),
                 r=[("ps", c), "s2", "modpp"], w=[("h2T", c)])
        H2K = [("h2T", c) for c in range(8)]
        if ut + 1 < 4:
            load_inputs(ut + 1)
        fi = 0
        for fg in range(6):
            nf = 4 if fg < 5 else 2
            ncol = nf * 128
            gt, gkey = ring_load(lambda t, fg=fg, ncol=ncol: (t[:].rearrange("p (k c) -> p k c", k=8)[:, :, 0:ncol], wgb_v[:, :, fg * 512: fg * 512 + ncol]), WGK)
            utile, ukey = ring_load(lambda t, fg=fg, ncol=ncol: (t[:].rearrange("p (k c) -> p k c", k=8)[:, :, 0:ncol], wub_v[:, :, fg * 512: fg * 512 + ncol]), WUK)
            gv = gt[:].rearrange("p (k c) -> p k c", k=8)
            uv = utile[:].rearrange("p (k c) -> p k c", k=8)
            for f in range(nf):
                bg_ = (fi % 4) * 2
                bu_ = bg_ + 1
                for k in range(8):
                    S.op("pe", (lambda k=k, f=f, bg_=bg_, gv=gv: pe.matmul(bank(bg_), gv[:, k, f * 128:(f + 1) * 128], h2T[:, k, :], start=(k == 0), stop=(k == 7))),
                         r=H2K + [gkey], w=[("ps", bg_)])
                for k in range(8):
                    S.op("pe", (lambda k=k, f=f, bu_=bu_, uv=uv: pe.matmul(bank(bu_), uv[:, k, f * 128:(f + 1) * 128], h2T[:, k, :], start=(k == 0), stop=(k == 7))),
                         r=H2K + [ukey], w=[("ps", bu_)])
                sgb = sg[fi % 2]
                S.op("act", (lambda bg_=bg_, sgb=sgb: act.activation(sgb[:], bank(bg_), AF.Silu)), r=[("ps", bg_)], w=[("sg", fi % 2)])
                fch = fg * 4 + f
                S.op("dve", (lambda bu_=bu_, sgb=sgb, fch=fch: dve.tensor_tensor(actT[:, fch, :], bank(bu_), sgb[:], ALU.mult)),
                     r=[("ps", bu_), ("sg", fi % 2)], w=[("actT", fch)])
                fi += 1
        for g in range(6):
            nk = 4 if g < 5 else 2
            wt, wkey = ring_load(lambda t, g=g, nk=nk: (t[:].rearrange("p (k c) -> p k c", k=4)[:, 0:nk, :], wdb_v[:, g * 4: g * 4 + nk, :]), WDK)
            wv = wt[:].rearrange("p (k c) -> p k c", k=4)
            for kk in range(nk):
                kc = g * 4 + kk
                for s in range(4):
                    for hf in range(2):
                        S.op("pe", (lambda kc=kc, kk=kk, s=s, hf=hf, wv=wv: pe.matmul(bank(s * 2 + hf), actT[:, kc, s * 128:(s + 1) * 128], wv[:, kk, hf * 512:(hf + 1) * 512],
                                                                                      start=(kc == 0), stop=(kc == NCH - 1))),
                             r=[("actT", kc), wkey], w=[("ps", s * 2 + hf)])
        for s in range(4):
            for hf in range(2):
                tb = tmpo[(s * 2 + hf) % 2]
                tk = ("tmpo", (s * 2 + hf) % 2)
                o = ob[obi % 4]
                ok = ("ob", obi % 4)
                obi += 1
                S.op("dve", (lambda s=s, hf=hf, tb=tb: dve.tensor_tensor(tb[:], bank(s * 2 + hf), g2bc[:, hf * 512:(hf + 1) * 512], ALU.mult)),
                     r=[("ps", s * 2 + hf), "g2bc"], w=[tk])
                S.op("dve", (lambda s=s, hf=hf, tb=tb, o=o: dve.tensor_tensor(o[:], tb[:], x1[:, s, hf * 512:(hf + 1) * 512], ALU.add)),
                     r=[tk, ("x1", s, hf)], w=[ok])
                r0 = ut * TT + s * 128
                S.dma((lambda o=o, r0=r0, hf=hf: sp.dma_start(out=out_d.ap()[r0:r0 + 128, hf * 512:(hf + 1) * 512], in_=o[:])), r=[ok], w=[("out", ut, s, hf)], sk=("dma",) + ok)
    S.fence()
    S.emit()
    return nc, S, dbg


def _consts():
    i = np.arange(128)[:, None]
    j = np.arange(128)[None, :]
    c = np.zeros((128, 7, 128), np.float32)
    c[:, 0] = (i == j)
    c[:, 1] = (i <= j)
    c[:, 2] = (i < j)
    c[:, 3] = (i >= j)
    c[:, 4] = (i > j)
    c[:, 5] = 1.0
    c[:, 6] = ((i // 64) == (j // 64))
    return c


def _pp(v):
    v = np.asarray(v, np.float32)
    return np.ascontiguousarray(v.reshape(-1, 128).T)


def make_in_maps(x, c, w_ada, b_ada, norm1_w, w_in, conv_w, conv_b, dt_bias, a_log, d_skip,
                 ssd_norm_w, q_norm_w, k_norm_w, w_out, norm2_w, w_gate, w_up, w_down):
    f = lambda a: np.ascontiguousarray(np.asarray(a, np.float32))
    x, c = f(x), f(c)
    w_ada, b_ada = f(w_ada)[0], f(b_ada)[0]
    w_in, conv_w, conv_b = f(w_in)[0], f(conv_w)[0], f(conv_b)[0]
    dt_bias, a_log, d_skip = f(dt_bias)[0], f(a_log)[0], f(d_skip)[0]
    ssd_norm_w, q_norm_w, k_norm_w = f(ssd_norm_w)[0], f(q_norm_w)[0], f(k_norm_w)[0]
    w_out, w_gate, w_up, w_down = f(w_out)[0], f(w_gate)[0], f(w_up)[0], f(w_down)[0]
    n1, n2 = f(norm1_w)[0], f(norm2_w)[0]
    consts = _consts()
    b_pp = np.concatenate([_pp(b_ada[0:1024]), _pp(b_ada[1024:2048]), _pp(b_ada[3072:4096]), _pp(b_ada[4096:5120])], axis=1)
    b_g = np.ascontiguousarray(np.stack([b_ada[2048:3072], b_ada[5120:6144]]))
    maps = []
    for core in range(8):
        b, g = core // 2, core % 2
        cols = np.concatenate([
            np.arange(g * 512, (g + 1) * 512),
            1024 + np.arange(g * 512, (g + 1) * 512),
            2048 + np.arange(g * 128, (g + 1) * 128),
            2304 + np.arange(g * 128, (g + 1) * 128),
            2576 + np.arange(g * 512, (g + 1) * 512),
            3600 + np.arange(g * 512, (g + 1) * 512),
            4624 + np.arange(g * 512, (g + 1) * 512),
            2560 + np.arange(g * 8, (g + 1) * 8),
        ])
        cch = np.concatenate([np.arange(g * 512, (g + 1) * 512), 1024 + np.arange(g * 128, (g + 1) * 128),
                              1280 + np.arange(g * 128, (g + 1) * 128)])
        cw = np.ascontiguousarray(conv_w[:, cch].T.reshape(6, 128, 4).transpose(1, 0, 2))
        cb = _pp(conv_b[cch])
        vec8 = np.zeros((3, 8), np.float32)
        vec8[0] = dt_bias[g * 8:(g + 1) * 8]
        vec8[1] = a_log[g * 8:(g + 1) * 8]
        dsk = np.ascontiguousarray(np.repeat(d_skip[g * 8:(g + 1) * 8], 64).reshape(4, 128).T)
        snw = _pp(ssd_norm_w[g * 512:(g + 1) * 512])
        qkw = np.ascontiguousarray(np.stack([np.tile(q_norm_w, 2), np.tile(k_norm_w, 2)], axis=1))
        flags = np.zeros((128, 2), np.float32)
        flags[:, g] = 1.0
        maps.append({
            "x": x[b], "xh": np.ascontiguousarray(x[b, g * 2048:(g + 1) * 2048]), "cT": _pp(c[b]),
            "w_ada": w_ada, "b_pp": b_pp, "b_g": b_g, "n1w": _pp(n1), "n2w": _pp(n2),
            "w_in": np.ascontiguousarray(w_in[:, cols]), "conv_w": cw, "conv_b": cb, "vec8": vec8,
            "dsk": dsk, "snw": snw, "qkw": qkw, "w_out": w_out, "w_gate": w_gate, "w_up": w_up, "w_down": w_down,
            "consts": consts, "flags": flags,
        })
    return maps


_CACHE = {}


def kernel(**inputs):
    if "nc" not in _CACHE:
        _CACHE["nc"] = build(False)[0]
    nc = _CACHE["nc"]
    maps = make_in_maps(**inputs)
    res = run_bass_kernel_spmd(nc, maps, core_ids=list(range(8)))
    out = np.empty((NB, SEQ, D), np.float32)
    for core in range(8):
        b, g = core // 2, core % 2
        out[b, g * 2048:(g + 1) * 2048] = res.results[core]["out"]
    return out
```

```python
import types
import numpy as np
import concourse.bass as bass
import concourse.mybir as mybir
from concourse.bass_utils import run_bass_kernel_spmd

F32 = mybir.dt.float32
BF16 = mybir.dt.bfloat16
AF = mybir.ActivationFunctionType
ALU = mybir.AluOpType

D = 1024
SEQ = 4096
NB = 4
DFF = 2816
NCH = 22
EPS = 1e-6
WIN = 2824
C_Z, C_X, C_B, C_C, C_Q, C_K, C_V, C_DT = 0, 512, 1024, 1152, 1280, 1792, 2304, 2816
TT = 512
NT = SEQ // TT
SB_BASE = 20480
SB_END = 229376
NO_ALIAS = False
STOP = None


def _freeze(fn):
    if fn is None or fn.__closure__ is None:
        return fn
    cells = []
    for c in fn.__closure__:
        try:
            cells.append(types.CellType(c.cell_contents))
        except ValueError:
            cells.append(c)
    return types.FunctionType(fn.__code__, fn.__globals__, fn.__name__, fn.__defaults__, tuple(cells))


class Sched:
    ENG = ("pe", "act", "dve", "pool", "sp")

    def __init__(self, nc):
        self.nc = nc
        self.e = {"pe": nc.tensor, "act": nc.scalar, "dve": nc.vector, "pool": nc.gpsimd, "sp": nc.sync}
        self.ops = []
        self.last_w = {}
        self.readers = {}
        self.fence_idx = None
        self.last_on = {}
        self.last_dma = {}

    def _add(self, eng, fn, r, w, kind, sk=None):
        idx = len(self.ops)
        deps = {}
        for k in r:
            p = self.last_w.get(k)
            if p is not None:
                deps[p] = True
        for k in w:
            p = self.last_w.get(k)
            if p is not None:
                deps.setdefault(p, False)
            for p in self.readers.get(k, ()):
                if p != idx:
                    deps.setdefault(p, False)
        if self.fence_idx is not None:
            deps[self.fence_idx] = True
        op = dict(eng=eng, fn=_freeze(fn), kind=kind, deps=deps, sk=sk, sig=False)
        for p, raw in deps.items():
            po = self.ops[p]
            need = po["kind"] != "c" or kind != "c" or po["eng"] != eng or raw or eng != "pe"
            if need:
                po["sig"] = True
        for k in r:
            self.readers.setdefault(k, []).append(idx)
        for k in w:
            self.last_w[k] = idx
            self.readers[k] = []
        self.ops.append(op)
        if kind == "c":
            self.last_on[eng] = idx
        else:
            self.last_dma[sk] = idx
        return idx

    def op(self, eng, fn, r=(), w=()):
        r = tuple(r)
        w = tuple(w) + tuple(k for k in r if isinstance(k, tuple) and k and k[0] in ("ps", "ps0") and k not in w)
        return self._add(eng, fn, r, w, "c")

    def dma(self, fn, r=(), w=(), q="sp", sk=None):
        w = tuple(w)
        if sk is None:
            sk = ("dma",) + tuple(w[:1])
        return self._add(q, fn, tuple(r), w, "d", sk)

    def cc(self, fn, r=(), w=(), sk=None):
        return self._add("pool", fn, tuple(r), tuple(w), "cc", sk)

    def fence(self):
        deps = {}
        for e, i in self.last_on.items():
            deps[i] = True
        for sk, i in self.last_dma.items():
            deps[i] = True
        idx = len(self.ops)
        op = dict(eng="sp", fn=None, kind="f", deps=deps, sk=None, sig=True)
        for p in deps:
            self.ops[p]["sig"] = True
        self.ops.append(op)
        self.fence_idx = idx
        self.last_on = {"sp": idx}
        self.last_dma = {}

    def emit(self):
        nc = self.nc
        sems = {}

        def sem_for(name):
            if name not in sems:
                sems[name] = nc.alloc_semaphore("s%d" % len(sems))
            return sems[name]

        cnt = {}
        for op in self.ops:
            if op["kind"] in ("c", "f"):
                key = ("eng", op["eng"])
                inc = 1
            elif op["kind"] == "d":
                key = op["sk"]
                inc = 16
                op["sig"] = True
            else:
                key = op["sk"]
                inc = 1
                op["sig"] = True
            if op["sig"]:
                cnt[key] = cnt.get(key, 0) + inc
                op["sem"] = key
                op["cnt"] = cnt[key]
                op["inc"] = inc
        seen = {e: {} for e in self.ENG}
        nwait = 0
        for op in self.ops:
            eng = op["eng"]
            E = self.e[eng]
            sn = seen[eng]
            need = {}
            for p, raw in op["deps"].items():
                po = self.ops[p]
                if po["kind"] == "c" and op["kind"] == "c" and po["eng"] == eng and not raw and eng == "pe":
                    continue
                k, c = po["sem"], po["cnt"]
                if sn.get(k, 0) >= c:
                    continue
                if need.get(k, 0) < c:
                    need[k] = c
            for k, c in need.items():
                E.wait_ge(sem_for(k), c)
                nwait += 1
                sn[k] = c
            own = ("eng", eng)
            for p, raw in op["deps"].items():
                po = self.ops[p]
                snap = po.get("snap")
                if snap:
                    skipped = po["kind"] == "c" and op["kind"] == "c" and po["eng"] == eng and not raw and eng == "pe"
                    for k, c in snap.items():
                        if skipped and k == own:
                            continue
                        if sn.get(k, 0) < c:
                            sn[k] = c
            if op["kind"] == "f":
                E.sem_inc(sem_for(op["sem"]), 1)
            else:
                ins = op["fn"]()
                if op["sig"]:
                    ins.then_inc(sem_for(op["sem"]), op["inc"])
            if op["sig"]:
                op["snap"] = dict(sn)
                op["snap"][op["sem"]] = op["cnt"]
            op["fn"] = None
        self.stats = (len(self.ops), nwait, len(sems))


class Alloc:
    def __init__(self, nc):
        self.nc = nc
        self.off = SB_BASE
        self.n = 0

    def __call__(self, shape, dt, name=None):
        nbytes = int(np.prod(shape[1:])) * (4 if dt == F32 else 2)
        nbytes = (nbytes + 63) // 64 * 64
        assert NO_ALIAS or self.off + nbytes <= SB_END, ("SBUF overflow", name, self.off, nbytes)
        self.n += 1
        t = self.nc.alloc_sbuf_tensor_at("t%d_%s" % (self.n, name or "x"), list(shape), dt, offset=self.off)
        self.off += nbytes
        return t

    def mark(self):
        return self.off

    def release(self, m):
        if not NO_ALIAS:
            self.off = m


def build(debug=False, upto=3, ntiles=NT, fake_cc=False, skip1a=False, only3=False):
    nc = bass.Bass("TRN2", target_bir_lowering=False)
    S = Sched(nc)
    A = Alloc(nc)
    pe, act, dve, pool, sp = nc.tensor, nc.scalar, nc.vector, nc.gpsimd, nc.sync

    def din(name, shape, dt=F32):
        return nc.dram_tensor(name, list(shape), dt, kind="ExternalInput")

    x_d = din("x", [SEQ, D])
    xh_d = din("xh", [SEQ // 2, D])
    cT_d = din("cT", [128, 8])
    wada_d = din("w_ada", [D, 6 * D])
    bpp_d = din("b_pp", [128, 32])
    bg_d = din("b_g", [2, D])
    n1w_d = din("n1w", [128, 8])
    n2w_d = din("n2w", [128, 8])
    win_d = din("w_in", [D, WIN])
    cw_d = din("conv_w", [128, 6, 4])
    cb_d = din("conv_b", [128, 6])
    vec8_d = din("vec8", [3, 8])
    dsk_d = din("dsk", [128, 4])
    snw_d = din("snw", [128, 4])
    qkw_d = din("qkw", [128, 2])
    wout_d = din("w_out", [2 * D, D])
    wg_d = din("w_gate", [D, DFF])
    wu_d = din("w_up", [D, DFF])
    wd_d = din("w_down", [DFF, D])
    consts_d = din("consts", [128, 7, 128])
    flags_d = din("flags", [128, 2])
    out_d = nc.dram_tensor("out", [SEQ // 2, D], F32, kind="ExternalOutput")

    winb_d = nc.dram_tensor("winb", [D, WIN], BF16, kind="ExternalOutput")
    woutb_d = nc.dram_tensor("woutb", [2 * D, D], BF16, kind="ExternalOutput")
    wgb_d = nc.dram_tensor("wgb", [D, DFF], BF16, kind="ExternalOutput")
    wub_d = nc.dram_tensor("wub", [D, DFF], BF16, kind="ExternalOutput")
    wdb_d = nc.dram_tensor("wdb", [DFF, D], BF16, kind="ExternalOutput")
    hT_d2 = nc.dram_tensor("hTs", [128, 8 * SEQ], BF16, kind="ExternalInput") if skip1a else nc.dram_tensor("hTs", [128, 8 * SEQ], BF16, kind="ExternalOutput")

    class _HT:
        def ap(self):
            return hT_d2.ap().rearrange("p (k t) -> p k t", k=8)
    hT_d = _HT()
    HS = SEQ // 2
    yl_ssd = [nc.dram_tensor("yl_ssd%d" % i, [512, HS], BF16) for i in range(2)]
    yl_sb = [nc.dram_tensor("yl_sb%d" % i, [512, HS], BF16) for i in range(2)]
    if only3:
        skip1a = True
        ya_ssd = [nc.dram_tensor("ya_ssd%d" % i, [1024, HS], BF16, kind="ExternalInput") for i in range(2)]
        ya_sb = [nc.dram_tensor("ya_sb%d" % i, [1024, HS], BF16, kind="ExternalInput") for i in range(2)]
    else:
        ya_ssd = [nc.dram_tensor("ya_ssd%d" % i, [1024, HS], BF16) for i in range(2)]
        ya_sb = [nc.dram_tensor("ya_sb%d" % i, [1024, HS], BF16) for i in range(2)]

    def gather(src, dst, hf, name, ci):
        keys = [(name, t) for t in range(hf * 4, min(ntiles, hf * 4 + 4))]
        if not keys:
            return
        if fake_cc:
            ncol = (min(ntiles, hf * 4 + 4) - hf * 4) * TT
            for hh_ in range(2):
                S.dma((lambda hh_=hh_: sp.dma_start(out=dst[hf].ap()[hh_ * 512:(hh_ + 1) * 512, 0:ncol], in_=src[hf].ap()[:, 0:ncol])),
                      r=keys, w=[("ya", name, hf)], sk=("dma", "fcc", name, hf, hh_))
        else:
            S.cc(lambda: pool.collective_compute("AllGather", ALU.bypass, replica_groups=[[0, 1], [2, 3], [4, 5], [6, 7]],
                                                 ins=[src[hf].ap().opt()], outs=[dst[hf].ap().opt()]),
                 r=keys, w=[("ya", name, hf)], sk=("cc", ci))

    dbg = {}

    def dbg_out(name, shape, dt=F32):
        if debug:
            dbg[name] = nc.dram_tensor(name, list(shape), dt, kind="ExternalOutput")
            return dbg[name]
        return None

    ps = nc.alloc_psum_tensor("ps", [128, 8 * 512], F32)

    def bank(b, c0=0, c1=512):
        return ps[:, b * 512 + c0: b * 512 + c1]

    cst = A([128, 7, 128], F32, "cst")
    ident = cst[:, 0, :]
    UI = cst[:, 1, :]
    US = cst[:, 2, :]
    LI = cst[:, 3, :]
    LS = cst[:, 4, :]
    ONESF = cst[:, 5, :]
    BD = cst[:, 6, :]
    cbf = A([128, 4, 128], BF16, "cbf")
    negLI_b, negI_b, ones_b, US_b = cbf[:, 0, :], cbf[:, 1, :], cbf[:, 2, :], cbf[:, 3, :]
    flags = A([128, 2], F32, "flags")
    sv = A([128, 96], F32, "sv")
    cT = sv[:, 0:8]
    condT = sv[:, 8:16]
    n1w = sv[:, 16:24]
    n2w = sv[:, 24:32]
    modpp = sv[:, 32:64]
    s1 = sv[:, 64:72]
    s2 = sv[:, 72:80]
    bpp = A([128, 32], F32, "bpp")
    g1bc = A([128, D], F32, "g1bc")
    g2bc = A([128, D], F32, "g2bc")
    cw = A([128, 6, 4], F32, "cw")
    cbv = A([128, 6], F32, "cbv")
    v8 = A([128, 3, 8], F32, "v8")
    Aneg = A([128, 8], F32, "Aneg")
    dsk = A([128, 4], F32, "dsk")
    snw = A([128, 4], F32, "snw")
    qkw = A([128, 2], F32, "qkw")
    epsc = A([128, 1], F32, "epsc")

    S.dma(lambda: sp.dma_start(out=cst[:], in_=consts_d.ap()), w=["cst"])
    for (t, d, k) in ((flags, flags_d, "flags"), (bpp, bpp_d, "bpp"), (cw, cw_d, "cw"), (cbv, cbv_d if False else cb_d, "cbv"),
                      (dsk, dsk_d, "dsk"), (snw, snw_d, "snw"), (qkw, qkw_d, "qkw")):
        S.dma((lambda t=t, d=d: sp.dma_start(out=t[:], in_=d.ap())), w=[k])
    S.dma(lambda: sp.dma_start(out=sv[:, 0:8], in_=cT_d.ap()), w=["cT"])
    S.dma(lambda: sp.dma_start(out=sv[:, 16:24], in_=n1w_d.ap()), w=["n1w"])
    S.dma(lambda: sp.dma_start(out=sv[:, 24:32], in_=n2w_d.ap()), w=["n2w"])
    for r in range(2):
        S.dma((lambda r=r: sp.dma_start(out=v8[:, r, :], in_=vec8_d.ap()[r:r + 1, :].partition_broadcast(128))),
              w=[("v8", r)])
    S.dma(lambda: sp.dma_start(out=g1bc[:], in_=bg_d.ap()[0:1, :].partition_broadcast(128)), w=["g1bc"])
    S.dma(lambda: sp.dma_start(out=g2bc[:], in_=bg_d.ap()[1:2, :].partition_broadcast(128)), w=["g2bc"])

    S.op("dve", lambda: dve.memset(epsc[:], EPS), w=["epsc"])
    S.op("dve", lambda: dve.tensor_scalar_mul(cbf[:, 0, :], LI, -1.0), r=["cst"], w=["cbf0"])
    S.op("dve", lambda: dve.tensor_scalar_mul(cbf[:, 1, :], ident, -1.0), r=["cst"], w=["cbf1"])
    S.op("dve", lambda: dve.tensor_copy(cbf[:, 2, :], ONESF), r=["cst"], w=["cbf2"])
    S.op("dve", lambda: dve.tensor_copy(cbf[:, 3, :], US), r=["cst"], w=["cbf3"])
    CB = ["cbf0", "cbf1", "cbf2", "cbf3"]
    S.op("dve", lambda: dve.tensor_scalar_mul(qkw[:, 0:1], qkw[:, 0:1], 0.125), r=["qkw"], w=["qkw"])
    S.op("act", lambda: act.activation(Aneg[:], v8[:, 1, :], AF.Exp), r=[("v8", 1)], w=["Aneg"])
    S.op("dve", lambda: dve.tensor_scalar_mul(Aneg[:], Aneg[:], -1.0), r=["Aneg"], w=["Aneg"])
    S.op("act", lambda: act.activation(condT, cT, AF.Silu), r=["cT"], w=["condT"])

    PW = 2048
    mP = A.mark()
    stf = [A([128, PW], F32, "stf%d" % i) for i in range(2)]
    stb = [A([128, PW], BF16, "stb%d" % i) for i in range(2)]
    prep_list = []

    def prep_matrix(src, dst, rows, cols, key):
        nr = rows // 128
        ncp = (cols + PW - 1) // PW
        cw_ = (cols + ncp - 1) // ncp
        for r in range(nr):
            for c in range(ncp):
                c0, c1 = c * cw_, min(cols, (c + 1) * cw_)
                prep_list.append((src, dst, r, c0, c1, (key, r)))

    prep_matrix(win_d, winb_d, D, WIN, "winb")
    prep_matrix(wout_d, woutb_d, 2 * D, D, "woutb")
    prep_matrix(wg_d, wgb_d, D, DFF, "wgb")
    prep_matrix(wu_d, wub_d, D, DFF, "wub")
    prep_matrix(wd_d, wdb_d, DFF, D, "wdb")
    prep_state = {"next_load": 0, "next_cast": 0}

    def prep_load(i):
        src, dst, r, c0, c1, key = prep_list[i]
        b = i % 2
        S.dma((lambda: pool.dma_start(out=stf[b][:, 0:c1 - c0], in_=src.ap()[r * 128:(r + 1) * 128, c0:c1])),
              w=[("stf", b)], q="pool")

    def prep_cast_store(i):
        src, dst, r, c0, c1, key = prep_list[i]
        b = i % 2
        S.op("pool", (lambda: pool.tensor_copy(stb[b][:, 0:c1 - c0], stf[b][:, 0:c1 - c0])),
             r=[("stf", b)], w=[("stb", b)])
        S.dma((lambda: pool.dma_start(out=dst.ap()[r * 128:(r + 1) * 128, c0:c1], in_=stb[b][:, 0:c1 - c0])),
              r=[("stb", b)], w=[key + (c0,)], q="pool", sk=("dma", "stbo", b))

    def prep_advance(n):
        for _ in range(n):
            i = prep_state["next_cast"]
            if i >= len(prep_list):
                return
            if prep_state["next_load"] == 0:
                prep_load(0)
                prep_state["next_load"] = 1
            if prep_state["next_load"] < len(prep_list) and prep_state["next_load"] == i + 1:
                prep_load(i + 1)
                prep_state["next_load"] = i + 2
            prep_cast_store(i)
            prep_state["next_cast"] = i + 1

    def prep_keys(key, rows, cols):
        nr = rows // 128
        ncp = (cols + PW - 1) // PW
        cw_ = (cols + ncp - 1) // ncp
        return [(key, r, c * cw_) for r in range(nr) for c in range(ncp)]

    def finish():
        prep_advance(1000)
        S.fence()
        S.emit()
        return nc, S, dbg

    prep_advance(16)

    m0 = A.mark()
    cbc = A([128, 8, 128], F32, "cbc")
    NWST = 6
    wst = [A([128, 2048], F32, "wst%d" % i) for i in range(NWST)]
    S.op("dve", lambda: dve.tensor_copy(cbc[:], condT.unsqueeze(2).to_broadcast([128, 8, 128])), r=["condT"], w=["cbc"])
    pc = 0
    for cg in range(3):
        for k in range(8):
            b = pc % NWST
            qn = "sp" if pc % 2 == 0 else "act"
            pc += 1
            S.dma((lambda b=b, k=k, cg=cg, qn=qn: S.e[qn].dma_start(out=wst[b][:], in_=wada_d.ap()[k * 128:(k + 1) * 128, cg * 2048:(cg + 1) * 2048])),
                  w=[("wst", b)], q=qn)
            st, sp_ = (k == 0), (k == 7)

            def ppmm(colbase, ncols, wofs, b=b, k=k, st=st, sp_=sp_):
                for cc in range(ncols):
                    S.op("pe", (lambda cc=cc: pe.matmul(bank(0, colbase + cc, colbase + cc + 1),
                                                        wst[b][:, wofs + cc * 128: wofs + (cc + 1) * 128],
                                                        condT[:, k:k + 1], start=(st and colbase == 0 and cc == 0), stop=sp_,
                                                        skip_group_check=True)),
                         r=[("wst", b), "condT"], w=[("ps0", colbase + cc)])

            def bcmm(bk, wofs, b=b, k=k, st=st, sp_=sp_):
                for h in range(2):
                    S.op("pe", (lambda h=h: pe.matmul(bank(bk + h), cbc[:, k, :],
                                                      wst[b][:, wofs + h * 512: wofs + (h + 1) * 512], start=st, stop=sp_)),
                         r=[("wst", b), "cbc"], w=[("ps", bk + h)])
            if cg == 0:
                ppmm(0, 16, 0)
            elif cg == 1:
                bcmm(1, 0)
                ppmm(16, 8, 1024)
            else:
                ppmm(24, 8, 0)
                bcmm(3, 1024)
    S.op("dve", lambda: dve.tensor_tensor(modpp, bank(0, 0, 32), bpp[:], ALU.add),
         r=[("ps0", c) for c in range(32)] + ["bpp"], w=["modpp"])
    for h in range(2):
        S.op("dve", (lambda h=h: dve.tensor_tensor(g1bc[:, h * 512:(h + 1) * 512], bank(1 + h), g1bc[:, h * 512:(h + 1) * 512], ALU.add)),
             r=[("ps", 1 + h), "g1bc"], w=["g1bc"])
        S.op("dve", (lambda h=h: dve.tensor_tensor(g2bc[:, h * 512:(h + 1) * 512], bank(3 + h), g2bc[:, h * 512:(h + 1) * 512], ALU.add)),
             r=[("ps", 3 + h), "g2bc"], w=["g2bc"])
    S.op("dve", lambda: dve.scalar_tensor_tensor(s1, modpp[:, 8:16], 1.0, n1w, ALU.add, ALU.mult), r=["modpp", "n1w"], w=["s1"])
    S.op("dve", lambda: dve.scalar_tensor_tensor(s2, modpp[:, 24:32], 1.0, n2w, ALU.add, ALU.mult), r=["modpp", "n2w"], w=["s2"])
    t1 = modpp[:, 0:8]
    t2 = modpp[:, 16:24]
    if debug:
        d_mod = dbg_out("d_mod", [128, 96])
        S.dma(lambda: sp.dma_start(out=d_mod.ap()[:, 0:80], in_=sv[:, 0:80]), r=["modpp", "s1", "s2", "condT", "cT", "n1w", "n2w"], w=["d_mod"])
        d_g = dbg_out("d_g", [128, 2 * D])
        S.dma(lambda: sp.dma_start(out=d_g.ap()[:, 0:D], in_=g1bc[:]), r=["g1bc"], w=["d_g1"])
        S.dma(lambda: sp.dma_start(out=d_g.ap()[:, D:2 * D], in_=g2bc[:]), r=["g2bc"], w=["d_g2"])
    S.fence()
    A.release(m0)
    if upto < 1:
        prep_advance(1000)
        S.fence()
        S.emit()
        return nc, S, dbg

    WINK = prep_keys("winb", D, WIN)
    winb_v = winb_d.ap().rearrange("(k p) c -> p k c", p=128)

    def rstd_from_ssq(ssq, n, keyr, keyw):
        S.op("act", lambda: act.activation(ssq, ssq, AF.Ln, bias=epsc[:], scale=1.0 / n), r=[keyr, "epsc"], w=[keyw])
        S.op("act", lambda: act.activation(ssq, ssq, AF.Exp, scale=-0.5), r=[keyw], w=[keyw])

    m1 = A.mark()
    W1 = 1288
    w1 = A([128, 8, W1], BF16, "w1")
    S.dma(lambda: sp.dma_start(out=w1[:, :, 0:1280], in_=winb_v[:, :, 0:1280]), r=WINK, w=["w1a"])
    S.dma(lambda: sp.dma_start(out=w1[:, :, 1280:1288], in_=winb_v[:, :, C_DT:C_DT + 8]), r=WINK, w=["w1b"])
    W1K = ["w1a", "w1b"]
    xs = [A([128, D], F32, "xs%d" % i) for i in range(4)]
    sq_junk = A([128, D], BF16, "sqj")
    ssq = A([128, 8], F32, "ssq")
    hT = A([128, 8, TT], BF16, "hT")
    zs = A([128, 4, TT], F32, "zs")
    u = A([128, 6, TT + 3], F32, "u")
    acc = A([128, 6, TT], F32, "acc")
    BTb = A([128, TT], BF16, "BTb")
    CTb = A([128, TT], BF16, "CTb")
    dtb = A([128, 4, 8], F32, "dtb")
    adt4 = A([128, 4, 8], F32, "adt")
    rseg2 = [A([128, 8, 128], F32, "rseg%d" % i) for i in range(2)]
    dec2 = [A([128, 8, 128], F32, "dec%d" % i) for i in range(2)]
    eac4 = A([128, 4, 4, 128], F32, "eac")
    dst4 = A([128, 4, 8], F32, "dst")
    cdec4 = A([128, 4, 8], F32, "cdec")
    xg4 = A([128, 4, 512], BF16, "xg")
    xgd4 = A([128, 4, 512], BF16, "xgd")
    Btok4 = A([128, 4, 128], BF16, "Btok")
    sctm2 = [A([128, 128], F32, "sctm%d" % i) for i in range(2)]
    G4 = A([128, 4, 8, 128], BF16, "G")
    ydg4 = A([128, 4, 512], F32, "ydg")
    prevT = A([128, 8, 64], F32, "prevT")
    prevb = A([128, 512], BF16, "prevb")
    tmpA = A([128, 512], F32, "tmpA")
    ytile = A([128, 4, TT], F32, "ytile")
    ysq = A([128, 4, TT], F32, "ysq")
    rbc = A([128, TT], F32, "rbc")
    yT = A([128, 4, TT], BF16, "yT")
    yl_ssd_v = [t_.ap().rearrange("(c p) t -> p c t", p=128) for t_ in yl_ssd]

    S.op("dve", lambda: dve.memset(u[:], 0.0), w=["u_halo"] + [("u", j) for j in range(6)])
    S.op("dve", lambda: dve.memset(prevT[:], 0.0), w=["prevT"])
    S.op("dve", lambda: dve.memset(prevb[:], 0.0), w=["prevb"])
    if debug and not skip1a:
        d_hT = dbg_out("d_hT", [128, 8, TT], BF16)
        d_xc = dbg_out("d_xc", [128, 6, TT])
        d_yt = dbg_out("d_yt", [128, 4, TT])

    if STOP == "w1":
        return finish()
    HTK = [("hT", c, s) for c in range(8) for s in range(4)]

    def hT_part1(tt):
        for s in range(4):
            r0 = (tt * 4 + s) * 128
            S.dma((lambda s=s, r0=r0: sp.dma_start(out=xs[s][:], in_=x_d.ap()[r0:r0 + 128, :])), w=[("xs", s)])
        for s in range(4):
            S.op("act", (lambda s=s: act.activation(sq_junk[:], xs[s][:], AF.Square, accum_out=ssq[:, s:s + 1])), r=[("xs", s)], w=["sqj", ("ssq", s)])
        for s in range(4):
            S.op("act", (lambda s=s: act.activation(ssq[:, s:s + 1], ssq[:, s:s + 1], AF.Ln, bias=epsc[:], scale=1.0 / D)), r=[("ssq", s), "epsc"], w=[("ssq", s)])
        for s in range(4):
            S.op("act", (lambda s=s: act.activation(ssq[:, s:s + 1], ssq[:, s:s + 1], AF.Exp, scale=-0.5)), r=[("ssq", s)], w=[("ssq", s)])
        for s in range(4):
            S.op("dve", (lambda s=s: dve.tensor_scalar_mul(xs[s][:], xs[s][:], ssq[:, s:s + 1])), r=[("xs", s), ("ssq", s)], w=[("xs", s)])

    def hT_tr(tt, s):
        for half in range(2):
            bk = 4 + (s % 2) * 2 + half
            for c4 in range(4):
                c = half * 4 + c4
                S.op("pe", (lambda c=c, c4=c4, bk=bk: pe.transpose(bank(bk, c4 * 128, (c4 + 1) * 128), xs[s][:, c * 128:(c + 1) * 128], ident)),
                     r=[("xs", s), "cst"], w=[("ps", bk)])

    def hT_ev(tt, s):
        for half in range(2):
            bk = 4 + (s % 2) * 2 + half
            for c4 in range(4):
                c = half * 4 + c4
                S.op("dve", (lambda c=c, c4=c4, bk=bk: dve.tensor_scalar(hT[:, c, s * 128:(s + 1) * 128], bank(bk, c4 * 128, (c4 + 1) * 128),
                                                                      s1[:, c:c + 1], t1[:, c:c + 1], ALU.mult, ALU.add)),
                     r=[("ps", bk), "s1", "modpp"], w=[("hT", c, s)])

    def hT_part2(tt):
        hT_tr(tt, 0); hT_tr(tt, 1); hT_ev(tt, 0); hT_tr(tt, 2); hT_ev(tt, 1); hT_tr(tt, 3); hT_ev(tt, 2); hT_ev(tt, 3)
        S.dma((lambda: sp.dma_start(out=hT_d.ap()[:, :, tt * TT:(tt + 1) * TT], in_=hT[:])), r=HTK, w=[("hTd", tt)], sk=("dma", "hTd"))
        if debug and tt == 1:
            S.dma(lambda: sp.dma_start(out=d_hT.ap(), in_=hT[:]), r=HTK, w=["d_hT"])

    if not skip1a:
        hT_part1(0)
        hT_part2(0)
    for tt in range(0 if skip1a else ntiles):
        prep_advance(6)
        for j in range(10):
            bk = 4 + (j % 4)
            for k in range(8):
                S.op("pe", (lambda j=j, k=k, bk=bk: pe.matmul(bank(bk), w1[:, k, j * 128:(j + 1) * 128], hT[:, k, :], start=(k == 0), stop=(k == 7))),
                     r=W1K + [("hT", k, s) for s in range(4)], w=[("ps", bk)])
            if j < 4:
                S.op("act", (lambda j=j, bk=bk: act.activation(zs[:, j, :], bank(bk), AF.Silu)), r=[("ps", bk)], w=[("zs", j)])
            else:
                jj = j - 4
                S.op("act", (lambda jj=jj, bk=bk: act.activation(acc[:, jj, :], bank(bk), AF.Identity, bias=cbv[:, jj:jj + 1], scale=cw[:, jj, 3:4])),
                     r=[("ps", bk), "cw", "cbv"], w=[("acc", jj)])
                S.op("act", (lambda jj=jj, bk=bk: act.copy(u[:, jj, 3:TT + 3], bank(bk))), r=[("ps", bk), "u_halo"], w=[("u", jj)])
                for kk in (2, 1, 0):
                    S.op("dve", (lambda jj=jj, kk=kk: dve.scalar_tensor_tensor(acc[:, jj, :], u[:, jj, kk:kk + TT], cw[:, jj, kk:kk + 1], acc[:, jj, :], ALU.mult, ALU.add)),
                         r=[("u", jj), ("acc", jj), "cw"], w=[("acc", jj)])
                if jj < 4:
                    S.op("act", (lambda jj=jj: act.activation(acc[:, jj, :], acc[:, jj, :], AF.Silu)), r=[("acc", jj)], w=[("acc", jj)])
                else:
                    S.op("act", (lambda jj=jj: act.activation(acc[:, jj, :], acc[:, jj, :], AF.Silu)), r=[("acc", jj)], w=[("acc", jj)])
                    tb = BTb if jj == 4 else CTb
                    S.op("dve", (lambda jj=jj, tb=tb: dve.tensor_copy(tb[:], acc[:, jj, :])), r=[("acc", jj)], w=["BTb" if jj == 4 else "CTb"])
        if STOP == "inproj":
            return finish()
        S.op("dve", lambda: dve.tensor_copy(u[:, :, 0:3], u[:, :, TT:TT + 3]), r=[("u", j) for j in range(6)], w=["u_halo"] + [("u", j) for j in range(6)])
        if debug and tt == 1:
            S.dma(lambda: sp.dma_start(out=d_xc.ap(), in_=acc[:]), r=[("acc", j) for j in range(6)], w=["d_xc"])
        for s in range(4):
            for k in range(8):
                S.op("pe", (lambda s=s, k=k: pe.matmul(bank(1, 264 + s * 8, 272 + s * 8), hT[:, k, s * 128:(s + 1) * 128], w1[:, k, 1280:1288], start=(k == 0), stop=(k == 7))),
                     r=W1K + [("hT", k, s)], w=[("ps", 1)])
        S.op("dve", lambda: dve.tensor_tensor(dtb[:], bank(1, 264, 296).rearrange("p (s h) -> p s h", s=4),
                                              v8[:, 0, :].unsqueeze(1).to_broadcast([128, 4, 8]), ALU.add),
             r=[("ps", 1), ("v8", 0)], w=["dtb"])
        S.op("act", lambda: act.activation(dtb[:], dtb[:], AF.Exp), r=["dtb"], w=["dtb"])
        S.op("act", lambda: act.activation(dtb[:], dtb[:], AF.Ln, bias=1.0), r=["dtb"], w=["dtb"])
        if STOP == "dt":
            return finish()
        def stA(ci):
            c0, c1 = ci * 128, (ci + 1) * 128
            dtc = dtb[:, ci, :]
            for j in range(4):
                S.op("pe", (lambda j=j: pe.transpose(bank(0, j * 128, (j + 1) * 128), acc[:, j, c0:c1], ident)),
                     r=[("acc", j), "cst"], w=[("ps", 0)])
            S.op("pe", (lambda: pe.transpose(bank(1, 0, 128), acc[:, 4, c0:c1], ident)), r=[("acc", 4), "cst"], w=[("ps", 1)])
            S.op("dve", (lambda: dve.tensor_tensor(xg4[:, ci, :].rearrange("p (h e) -> p h e", h=8), bank(0).rearrange("p (h e) -> p h e", h=8),
                                                   dtc.unsqueeze(2).to_broadcast([128, 8, 64]), ALU.mult)),
                 r=[("ps", 0), "dtb"], w=[("xg", ci)])
            S.op("act", (lambda: act.copy(Btok4[:, ci, :], bank(1, 0, 128))), r=[("ps", 1)], w=[("Btok", ci)])

        def stB1(ci):
            dtc = dtb[:, ci, :]
            rs = rseg2[ci % 2]
            S.op("dve", (lambda: dve.tensor_tensor(adt4[:, ci, :], dtc, Aneg[:], ALU.mult)), r=["dtb", "Aneg"], w=[("adt", ci)])
            S.op("dve", (lambda: dve.tensor_tensor(rs[:], UI.unsqueeze(1).to_broadcast([128, 8, 128]),
                                                   adt4[:, ci, :].unsqueeze(2).to_broadcast([128, 8, 128]), ALU.mult)),
                 r=[("adt", ci), "cst"], w=[("rseg", ci % 2)])

        def stB2(ci):
            rs = rseg2[ci % 2]
            dc = dec2[ci % 2]
            for hf in range(2):
                rv = rs[:, hf * 4:(hf + 1) * 4, :].rearrange("p h l -> p (h l)")
                S.op("pe", (lambda hf=hf, rv=rv: pe.matmul(bank(2 + hf), LS, rv, start=True, stop=True)), r=[("rseg", ci % 2), "cst"], w=[("ps", 2 + hf)])
                S.op("pe", (lambda hf=hf, rv=rv: pe.matmul(bank(4 + hf), ONESF, rv, start=True, stop=True)), r=[("rseg", ci % 2), "cst"], w=[("ps", 4 + hf)])
            S.op("pe", (lambda: pe.matmul(bank(1, 256, 264), LS, adt4[:, ci, :], start=True, stop=True)), r=[("adt", ci), "cst"], w=[("ps", 1)])
            S.op("act", (lambda: act.activation(dst4[:, ci, :], bank(1, 256, 264), AF.Exp)), r=[("ps", 1)], w=[("dst", ci)])
            for hf in range(2):
                S.op("act", (lambda hf=hf: act.activation(dc[:, hf * 4:(hf + 1) * 4, :].rearrange("p h l -> p (h l)"), bank(2 + hf), AF.Exp)),
                     r=[("ps", 2 + hf)], w=[("dec", ci % 2, hf)])
            acb = ps[:, 4 * 512: 6 * 512].rearrange("p (pr two l) -> p pr two l", pr=4, two=2)
            S.op("act", (lambda: act.activation(eac4[0:64, ci, :, :], acb[0:64, :, 0, :], AF.Exp)), r=[("ps", 4), ("ps", 5)], w=[("eac", ci, 0)])
            S.op("act", (lambda: act.activation(eac4[64:128, ci, :, :], acb[64:128, :, 1, :], AF.Exp)), r=[("ps", 4), ("ps", 5)], w=[("eac", ci, 1)])
            acl = ps[:, 4 * 512: 6 * 512].rearrange("p (h l) -> p h l", h=8)
            S.op("act", (lambda: act.activation(cdec4[:, ci, :], acl[:, :, 127], AF.Exp)), r=[("ps", 4), ("ps", 5)], w=[("cdec", ci)])

        def stC(ci):
            c0, c1 = ci * 128, (ci + 1) * 128
            dc = dec2[ci % 2]
            sm = sctm2[ci % 2]
            S.op("pe", (lambda: pe.matmul(bank(1, 128, 256), BTb[:, c0:c1], CTb[:, c0:c1], start=True, stop=True)),
                 r=["BTb", "CTb"], w=[("ps", 1)])
            S.op("dve", (lambda: dve.tensor_tensor(sm[:], bank(1, 128, 256), UI, ALU.mult)), r=[("ps", 1), "cst"], w=[("sctm", ci % 2)])
            S.op("dve", (lambda: dve.tensor_tensor(xgd4[:, ci, :].rearrange("p (h e) -> p h e", h=8), xg4[:, ci, :].rearrange("p (h e) -> p h e", h=8),
                                                   dst4[:, ci, :].unsqueeze(2).to_broadcast([128, 8, 64]), ALU.mult)),
                 r=[("xg", ci), ("dst", ci)], w=[("xgd", ci)])
            S.op("dve", (lambda: dve.tensor_tensor(G4[:, ci, :, :], dc[:], sm[:].unsqueeze(1).to_broadcast([128, 8, 128]), ALU.mult)),
                 r=[("dec", ci % 2, 0), ("dec", ci % 2, 1), ("sctm", ci % 2)], w=[("G", ci)])

        def stD(ci):
            for h in range(8):
                pr, hh = h // 2, h % 2
                S.op("pe", (lambda h=h, pr=pr, hh=hh: pe.matmul(ps[hh * 64:(hh + 1) * 64, 6 * 512 + pr * 128: 6 * 512 + (pr + 1) * 128],
                                                               xg4[:, ci, h * 64:(h + 1) * 64], G4[:, ci, h, :], start=True, stop=True)),
                     r=[("xg", ci), ("G", ci)], w=[("ps", 6)])
            S.op("act", (lambda: act.copy(ydg4[:, ci, :], bank(6))), r=[("ps", 6)], w=[("ydg", ci)])

        def stII(ci):
            c0, c1 = ci * 128, (ci + 1) * 128
            for pr in range(4):
                S.op("pe", (lambda pr=pr: pe.matmul(bank(7, pr * 128, (pr + 1) * 128), prevb[:, pr * 128:(pr + 1) * 128], CTb[:, c0:c1], start=True, stop=True)),
                     r=["prevb", "CTb"], w=[("ps", 7)])
            S.op("dve", (lambda: dve.tensor_tensor(tmpA[:], bank(7), eac4[:, ci, :, :].rearrange("p a l -> p (a l)"), ALU.mult)),
                 r=[("ps", 7), ("eac", ci, 0), ("eac", ci, 1)], w=["tmpA"])
            S.op("pe", (lambda: pe.matmul(bank(7), Btok4[:, ci, :], xgd4[:, ci, :], start=True, stop=True)), r=[("Btok", ci), ("xgd", ci)], w=[("ps", 7)])
            S.op("dve", (lambda: dve.tensor_tensor(prevT[:], prevT[:], cdec4[:, ci, :].unsqueeze(2).to_broadcast([128, 8, 64]), ALU.mult)),
                 r=["prevT", ("cdec", ci)], w=["prevT"])
            S.op("dve", (lambda: dve.tensor_tensor(ytile[:, :, c0:c1], ydg4[:, ci, :].rearrange("p (a l) -> p a l", a=4),
                                                   tmpA[:].rearrange("p (a l) -> p a l", a=4), ALU.add)),
                 r=[("ydg", ci), "tmpA"], w=[("ytile", ci)])
            S.op("dve", (lambda: dve.tensor_tensor(prevT[:], prevT[:], bank(7).rearrange("p (h e) -> p h e", h=8), ALU.add)),
                 r=["prevT", ("ps", 7)], w=["prevT"])
            S.op("dve", (lambda: dve.tensor_copy(prevb[:], prevT[:].rearrange("p h e -> p (h e)"))), r=["prevT"], w=["prevb"])

        for ci in range(4):
            stA(ci)
        stB1(0); stB1(1); stB2(0); stB1(2); stB2(1); stC(0); stB1(3); stB2(2); stC(1); stB2(3); stC(2); stC(3)
        for ci in range(4):
            stD(ci)
        for ci in range(4):
            stII(ci)
        if STOP == "chunks":
            return finish()
        if tt + 1 < ntiles:
            hT_part1(tt + 1)
        YK = [("ytile", ci) for ci in range(4)]
        S.op("dve", lambda: dve.tensor_tensor(ysq[:], acc[:, 0:4, :], dsk[:].unsqueeze(2).to_broadcast([128, 4, TT]), ALU.mult),
             r=[("acc", j) for j in range(4)] + ["dsk"], w=["ysq"])
        S.op("dve", lambda: dve.tensor_tensor(ytile[:], ytile[:], ysq[:], ALU.add), r=YK + ["ysq"], w=YK)
        S.op("dve", lambda: dve.tensor_tensor(ytile[:], ytile[:], zs[:], ALU.mult), r=YK + [("zs", j) for j in range(4)], w=YK)
        if debug and tt == 1:
            S.dma(lambda: sp.dma_start(out=d_yt.ap(), in_=ytile[:]), r=YK, w=["d_yt"])
        S.op("act", lambda: act.activation(ysq[:], ytile[:], AF.Square), r=YK, w=["ysq"])
        for a in range(4):
            S.op("pe", (lambda a=a: pe.matmul(bank(2), ONESF, ysq[:, a, :], start=(a == 0), stop=(a == 3))), r=["ysq", "cst"], w=[("ps", 2)])
        S.op("act", lambda: act.activation(rbc[:], bank(2), AF.Ln, bias=epsc[:], scale=1.0 / 512), r=[("ps", 2), "epsc"], w=["rbc"])
        S.op("act", lambda: act.activation(rbc[:], rbc[:], AF.Exp, scale=-0.5), r=["rbc"], w=["rbc"])
        S.op("dve", lambda: dve.tensor_tensor(ytile[:], ytile[:], rbc[:].unsqueeze(1).to_broadcast([128, 4, TT]), ALU.mult), r=YK + ["rbc"], w=YK)
        S.op("dve", lambda: dve.tensor_tensor(yT[:], ytile[:], snw[:].unsqueeze(2).to_broadcast([128, 4, TT]), ALU.mult), r=YK + ["snw"], w=["yT"])
        S.dma((lambda tt=tt: sp.dma_start(out=yl_ssd_v[tt // 4][:, :, (tt % 4) * TT:(tt % 4 + 1) * TT], in_=yT[:])), r=["yT"], w=[("yl_ssd", tt)], sk=("dma", "yT"))
        if tt + 1 < ntiles:
            hT_part2(tt + 1)
        if tt == 3:
            prep_advance(1000)
            gather(yl_ssd, ya_ssd, 0, "yl_ssd", 0)

    prep_advance(1000)
    if debug and not skip1a:
        d_yssd = dbg_out("d_yssd", [512, SEQ], BF16)
        for hf_ in range(2):
            nt_ = min(ntiles, hf_ * 4 + 4) - hf_ * 4
            if nt_ > 0:
                S.dma((lambda hf_=hf_, nt_=nt_: sp.dma_start(out=d_yssd.ap()[:, hf_ * HS: hf_ * HS + nt_ * TT], in_=yl_ssd[hf_].ap()[:, 0:nt_ * TT])),
                      r=[("yl_ssd", t) for t in range(ntiles)], w=[("d_yssd", hf_)])
    S.fence()
    A.release(m1)
    if upto < 2:
        S.emit()
        return nc, S, dbg
    if not skip1a:
        gather(yl_ssd, ya_ssd, 1, "yl_ssd", 1)

    m2 = A.mark()
    W2 = 1536
    w2 = A([128, 8, W2], BF16, "w2")
    S.dma(lambda: sp.dma_start(out=w2[:], in_=winb_v[:, :, C_Q:C_Q + W2]), r=WINK, w=["w2"])
    kT = A([128, 4, SEQ], BF16, "kT")
    vb = A([128, 32, 512], BF16, "vb")
    hT2 = [A([128, 8, TT], BF16, "hT2_%d" % i) for i in range(2)]
    qT = A([128, 4, TT], BF16, "qT")
    sqf = [A([128, TT], F32, "sqf%d" % i) for i in range(2)]
    rq = [A([128, TT], F32, "rq%d" % i) for i in range(2)]
    ebuf = [A([128, 512], F32, "e%d" % i) for i in range(2)]
    spb2 = [[A([128, 512], BF16, "sp%d_%d" % (i, q)) for q in range(2)] for i in range(2)]
    spm = [A([128, 512], BF16, "spm%d" % i) for i in range(2)]
    wb2 = [[A([128, 512], BF16, "w%d_%d" % (i, q)) for q in range(2)] for i in range(2)]
    wm = [A([128, 512], BF16, "wm%d" % i) for i in range(2)]
    cbb = [A([128, 512], BF16, "cb%d" % i) for i in range(2)]
    ysb = A([128, 4, TT], BF16, "ysb")
    yl_sb_v = [t_.ap().rearrange("(c p) t -> p c t", p=128) for t_ in yl_sb]
    US4 = US_b.unsqueeze(1).to_broadcast([128, 4, 128])
    ZBP = ((0, 1), (2, 3))
    CBK = (4, 5)
    OB = 6
    if debug and not only3:
        d_q = dbg_out("d_q", [128, 4, TT], BF16)

    for tt in range(0 if only3 else ntiles):
        hb = hT2[tt % 2]
        kh = ("hT2", tt % 2)

        def load_h(t_):
            S.dma((lambda: sp.dma_start(out=hT2[t_ % 2][:], in_=hT_d.ap()[:, :, t_ * TT:(t_ + 1) * TT])), r=[("hTd", t_)], w=[("hT2", t_ % 2)])
        if tt == 0:
            load_h(0)
        rot = [7, 6, 5]
        ri = 0
        for j in range(8):
            isq = j < 4
            c = j % 4
            bq = rot[ri % 3]
            bs = rot[(ri + 1) % 3]
            ri += 2
            col0 = (0 if isq else 512) + c * 128
            for k in range(8):
                S.op("pe", (lambda k=k, bq=bq, col0=col0: pe.matmul(bank(bq), w2[:, k, col0:col0 + 128], hb[:, k, :], start=(k == 0), stop=(k == 7))),
                     r=["w2", kh], w=[("ps", bq)])
            sb_ = sqf[j % 2]
            rb_ = rq[j % 2]
            S.op("act", (lambda bq=bq, sb_=sb_: act.activation(sb_[:], bank(bq), AF.Square)), r=[("ps", bq)], w=[("sqf", j % 2)])
            S.op("pe", (lambda bs=bs, sb_=sb_: pe.matmul(bank(bs), BD, sb_[:], start=True, stop=True)), r=[("sqf", j % 2), "cst"], w=[("ps", bs)])
            S.op("act", (lambda bs=bs, rb_=rb_: act.activation(rb_[:], bank(bs), AF.Ln, bias=epsc[:], scale=1.0 / 64)), r=[("ps", bs), "epsc"], w=[("rq", j % 2)])
            S.op("act", (lambda rb_=rb_: act.activation(rb_[:], rb_[:], AF.Exp, scale=-0.5)), r=[("rq", j % 2)], w=[("rq", j % 2)])
            if isq:
                S.op("dve", (lambda c=c, bq=bq, rb_=rb_: dve.scalar_tensor_tensor(qT[:, c, :], bank(bq), qkw[:, 0:1], rb_[:], ALU.mult, ALU.mult)),
                     r=[("ps", bq), ("rq", j % 2), "qkw"], w=[("qT", c)])
            else:
                S.op("dve", (lambda c=c, bq=bq, rb_=rb_, tt=tt: dve.scalar_tensor_tensor(kT[:, c, tt * TT:(tt + 1) * TT], bank(bq), qkw[:, 1:2], rb_[:], ALU.mult, ALU.mult)),
                     r=[("ps", bq), ("rq", j % 2), "qkw"], w=[("kT", c, tt)])
        for s in range(4):
            bq = rot[ri % 3]
            ri += 1
            for k in range(8):
                S.op("pe", (lambda k=k, bq=bq, s=s: pe.matmul(bank(bq), hb[:, k, s * 128:(s + 1) * 128], w2[:, k, 1024:1536], start=(k == 0), stop=(k == 7))),
                     r=["w2", kh], w=[("ps", bq)])
            S.op("dve", (lambda bq=bq, s=s, tt=tt: dve.tensor_copy(vb[:, tt * 4 + s, :], bank(bq))), r=[("ps", bq)], w=[("vb", tt * 4 + s)])
        if debug and tt == 1:
            S.dma(lambda: sp.dma_start(out=d_q.ap(), in_=qT[:]), r=[("qT", c) for c in range(4)], w=["d_q"])

        if tt + 1 < ntiles:
            load_h(tt + 1)
        if STOP == "qkv":
            return finish()
        steps = [("d", d) for d in range(4)] + [("o", j) for j in range(4 * tt - 1, -1, -1)]
        ns = len(steps)
        for c in range(4):
            def lo_of(n):
                kind, v = steps[n]
                return v * 128 if kind == "d" else 0

            def zmm(n, hh, c=c, tt=tt):
                kind, v = steps[n]
                p0, p1 = hh * 64, (hh + 1) * 64
                zb = ZBP[hh][n % 2]
                if kind == "d":
                    for a in range(v, 4):
                        j = 4 * tt + a - v
                        S.op("pe", (lambda a=a, j=j, v=v: pe.matmul(bank(zb, a * 128, (a + 1) * 128), kT[p0:p1, c, j * 128:(j + 1) * 128],
                                                                qT[p0:p1, c, a * 128:(a + 1) * 128], start=(a == v), stop=False, skip_group_check=True)),
                             r=[("kT", c, j // 4), ("qT", c)], w=[("ps", zb)])
                else:
                    j = v
                    S.op("pe", (lambda j=j: pe.matmul(bank(zb), kT[p0:p1, c, j * 128:(j + 1) * 128], qT[p0:p1, c, :], start=True, stop=False, skip_group_check=True)),
                         r=[("kT", c, j // 4), ("qT", c)], w=[("ps", zb)])

            def act_e(n, hh):
                lo = lo_of(n)
                zb = ZBP[hh][n % 2]
                S.op("act", (lambda: act.activation(ebuf[hh][:, lo:512], bank(zb, lo, 512), AF.Exp)), r=[("ps", zb)], w=[("e", hh)])

            def act_sp(n, hh):
                lo = lo_of(n)
                pq = n % 2
                spt = spb2[hh][pq]
                S.op("act", (lambda: act.activation(spt[:, lo:512], ebuf[hh][:, lo:512], AF.Ln, bias=1.0)), r=[("e", hh)], w=[("sp", hh, pq)])
                if n == 0:
                    S.op("dve", (lambda: dve.tensor_tensor(spm[hh][:].rearrange("p (a l) -> p a l", a=4), spt[:].rearrange("p (a l) -> p a l", a=4), US4, ALU.mult)),
                         r=[("sp", hh, pq)] + CB, w=[("spm", hh)])

            def pe_tio(n, hh):
                lo = lo_of(n)
                pq = n % 2
                zb = ZBP[hh][pq]
                first = (n == 0)
                src = spm[hh] if first else spb2[hh][pq]
                skey = ("spm", hh) if first else ("sp", hh, pq)
                S.op("pe", (lambda: pe.matmul(bank(zb, lo, 512), negLI_b, src[:, lo:512], start=False, stop=first, skip_group_check=True)),
                     r=[skey] + CB, w=[("ps", zb)])
                if not first:
                    S.op("pe", (lambda: pe.matmul(bank(zb, lo, 512), negI_b, cbb[hh][:, lo:512], start=False, stop=True, skip_group_check=True)),
                         r=[("cb", hh)] + CB, w=[("ps", zb)])
                if n < ns - 1:
                    S.op("pe", (lambda: pe.matmul(bank(CBK[hh], lo, 512), ones_b, src[:, lo:512], start=first, stop=False, skip_group_check=True)),
                         r=[skey] + CB, w=[("ps", CBK[hh])])
                    S.op("dve", (lambda: dve.tensor_copy(cbb[hh][:], bank(CBK[hh]))), r=[("ps", CBK[hh])], w=[("cb", hh)])

            def act_w(n, hh):
                lo = lo_of(n)
                pq = n % 2
                zb = ZBP[hh][pq]
                wt = wb2[hh][pq]
                S.op("act", (lambda: act.activation(wt[:, lo:512], bank(zb, lo, 512), AF.Exp)), r=[("ps", zb)], w=[("w", hh, pq)])
                if n == 0:
                    S.op("dve", (lambda: dve.tensor_tensor(wm[hh][:].rearrange("p (a l) -> p a l", a=4), wt[:].rearrange("p (a l) -> p a l", a=4), US4, ALU.mult)),
                         r=[("w", hh, pq)] + CB, w=[("wm", hh)])

            def pe_pv(n, hh):
                kind, v = steps[n]
                pq = n % 2
                first = (n == 0)
                last = (n == ns - 1)
                p0, p1 = hh * 64, (hh + 1) * 64
                vc0 = c * 128 + hh * 64
                wsrc = wm[hh] if first else wb2[hh][pq]
                wkey = ("wm", hh) if first else ("w", hh, pq)
                if kind == "d":
                    for a in range(v, 4):
                        j = 4 * tt + a - v
                        S.op("pe", (lambda a=a, j=j: pe.matmul(ps[p0:p1, OB * 512 + a * 128: OB * 512 + (a + 1) * 128], vb[:, j, vc0:vc0 + 64],
                                                                wsrc[:, a * 128:(a + 1) * 128], start=(first and a == 0), stop=False, skip_group_check=True)),
                             r=[("vb", j), wkey], w=[("ps", OB)])
                else:
                    j = v
                    S.op("pe", (lambda j=j: pe.matmul(ps[p0:p1, OB * 512: OB * 512 + 512], vb[:, j, vc0:vc0 + 64], wsrc[:, :], start=False, stop=last,
                                                      skip_group_check=True)),
                         r=[("vb", j), wkey], w=[("ps", OB)])

            for n0 in range(min(2, ns)):
                for hh in range(2):
                    zmm(n0, hh)
            for hh in range(2):
                act_e(0, hh)
            for hh in range(2):
                act_sp(0, hh)
                pe_tio(0, hh)
            for n in range(ns):
                for hh in range(2):
                    if n + 1 < ns:
                        act_e(n + 1, hh)
                    act_w(n, hh)
                    if n + 1 < ns:
                        act_sp(n + 1, hh)
                        pe_tio(n + 1, hh)
                    if n + 2 < ns:
                        zmm(n + 2, hh)
                for hh in range(2):
                    pe_pv(n, hh)
                if STOP == "step0":
                    return finish()
            S.op("dve", (lambda c=c: dve.tensor_copy(ysb[:, c, :], bank(OB))), r=[("ps", OB)], w=[("ysb", c)])
            if STOP == "chunk0":
                return finish()
        S.dma((lambda tt=tt: sp.dma_start(out=yl_sb_v[tt // 4][:, :, (tt % 4) * TT:(tt % 4 + 1) * TT], in_=ysb[:])), r=[("ysb", c) for c in range(4)], w=[("yl_sb", tt)], sk=("dma", "ysb"))
        if tt == 3:
            gather(yl_sb, ya_sb, 0, "yl_sb", 2)

    if debug and not only3:
        d_ysb = dbg_out("d_ysb", [512, SEQ], BF16)
        for hf_ in range(2):
            nt_ = min(ntiles, hf_ * 4 + 4) - hf_ * 4
            if nt_ > 0:
                S.dma((lambda hf_=hf_, nt_=nt_: sp.dma_start(out=d_ysb.ap()[:, hf_ * HS: hf_ * HS + nt_ * TT], in_=yl_sb[hf_].ap()[:, 0:nt_ * TT])),
                      r=[("yl_sb", t) for t in range(ntiles)], w=[("d_ysb", hf_)])
    S.fence()
    A.release(mP)
    if upto < 3:
        S.emit()
        return nc, S, dbg
    if not only3:
        gather(yl_sb, ya_sb, 1, "yl_sb", 3)

    ya_ssd_v = [t_.ap().rearrange("(c p) t -> p c t", p=128) for t_ in ya_ssd]
    ya_sb_v = [t_.ap().rearrange("(c p) t -> p c t", p=128) for t_ in ya_sb]
    woutb_v = woutb_d.ap().rearrange("(k p) c -> p k c", p=128)
    wgb_v = wgb_d.ap().rearrange("(k p) c -> p k c", p=128)
    wub_v = wub_d.ap().rearrange("(k p) c -> p k c", p=128)
    wdb_v = wdb_d.ap().rearrange("(k p) c -> p k c", p=128)
    WOK = prep_keys("woutb", 2 * D, D)
    WGK = prep_keys("wgb", D, DFF)
    WUK = prep_keys("wub", D, DFF)
    WDK = prep_keys("wdb", DFF, D)
    yh = [A([128, 16, TT], BF16, "yh%d" % i) for i in range(2)]
    xt = A([128, 4, D], F32, "xt")
    x1 = A([128, 4, D], F32, "x1")
    xn = [A([128, D], F32, "xn%d" % i) for i in range(2)]
    h2T = A([128, 8, TT], BF16, "h2T")
    actT = A([128, NCH, TT], BF16, "actT")
    sg = [A([128, TT], F32, "sg%d" % i) for i in range(2)]
    tmpo = [A([128, 512], F32, "tmpo%d" % i) for i in range(2)]
    ob = [A([128, 512], F32, "ob%d" % i) for i in range(4)]
    ssq3 = A([128, 4], F32, "ssq3")
    sqj3 = A([128, D], BF16, "sqj3")
    NRING = 6
    ring = [A([128, 4096], BF16, "ring%d" % i) for i in range(NRING)]
    ring_i = [0]

    def ring_load(fn_src, rkeys):
        i = ring_i[0] % NRING
        ring_i[0] += 1
        t = ring[i]
        o, i_ = fn_src(t)
        S.dma((lambda o=o, i_=i_: sp.dma_start(out=o, in_=i_)), r=rkeys, w=[("ring", i)])
        return t, ("ring", i)

    obi = 0
    def load_inputs(ut):
        for hlf in range(2):
            t0 = ut * TT
            S.dma((lambda hlf=hlf, t0=t0: sp.dma_start(out=yh[hlf][:, 0:8, :], in_=ya_ssd_v[hlf][:, :, t0:t0 + TT])), r=[("ya", "yl_ssd", hlf)], w=[("yh", hlf, 0)])
            S.dma((lambda hlf=hlf, t0=t0: act.dma_start(out=yh[hlf][:, 8:16, :], in_=ya_sb_v[hlf][:, :, t0:t0 + TT])), r=[("ya", "yl_sb", hlf)], w=[("yh", hlf, 1)], q="act")
        for q4 in range(4):
            pt = q4 // 2
            v0 = yh[0][:, q4 * 4:(q4 + 1) * 4, :].rearrange("p c t -> p (c t)")
            v1 = yh[1][:, q4 * 4:(q4 + 1) * 4, :].rearrange("p c t -> p (c t)")
            S.op("dve", (lambda v0=v0: dve.tensor_scalar_mul(v0, v0, flags[:, 0:1])), r=[("yh", 0, pt), "flags"], w=[("yh", 0, pt)])
            S.op("dve", (lambda v0=v0, v1=v1: dve.scalar_tensor_tensor(v0, v1, flags[:, 1:2], v0, ALU.mult, ALU.add)),
                 r=[("yh", 0, pt), ("yh", 1, pt), "flags"], w=[("yh", 0, pt)])
        S.dma((lambda ut=ut: sp.dma_start(out=xt[:], in_=xh_d.ap()[ut * TT:(ut + 1) * TT, :].rearrange("(s p) d -> p s d", p=128))), w=["xt"])

    load_inputs(0)
    for ut in range(4):
        for g4 in range(4):
            wt, wkey = ring_load(lambda t, g4=g4: (t[:].rearrange("p (k c) -> p k c", k=4), woutb_v[:, g4 * 4:(g4 + 1) * 4, :]), WOK)
            wv = wt[:].rearrange("p (k c) -> p k c", k=4)
            for kk in range(4):
                kc = g4 * 4 + kk
                for s in range(4):
                    for hf in range(2):
                        S.op("pe", (lambda kc=kc, kk=kk, s=s, hf=hf, wv=wv: pe.matmul(bank(s * 2 + hf), yh[0][:, kc, s * 128:(s + 1) * 128], wv[:, kk, hf * 512:(hf + 1) * 512],
                                                                                      start=(kc == 0), stop=(kc == 15))),
                             r=[("yh", 0, 0), ("yh", 0, 1), wkey], w=[("ps", s * 2 + hf)])
        for s in range(4):
            for hf in range(2):
                tb = tmpo[(s * 2 + hf) % 2]
                tk = ("tmpo", (s * 2 + hf) % 2)
                S.op("dve", (lambda s=s, hf=hf, tb=tb: dve.tensor_tensor(tb[:], bank(s * 2 + hf), g1bc[:, hf * 512:(hf + 1) * 512], ALU.mult)),
                     r=[("ps", s * 2 + hf), "g1bc"], w=[tk])
                S.op("dve", (lambda s=s, hf=hf, tb=tb: dve.tensor_tensor(x1[:, s, hf * 512:(hf + 1) * 512], tb[:], xt[:, s, hf * 512:(hf + 1) * 512], ALU.add)),
                     r=[tk, "xt"], w=[("x1", s, hf)])
        for s in range(4):
            sc = ssq3[:, s:s + 1]
            S.op("act", (lambda s=s, sc=sc: act.activation(sqj3[:], x1[:, s, :], AF.Square, accum_out=sc)), r=[("x1", s, 0), ("x1", s, 1)], w=["sqj3", ("ssq3", s)])
            rstd_from_ssq(sc, D, ("ssq3", s), ("ssq3", s))
            xb = xn[s % 2]
            kx = ("xn", s % 2)
            S.op("dve", (lambda s=s, sc=sc, xb=xb: dve.tensor_scalar_mul(xb[:], x1[:, s, :], sc)), r=[("x1", s, 0), ("x1", s, 1), ("ssq3", s)], w=[kx])
            for c in range(8):
                S.op("pe", (lambda c=c, s=s, xb=xb: pe.transpose(bank(c, s * 128, (s + 1) * 128), xb[:, c * 128:(c + 1) * 128], ident)),
                     r=[kx, "cst"], w=[("ps", c)])
        for c in range(8):
            S.op("dve", (lambda c=c: dve.tensor_scalar(h2T[:, c, :], bank(c), s2[:, c:c + 1], t2[:, c:c + 1], ALU.mult, ALU.add)),
                 r=[("ps", c), "s2", "modpp"], w=[("h2T", c)])
        H2K = [("h2T", c) for c in range(8)]
        if ut + 1 < 4:
            load_inputs(ut + 1)
        fi = 0
        for fg in range(6):
            nf = 4 if fg < 5 else 2
            ncol = nf * 128
            gt, gkey = ring_load(lambda t, fg=fg, ncol=ncol: (t[:].rearrange("p (k c) -> p k c", k=8)[:, :, 0:ncol], wgb_v[:, :, fg * 512: fg * 512 + ncol]), WGK)
            utile, ukey = ring_load(lambda t, fg=fg, ncol=ncol: (t[:].rearrange("p (k c) -> p k c", k=8)[:, :, 0:ncol], wub_v[:, :, fg * 512: fg * 512 + ncol]), WUK)
            gv = gt[:].rearrange("p (k c) -> p k c", k=8)
            uv = utile[:].rearrange("p (k c) -> p k c", k=8)
            for f in range(nf):
                bg_ = (fi % 4) * 2
                bu_ = bg_ + 1
                for k in range(8):
                    S.op("pe", (lambda k=k, f=f, bg_=bg_, gv=gv: pe.matmul(bank(bg_), gv[:, k, f * 128:(f + 1) * 128], h2T[:, k, :], start=(k == 0), stop=(k == 7))),
                         r=H2K + [gkey], w=[("ps", bg_)])
                for k in range(8):
                    S.op("pe", (lambda k=k, f=f, bu_=bu_, uv=uv: pe.matmul(bank(bu_), uv[:, k, f * 128:(f + 1) * 128], h2T[:, k, :], start=(k == 0), stop=(k == 7))),
                         r=H2K + [ukey], w=[("ps", bu_)])
                sgb = sg[fi % 2]
                S.op("act", (lambda bg_=bg_, sgb=sgb: act.activation(sgb[:], bank(bg_), AF.Silu)), r=[("ps", bg_)], w=[("sg", fi % 2)])
                fch = fg * 4 + f
                S.op("dve", (lambda bu_=bu_, sgb=sgb, fch=fch: dve.tensor_tensor(actT[:, fch, :], bank(bu_), sgb[:], ALU.mult)),
                     r=[("ps", bu_), ("sg", fi % 2)], w=[("actT", fch)])
                fi += 1
        for g in range(6):
            nk = 4 if g < 5 else 2
            wt, wkey = ring_load(lambda t, g=g, nk=nk: (t[:].rearrange("p (k c) -> p k c", k=4)[:, 0:nk, :], wdb_v[:, g * 4: g * 4 + nk, :]), WDK)
            wv = wt[:].rearrange("p (k c) -> p k c", k=4)
            for kk in range(nk):
                kc = g * 4 + kk
                for s in range(4):
                    for hf in range(2):
                        S.op("pe", (lambda kc=kc, kk=kk, s=s, hf=hf, wv=wv: pe.matmul(bank(s * 2 + hf), actT[:, kc, s * 128:(s + 1) * 128], wv[:, kk, hf * 512:(hf + 1) * 512],
                                                                                      start=(kc == 0), stop=(kc == NCH - 1))),
                             r=[("actT", kc), wkey], w=[("ps", s * 2 + hf)])
        for s in range(4):
            for hf in range(2):
                tb = tmpo[(s * 2 + hf) % 2]
                tk = ("tmpo", (s * 2 + hf) % 2)
                o = ob[obi % 4]
                ok = ("ob", obi % 4)
                obi += 1
                S.op("dve", (lambda s=s, hf=hf, tb=tb: dve.tensor_tensor(tb[:], bank(s * 2 + hf), g2bc[:, hf * 512:(hf + 1) * 512], ALU.mult)),
                     r=[("ps", s * 2 + hf), "g2bc"], w=[tk])
                S.op("dve", (lambda s=s, hf=hf, tb=tb, o=o: dve.tensor_tensor(o[:], tb[:], x1[:, s, hf * 512:(hf + 1) * 512], ALU.add)),
                     r=[tk, ("x1", s, hf)], w=[ok])
                r0 = ut * TT + s * 128
                S.dma((lambda o=o, r0=r0, hf=hf: sp.dma_start(out=out_d.ap()[r0:r0 + 128, hf * 512:(hf + 1) * 512], in_=o[:])), r=[ok], w=[("out", ut, s, hf)], sk=("dma",) + ok)
    S.fence()
    S.emit()
    return nc, S, dbg


def _consts():
    i = np.arange(128)[:, None]
    j = np.arange(128)[None, :]
    c = np.zeros((128, 7, 128), np.float32)
    c[:, 0] = (i == j)
    c[:, 1] = (i <= j)
    c[:, 2] = (i < j)
    c[:, 3] = (i >= j)
    c[:, 4] = (i > j)
    c[:, 5] = 1.0
    c[:, 6] = ((i // 64) == (j // 64))
    return c


def _pp(v):
    v = np.asarray(v, np.float32)
    return np.ascontiguousarray(v.reshape(-1, 128).T)


def make_in_maps(x, c, w_ada, b_ada, norm1_w, w_in, conv_w, conv_b, dt_bias, a_log, d_skip,
                 ssd_norm_w, q_norm_w, k_norm_w, w_out, norm2_w, w_gate, w_up, w_down):
    f = lambda a: np.ascontiguousarray(np.asarray(a, np.float32))
    x, c = f(x), f(c)
    w_ada, b_ada = f(w_ada)[0], f(b_ada)[0]
    w_in, conv_w, conv_b = f(w_in)[0], f(conv_w)[0], f(conv_b)[0]
    dt_bias, a_log, d_skip = f(dt_bias)[0], f(a_log)[0], f(d_skip)[0]
    ssd_norm_w, q_norm_w, k_norm_w = f(ssd_norm_w)[0], f(q_norm_w)[0], f(k_norm_w)[0]
    w_out, w_gate, w_up, w_down = f(w_out)[0], f(w_gate)[0], f(w_up)[0], f(w_down)[0]
    n1, n2 = f(norm1_w)[0], f(norm2_w)[0]
    consts = _consts()
    b_pp = np.concatenate([_pp(b_ada[0:1024]), _pp(b_ada[1024:2048]), _pp(b_ada[3072:4096]), _pp(b_ada[4096:5120])], axis=1)
    b_g = np.ascontiguousarray(np.stack([b_ada[2048:3072], b_ada[5120:6144]]))
    maps = []
    for core in range(8):
        b, g = core // 2, core % 2
        cols = np.concatenate([
            np.arange(g * 512, (g + 1) * 512),
            1024 + np.arange(g * 512, (g + 1) * 512),
            2048 + np.arange(g * 128, (g + 1) * 128),
            2304 + np.arange(g * 128, (g + 1) * 128),
            2576 + np.arange(g * 512, (g + 1) * 512),
            3600 + np.arange(g * 512, (g + 1) * 512),
            4624 + np.arange(g * 512, (g + 1) * 512),
            2560 + np.arange(g * 8, (g + 1) * 8),
        ])
        cch = np.concatenate([np.arange(g * 512, (g + 1) * 512), 1024 + np.arange(g * 128, (g + 1) * 128),
                              1280 + np.arange(g * 128, (g + 1) * 128)])
        cw = np.ascontiguousarray(conv_w[:, cch].T.reshape(6, 128, 4).transpose(1, 0, 2))
        cb = _pp(conv_b[cch])
        vec8 = np.zeros((3, 8), np.float32)
        vec8[0] = dt_bias[g * 8:(g + 1) * 8]
        vec8[1] = a_log[g * 8:(g + 1) * 8]
        dsk = np.ascontiguousarray(np.repeat(d_skip[g * 8:(g + 1) * 8], 64).reshape(4, 128).T)
        snw = _pp(ssd_norm_w[g * 512:(g + 1) * 512])
        qkw = np.ascontiguousarray(np.stack([np.tile(q_norm_w, 2), np.tile(k_norm_w, 2)], axis=1))
        flags = np.zeros((128, 2), np.float32)
        flags[:, g] = 1.0
        maps.append({
            "x": x[b], "xh": np.ascontiguousarray(x[b, g * 2048:(g + 1) * 2048]), "cT": _pp(c[b]),
            "w_ada": w_ada, "b_pp": b_pp, "b_g": b_g, "n1w": _pp(n1), "n2w": _pp(n2),
            "w_in": np.ascontiguousarray(w_in[:, cols]), "conv_w": cw, "conv_b": cb, "vec8": vec8,
            "dsk": dsk, "snw": snw, "qkw": qkw, "w_out": w_out, "w_gate": w_gate, "w_up": w_up, "w_down": w_down,
            "consts": consts, "flags": flags,
        })
    return maps


_CACHE = {}


def kernel(**inputs):
    if "nc" not in _CACHE:
        _CACHE["nc"] = build(False)[0]
    nc = _CACHE["nc"]
    maps = make_in_maps(**inputs)
    res = run_bass_kernel_spmd(nc, maps, core_ids=list(range(8)))
    out = np.empty((NB, SEQ, D), np.float32)
    for core in range(8):
        b, g = core // 2, core % 2
        out[b, g * 2048:(g + 1) * 2048] = res.results[core]["out"]
    return out
```
